# Optimizing a Trainium2 kernel written in Bass

```python
import jax
import jax.numpy as jnp
from jax import lax
import numpy as np

D_MODEL = 4096
BATCH = 4
SEQ = 2048
DEPTH = 2
DEC_BATCH = 8
DEC_SEQ = 1
PAST_LEN = 16384
PAGE_SIZE = 128

N_A_LAYERS = DEPTH // 2
N_B_LAYERS = DEPTH - N_A_LAYERS
D_FF = 4 * D_MODEL
PLE_DIM = 256
POOL_WINDOWS = (2, 4, 8, 16)
POOL_GROUP = D_MODEL // len(POOL_WINDOWS)
POOL_STATE = max(POOL_WINDOWS) - 1
HEAD_DIM = 128
N_HEADS = D_MODEL // HEAD_DIM
N_KV_HEADS = 4
GROUP_SIZE = N_HEADS // N_KV_HEADS
N_BRANCH = 3
N_KV_TENSORS = 6
CMP_BLOCK = 32
CMP_STRIDE = 16
CMP_HIDDEN = 2 * HEAD_DIM
SEL_BLOCK = 64
N_SEL = 16
WINDOW = 512
SEL_QBLOCK = 32
WIN_QBLOCK = 128
ROPE_THETA = 10000.0
EPS = 1e-6
SCALE = HEAD_DIM ** -0.5
NEG = -1e30
FORCE = 1e9
INVALID_POS = -(2 ** 30)

kernel_name = 'yoco_pool_nsa_decoder_step'


def rmsnorm(x, g):
    xf = x.astype(jnp.float32)
    y = xf * lax.rsqrt(jnp.mean(xf * xf, axis=-1, keepdims=True) + EPS)
    return (y * g.astype(jnp.float32)).astype(x.dtype)


def rope(x, pos):
    half = HEAD_DIM // 2
    inv = ROPE_THETA ** (-jnp.arange(half, dtype=jnp.float32) / half)
    ang = pos.astype(jnp.float32)[:, None] * inv[None, :]
    shape = (1, pos.shape[0]) + (1,) * (x.ndim - 3) + (half,)
    cos = jnp.cos(ang).reshape(shape)
    sin = jnp.sin(ang).reshape(shape)
    xf = x.astype(jnp.float32)
    x1, x2 = xf[..., :half], xf[..., half:]
    return jnp.concatenate([x1 * cos - x2 * sin, x2 * cos + x1 * sin], axis=-1).astype(x.dtype)


def _pad_time(x, mult):
    pad = (-x.shape[1]) % mult
    return jnp.pad(x, ((0, 0), (0, pad)) + ((0, 0),) * (x.ndim - 2))


def _blocks(x, axis, qb, nb):
    widths = [(0, 0)] * x.ndim
    widths[axis] = (0, nb * qb - x.shape[axis])
    x = jnp.pad(x, widths)
    x = x.reshape(x.shape[:axis] + (nb, qb) + x.shape[axis + 1:])
    return jnp.moveaxis(x, axis, 0)


def _unblocks(y, t):
    y = jnp.moveaxis(y, 0, 1)
    y = y.reshape((y.shape[0], y.shape[1] * y.shape[2]) + y.shape[3:])
    return y[:, :t]


def pool_mixer(a, prefix, pos, w_pool, scale):
    B, T, _ = a.shape
    seq = jnp.concatenate([prefix, a], axis=1)
    cs = jnp.pad(jnp.cumsum(seq.astype(jnp.float32), axis=1), ((0, 0), (1, 0), (0, 0)))
    upto = cs[:, POOL_STATE + 1:]
    x_t = seq[:, POOL_STATE:].astype(jnp.float32)
    diffs = []
    for g, w in enumerate(POOL_WINDOWS):
        c = slice(g * POOL_GROUP, (g + 1) * POOL_GROUP)
        before = cs[:, POOL_STATE + 1 - w: POOL_STATE + 1 - w + T, c]
        cnt = jnp.minimum(pos + 1, w).astype(jnp.float32)[None, :, None]
        diffs.append((upto[..., c] - before) / cnt - x_t[..., c])
    d = jnp.stack(diffs, axis=2).astype(a.dtype)
    out = jnp.einsum('btgc,gce->btge', d, w_pool).reshape(B, T, D_MODEL)
    return out * scale, seq[:, -POOL_STATE:]


def shared_kv_rows(s, pos, g_kv, w_kv, g_k_sel, g_k_win):
    hkv = rmsnorm(s, g_kv)
    kv = jnp.einsum('btd,dngk->btngk', hkv, w_kv)
    k_cmp = kv[:, :, 0]
    v_cmp = kv[:, :, 1]
    k_sel = rope(rmsnorm(kv[:, :, 2], g_k_sel), pos)
    v_sel = kv[:, :, 3]
    k_win = rope(rmsnorm(kv[:, :, 4], g_k_win), pos)
    v_win = kv[:, :, 5]
    return (k_cmp, v_cmp, k_sel, v_sel, k_win, v_win)


def compress(raw, w1, w2, pe):
    B, Tk, G, hd = raw.shape
    n_sub = Tk // CMP_STRIDE
    ratio = CMP_BLOCK // CMP_STRIDE
    sub = raw.reshape(B, n_sub, CMP_STRIDE, G, hd).transpose(0, 1, 3, 2, 4).reshape(B, n_sub, G, CMP_STRIDE * hd)
    w1s = w1.reshape(ratio, CMP_STRIDE * hd, CMP_HIDDEN)
    nc = n_sub - ratio + 1
    pre = pe.reshape(-1) @ w1
    for i in range(ratio):
        pre = pre + jnp.einsum('bngc,ch->bngh', sub[:, i:i + nc], w1s[i])
    return jnp.einsum('bngh,hd->bngd', jax.nn.silu(pre), w2)


def cmp_sel_branches(q, qr, q_pos, ck, cv, ksb, vsb):
    Tq = q.shape[1]
    nc = ck.shape[1]
    ns = ksb.shape[2]
    ratio = CMP_BLOCK // CMP_STRIDE
    per_sel = SEL_BLOCK // CMP_STRIDE
    n_sel = min(N_SEL, ns)
    cmp_end = jnp.arange(nc) * CMP_STRIDE + CMP_BLOCK - 1
    blk = jnp.arange(ns)
    qb = min(SEL_QBLOCK, Tq)
    nb = -(-Tq // qb)
    gather = jax.vmap(jax.vmap(lambda blocks, ids: blocks[ids]))

    def body(args):
        qblk, rblk, pblk = args
        ok_c = (cmp_end[None, :] <= pblk[:, None])[None, :, None, None, :]
        s_c = jnp.einsum('bqgrd,bngd->bqgrn', qblk, ck).astype(jnp.float32) * SCALE
        p_c = jnp.where(ok_c, jax.nn.softmax(jnp.where(ok_c, s_c, NEG), axis=-1), 0.0)
        o_cmp = jnp.einsum('bqgrn,bngd->bqgrd', p_c.astype(cv.dtype), cv)
        imp = jnp.pad(p_c.sum(axis=3), ((0, 0), (0, 0), (0, 0), (ratio - 1, ratio - 1)))
        p_slc = 0.0
        for m in range(per_sel):
            for n in range(ratio):
                o0 = m - n + ratio - 1
                p_slc = p_slc + imp[..., o0:o0 + per_sel * (ns - 1) + 1:per_sel]
        cur = pblk // SEL_BLOCK
        vis = blk[None, :] * SEL_BLOCK <= pblk[:, None]
        forced = vis & ((blk[None, :] == 0) | (blk[None, :] == cur[:, None]) | (blk[None, :] == cur[:, None] - 1))
        score = jnp.where(vis[None, :, None, :], p_slc, NEG)
        score = jnp.where(forced[None, :, None, :], FORCE, score)
        top_s, idx = lax.top_k(score, n_sel)
        ids = idx.transpose(0, 2, 1, 3)
        valid = (top_s > 0.5 * NEG).transpose(0, 2, 1, 3)
        kg = gather(ksb, ids)
        vg = gather(vsb, ids)
        s = jnp.einsum('bqgrd,bgqnld->bqgrnl', rblk, kg).astype(jnp.float32) * SCALE
        kpos = ids[..., None] * SEL_BLOCK + jnp.arange(SEL_BLOCK)
        ok = (kpos <= pblk[None, None, :, None, None]) & valid[..., None]
        ok = ok.transpose(0, 2, 1, 3, 4)[:, :, :, None]
        s = jnp.where(ok, s, NEG)
        shp = s.shape
        p = jax.nn.softmax(s.reshape(shp[:4] + (-1,)), axis=-1).reshape(shp)
        p = jnp.where(ok, p, 0.0).astype(vg.dtype)
        o_sel = jnp.einsum('bqgrnl,bgqnld->bqgrd', p, vg)
        return o_cmp, o_sel

    o_cmp, o_sel = lax.map(body, (_blocks(q, 1, qb, nb), _blocks(qr, 1, qb, nb), _blocks(q_pos, 0, qb, nb)))
    return _unblocks(o_cmp, Tq), _unblocks(o_sel, Tq)


def window_branch(qr, q_pos, kw, vw, kw_pos):
    Tq = qr.shape[1]
    off = kw.shape[1] - Tq
    qb = min(WIN_QBLOCK, Tq)
    nb = -(-Tq // qb)
    pad_q = nb * qb - Tq
    kp = jnp.pad(kw, ((0, 0), (WINDOW, pad_q), (0, 0), (0, 0)))
    vp = jnp.pad(vw, ((0, 0), (WINDOW, pad_q), (0, 0), (0, 0)))
    pp = jnp.pad(kw_pos, (WINDOW, pad_q), constant_values=INVALID_POS)
    span = qb + WINDOW

    def body(args):
        i, qblk, pblk = args
        start = off + i * qb
        kb = lax.dynamic_slice_in_dim(kp, start, span, axis=1)
        vb = lax.dynamic_slice_in_dim(vp, start, span, axis=1)
        pk = lax.dynamic_slice_in_dim(pp, start, span, axis=0)
        s = jnp.einsum('bqgrd,bkgd->bqgrk', qblk, kb).astype(jnp.float32) * SCALE
        ok = (pk[None, :] <= pblk[:, None]) & (pk[None, :] > pblk[:, None] - WINDOW)
        ok = ok[None, :, None, None, :]
        p = jnp.where(ok, jax.nn.softmax(jnp.where(ok, s, NEG), axis=-1), 0.0).astype(vb.dtype)
        return jnp.einsum('bqgrk,bkgd->bqgrd', p, vb)

    out = lax.map(body, (jnp.arange(nb), _blocks(qr, 1, qb, nb), _blocks(q_pos, 0, qb, nb)))
    return _unblocks(out, Tq)


def nsa_mixer(a, pos, ck, cv, ksb, vsb, win_ctx, w_qg, g_q, w_o):
    B, T, _ = a.shape
    qg = jnp.einsum('btd,de->bte', a, w_qg)
    q = rmsnorm(qg[..., :N_HEADS * HEAD_DIM].reshape(B, T, N_KV_HEADS, GROUP_SIZE, HEAD_DIM), g_q)
    gates = jax.nn.sigmoid(qg[..., N_HEADS * HEAD_DIM:].astype(jnp.float32)).reshape(B, T, N_KV_HEADS, GROUP_SIZE, N_BRANCH)
    qr = rope(q, pos)
    o_cmp, o_sel = cmp_sel_branches(q, qr, pos, ck, cv, ksb, vsb)
    kw, vw, kw_pos = win_ctx
    o_win = window_branch(qr, pos, kw, vw, kw_pos)
    o = gates[..., 0:1] * o_cmp + gates[..., 1:2] * o_sel + gates[..., 2:3] * o_win
    return jnp.einsum('bte,ed->btd', o.astype(a.dtype).reshape(B, T, N_HEADS * HEAD_DIM), w_o)


def trunk(x, p, pos, pool_prefix, make_ctx, W):
    h = x
    pool_new = []
    kv_rows = None
    win_state = None
    ck = cv = ksb = vsb = win_ctx = None
    for layer in range(DEPTH):
        a = rmsnorm(h, W['g_mix'][layer])
        if layer < N_A_LAYERS:
            mix, st = pool_mixer(a, pool_prefix[layer], pos, W['w_pool'][layer], W['pool_scale'][layer])
            pool_new.append(st)
        else:
            j = layer - N_A_LAYERS
            if j == 0:
                kv_rows = shared_kv_rows(h, pos, W['g_kv'], W['w_kv'], W['g_k_sel'], W['g_k_win'])
                (kc, vc, ksl, vsl), win_ctx, win_state = make_ctx(kv_rows)
                ck = rmsnorm(compress(kc, W['w_cmp_k1'], W['w_cmp_k2'], W['pe_cmp_k']), W['g_k_cmp'])
                cv = compress(vc, W['w_cmp_v1'], W['w_cmp_v2'], W['pe_cmp_v'])
                Bk, Tk = ksl.shape[:2]
                ns = Tk // SEL_BLOCK
                ksb = ksl.reshape(Bk, ns, SEL_BLOCK, N_KV_HEADS, HEAD_DIM).transpose(0, 3, 1, 2, 4)
                vsb = vsl.reshape(Bk, ns, SEL_BLOCK, N_KV_HEADS, HEAD_DIM).transpose(0, 3, 1, 2, 4)
            mix = nsa_mixer(a, pos, ck, cv, ksb, vsb, win_ctx, W['w_qg'][j], W['g_q'][j], W['w_o'][j])
        h = h + mix
        m = rmsnorm(h, W['g_ffn'][layer])
        u = jnp.square(jax.nn.relu(jnp.einsum('btd,df->btf', m, W['w_up'][layer])))
        h = h + jnp.einsum('btf,fd->btd', u, W['w_down'][layer])
        gate = jax.nn.sigmoid(jnp.einsum('btd,de->bte', rmsnorm(h, W['g_ple'][layer]), W['w_ple_gate'][layer]).astype(jnp.float32))
        h = h + (jnp.einsum('btk,kd->btd', p[layer], W['w_ple'][layer]) * gate).astype(h.dtype)
    return h, jnp.stack(pool_new, axis=0), kv_rows, win_state


def setup_inputs(seed: int = 0) -> dict:
    key = jax.random.key(seed)
    keys = iter(jax.random.split(key, 40))

    def nrm(shape, scale=1.0):
        return jax.random.normal(next(keys), shape, jnp.float32) * scale

    def gain(shape):
        return 1.0 + 0.05 * nrm(shape)

    n_pages = PAST_LEN // PAGE_SIZE
    n_phys = (5 * DEC_BATCH * n_pages + 3) // 4
    wb = min(WINDOW, PAST_LEN)
    kv_page = (n_phys, PAGE_SIZE, N_KV_HEADS, HEAD_DIM)
    perm = jax.random.permutation(next(keys), n_phys)[:DEC_BATCH * n_pages]
    return {
        'x_prompt': nrm((BATCH, SEQ, D_MODEL)),
        'x_sample': nrm((DEC_BATCH, DEC_SEQ, D_MODEL)),
        'state_pool': nrm((N_A_LAYERS, DEC_BATCH, POOL_STATE, D_MODEL)),
        'cache_k_cmp': nrm(kv_page),
        'cache_v_cmp': nrm(kv_page),
        'cache_k_sel': nrm(kv_page),
        'cache_v_sel': nrm(kv_page),
        'state_k_win': nrm((DEC_BATCH, wb, N_KV_HEADS, HEAD_DIM)),
        'state_v_win': nrm((DEC_BATCH, wb, N_KV_HEADS, HEAD_DIM)),
        'page_table': perm.reshape(DEC_BATCH, n_pages).astype(jnp.int32),
        'p_prompt': nrm((DEPTH, BATCH, SEQ, PLE_DIM)),
        'p_sample': nrm((DEPTH, DEC_BATCH, DEC_SEQ, PLE_DIM)),
        'g_mix': gain((DEPTH, D_MODEL)),
        'w_pool': nrm((N_A_LAYERS, len(POOL_WINDOWS), POOL_GROUP, POOL_GROUP), POOL_GROUP ** -0.5),
        'pool_scale': gain((N_A_LAYERS, D_MODEL)),
        'g_kv': gain((D_MODEL,)),
        'w_kv': nrm((D_MODEL, N_KV_TENSORS, N_KV_HEADS, HEAD_DIM), D_MODEL ** -0.5),
        'g_k_cmp': gain((HEAD_DIM,)),
        'g_k_sel': gain((HEAD_DIM,)),
        'g_k_win': gain((HEAD_DIM,)),
        'w_cmp_k1': nrm((CMP_BLOCK * HEAD_DIM, CMP_HIDDEN), (CMP_BLOCK * HEAD_DIM) ** -0.5),
        'w_cmp_k2': nrm((CMP_HIDDEN, HEAD_DIM), CMP_HIDDEN ** -0.5),
        'pe_cmp_k': nrm((CMP_BLOCK, HEAD_DIM), 0.5),
        'w_cmp_v1': nrm((CMP_BLOCK * HEAD_DIM, CMP_HIDDEN), (CMP_BLOCK * HEAD_DIM) ** -0.5),
        'w_cmp_v2': nrm((CMP_HIDDEN, HEAD_DIM), CMP_HIDDEN ** -0.5),
        'pe_cmp_v': nrm((CMP_BLOCK, HEAD_DIM), 0.5),
        'w_qg': nrm((N_B_LAYERS, D_MODEL, N_HEADS * HEAD_DIM + N_BRANCH * N_HEADS), D_MODEL ** -0.5),
        'g_q': gain((N_B_LAYERS, HEAD_DIM)),
        'w_o': nrm((N_B_LAYERS, N_HEADS * HEAD_DIM, D_MODEL), (N_HEADS * HEAD_DIM) ** -0.5),
        'g_ffn': gain((DEPTH, D_MODEL)),
        'w_up': nrm((DEPTH, D_MODEL, D_FF), D_MODEL ** -0.5),
        'w_down': nrm((DEPTH, D_FF, D_MODEL), 0.5 * D_FF ** -0.5),
        'g_ple': gain((DEPTH, D_MODEL)),
        'w_ple': nrm((DEPTH, PLE_DIM, D_MODEL), PLE_DIM ** -0.5),
        'w_ple_gate': nrm((DEPTH, D_MODEL, D_MODEL), D_MODEL ** -0.5),
    }


def reference(x_prompt, x_sample, state_pool, cache_k_cmp, cache_v_cmp, cache_k_sel, cache_v_sel, state_k_win, state_v_win, page_table, p_prompt, p_sample, g_mix, w_pool, pool_scale, g_kv, w_kv, g_k_cmp, g_k_sel, g_k_win, w_cmp_k1, w_cmp_k2, pe_cmp_k, w_cmp_v1, w_cmp_v2, pe_cmp_v, w_qg, g_q, w_o, g_ffn, w_up, w_down, g_ple, w_ple, w_ple_gate):
    W = dict(g_mix=g_mix, w_pool=w_pool, pool_scale=pool_scale, g_kv=g_kv, w_kv=w_kv, g_k_cmp=g_k_cmp,
             g_k_sel=g_k_sel, g_k_win=g_k_win, w_cmp_k1=w_cmp_k1, w_cmp_k2=w_cmp_k2, pe_cmp_k=pe_cmp_k,
             w_cmp_v1=w_cmp_v1, w_cmp_v2=w_cmp_v2, pe_cmp_v=pe_cmp_v, w_qg=w_qg, g_q=g_q, w_o=w_o,
             g_ffn=g_ffn, w_up=w_up, w_down=w_down, g_ple=g_ple, w_ple=w_ple, w_ple_gate=w_ple_gate)
    t_p = x_prompt.shape[1]
    t_s = x_sample.shape[1]
    pos_p = jnp.arange(t_p, dtype=jnp.int32)
    pos_s = PAST_LEN + jnp.arange(t_s, dtype=jnp.int32)
    caches = (cache_k_cmp, cache_v_cmp, cache_k_sel, cache_v_sel)

    def prompt_ctx(rows):
        full = tuple(_pad_time(r, SEL_BLOCK) for r in rows[:4])
        nw = min(WINDOW, t_p)
        return full, (rows[4], rows[5], pos_p), (rows[4][:, -nw:], rows[5][:, -nw:])

    def sample_ctx(rows):
        def past(pool):
            g = pool[page_table]
            return g.reshape((g.shape[0], g.shape[1] * g.shape[2]) + g.shape[3:])
        full = tuple(_pad_time(jnp.concatenate([past(c), r], axis=1), SEL_BLOCK) for c, r in zip(caches, rows[:4]))
        wb = state_k_win.shape[1]
        kw = jnp.concatenate([state_k_win, rows[4]], axis=1)
        vw = jnp.concatenate([state_v_win, rows[5]], axis=1)
        kw_pos = jnp.concatenate([jnp.arange(PAST_LEN - wb, PAST_LEN, dtype=jnp.int32), pos_s])
        return full, (kw, vw, kw_pos), (kw[:, -wb:], vw[:, -wb:])

    pool_zero = jnp.zeros((N_A_LAYERS, x_prompt.shape[0], POOL_STATE, D_MODEL), x_prompt.dtype)
    y_p, pool_p, rows_p, win_p = trunk(x_prompt, p_prompt, pos_p, pool_zero, prompt_ctx, W)
    y_s, pool_s, rows_s, win_s = trunk(x_sample, p_sample, pos_s, state_pool, sample_ctx, W)
    return (y_p, y_s, pool_p, pool_s, rows_p[0], rows_p[1], rows_p[2], rows_p[3], win_p[0], win_p[1], rows_s[0], rows_s[1], rows_s[2], rows_s[3], win_s[0], win_s[1])
```

```python
import contextlib
import numpy as np
import concourse.bass as bass
import concourse.mybir as mybir
from concourse.bass_utils import run_bass_kernel_spmd

AF = mybir.ActivationFunctionType
ALU = mybir.AluOpType
F32, BF16, I32 = mybir.dt.float32, mybir.dt.bfloat16, mybir.dt.int32

D = 4096
KC = 32
DFF = 16384
T = 2048
TW = 512
NT = T // TW
PLE = 256
HD = 128
G = 4
EPS = 1e-6
NSMP = 2
NSEQ = 1
PAST = 16384
WIN = 512


class Sem:
    def __init__(self, h):
        self.h = h
        self.n = 0


class Prog:
    ENG = ("pe", "act", "dve", "pool", "sp")

    def __init__(self, nc, stack):
        self.nc = nc
        self.stack = stack
        self.q = {e: [] for e in self.ENG}
        self.nsem = 0
        self.esem = {e: self.newsem("e_" + e) for e in self.ENG}
        self.strict = set()
        self.last = {}
        self.lastsig = {}

    def newsem(self, name):
        self.nsem += 1
        return Sem(self.stack.enter_context(self.nc.semaphore(name)))

    def op(self, eng, fn, waits=(), sig=False, sem=None, dma=False, force=()):
        tok = None
        inc = None
        force = list(force)
        if eng in self.strict and sem is None:
            sig = True
            force.append(self.last.get(eng))
        if sig or sem is not None:
            sm = sem if sem is not None else self.esem[eng]
            k = 16 if dma else 1
            sm.n += k
            tok = (sm, sm.n)
            inc = (sm.h, k)
        self.q[eng].append((fn, [w for w in waits if w is not None], inc, [w for w in force if w is not None]))
        if sem is None and tok is not None:
            self.last[eng] = tok
            self.lastsig[eng] = tok
        return tok

    def replay(self, eng, e):
        seen = {}
        own = self.esem[eng]
        for fn, waits, inc, force in self.q[eng]:
            for (sm, v) in force:
                e.wait_ge(sm.h, v)
            for (sm, v) in waits:
                if sm is own:
                    continue
                if seen.get(id(sm), -1) >= v:
                    continue
                e.wait_ge(sm.h, v)
                seen[id(sm)] = v
            if fn is None:
                continue
            r = fn(e)
            if inc is not None:
                r.then_inc(inc[0], inc[1])


def build(stages=("A",), debug=False):
    nc = bass.Bass("TRN2", target_bir_lowering=False)
    stack = contextlib.ExitStack()
    P = Prog(nc, stack)

    def din(name, shape, dt=F32):
        return nc.dram_tensor(name, list(shape), dt, kind="ExternalInput").ap()

    def dout(name, shape, dt=F32):
        return nc.dram_tensor(name, list(shape), dt, kind="ExternalOutput").ap()

    def dint(name, shape, dt):
        return nc.dram_tensor(name, list(shape), dt, kind="Internal").ap()

    _fregs = {}

    def freg(e, v):
        if v not in _fregs:
            _fregs[v] = e.to_reg(v)
        return _fregs[v]

    def sb(name, shape, dt):
        return stack.enter_context(nc.sbuf_tensor(name, list(shape), dt))

    x_all = din("x", [NSEQ, T, D])
    xs_all = din("xs", [NSEQ, NSMP, D])
    sp_all = din("spool", [NSEQ, NSMP, 15, D])
    p_all = din("p", [NSEQ, 2, T, PLE])
    ps_all = din("psm", [NSEQ, 2, NSMP, PLE])
    vec_d = din("vecs", [8, D])
    hv_d = din("hvecs", [4, HD])
    cs_d = din("cossin", [T + 1, 2, 64])
    wpool_d = din("w_pool", [4, 1024, 1024])
    NL = 2 if "B" in stages else 1
    wup_d = [din("w_up%d" % l, [D, DFF]) for l in range(NL)]
    wdn_d = [din("w_down%d" % l, [DFF, D]) for l in range(NL)]
    wgt_d = [din("w_gate%d" % l, [D, D]) for l in range(NL)]
    wple_d = [din("w_ple%d" % l, [PLE, D]) for l in range(NL)]
    wkv_d = din("w_kv", [D, 6 * 512])
    skw_all = din("st_kwin", [NSEQ, NSMP, WIN, 512])
    svw_all = din("st_vwin", [NSEQ, NSMP, WIN, 512])

    y_all = dout("y", [NSEQ, T, D])
    ys_all = dout("ysm", [NSEQ, NSMP, D])
    poolp_all = dout("pool_p", [NSEQ, 15, D])
    pools_all = dout("pool_s", [NSEQ, NSMP, 15, D])
    kv_all = [dout("kv%d_p" % n, [NSEQ, T, 512]) for n in range(4)]
    win_all = [dout("kwin_p", [NSEQ, WIN, 512]), dout("vwin_p", [NSEQ, WIN, 512])]
    kvs_all = [dout("kv%d_s" % n, [NSEQ, NSMP, 512]) for n in range(4)]
    wins_all = [dout("kwin_s", [NSEQ, NSMP, WIN, 512]), dout("vwin_s", [NSEQ, NSMP, WIN, 512])]

    _once = {}

    def din_once(name, shape, dt=F32):
        if name not in _once:
            _once[name] = din(name, shape, dt)
        return _once[name]

    def sb_once(name, shape, dt):
        if name not in _once:
            _once[name] = sb(name, shape, dt)
        return _once[name]

    h1_s = dint("h1_scr", [NT + 1, 128, KC * TW], F32)
    a1_s = dint("a1_scr", [NT + 1, 128, KC * TW], BF16)
    oT_s = dint("oT_scr", [NT + 1, 128, KC * TW], BF16)
    kw_s = [dint("kwin_scr", [T, 512], F32), dint("vwin_scr", [T, 512], F32)]
    dbg_o = dout("dbg", [6, 128, KC * 2]) if debug else None

    def dbg_dump(i, waits):
        if dbg_o is None:
            return
        ring_out.issue("sp", lambda e: e.dma_start(out=dbg_o[i].rearrange("p (k t) -> p k t", k=KC), in_=h[:, :, 0:2]),
                       waits=waits)

    h = sb("h", [128, KC, TW], F32)
    a = sb("a", [128, KC, TW], BF16)
    scA = sb("scA", [128, 8, TW + 16], F32)
    scB = sb("scB", [128, 8, TW + 16], F32)
    NWB = 2
    wb = [sb("wb%d" % i, [128, 8192], BF16) for i in range(NWB)]
    hx = sb("hx", [128, 1536], F32)
    halo = hx[:, 0:512].rearrange("p (k t) -> p k t", k=KC)
    rstd = sb("rstd", [128, TW], F32)
    rt = sb("rt", [128, TW], F32)
    gate_t = sb("gate_t", [128, 2, TW], F32)
    gcol = sb("gcol", [128, 8, KC], F32)
    ident = sb("ident", [128, 128], F32)
    ones_bf = sb("ones_bf", [128, 128], BF16)
    ones_f = sb("ones_f", [128, 128], F32)
    eps_t = sb("eps_t", [128, 1], F32)
    mhalf = sb("mhalf", [128, 1], F32)
    invc = hx[:, 512:1024].rearrange("p (g k t) -> p g k t", g=4, k=8)
    pT = sb("pT", [128, 2, TW], BF16)
    stg = [sb("stg%d" % i, [128, 512], F32) for i in range(3)]
    stgT = []
    cs_t = sb("cs_t", [128, 2, 64], F32)
    hvb = sb("hvb", [128, 4, HD], F32)
    ss4 = sb("ss4", [128, 8], F32)
    rp = [sb("rp%d" % i, [128, 4, 64], F32) for i in range(4)]
    junk = sb("junk", [128, 512], F32)
    junk2 = sb("junk2", [128, 2, 512], F32)

    u_bf = scA[:].bitcast(BF16) if hasattr(scA[:], "bitcast") else None

    ps = [stack.enter_context(nc.psum_tensor("ps%d" % i, [128, 512], F32)) for i in range(8)]

    ps_free = [None] * 8
    wb_free = [None] * NWB
    wb_sem = [P.newsem("wbs%d" % i) for i in range(NWB)]
    wb_ctr = [0]
    out_tokens = []

    class Ring:
        def __init__(self, name, k):
            self.sems = [P.newsem("%s%d" % (name, i)) for i in range(k)]
            self.last = [None] * k
            self.i = 0

        def issue(self, eng, fn, waits=()):
            i = self.i % len(self.sems)
            self.i += 1
            tk = P.op(eng, fn, waits=list(waits) + [self.last[i]], sem=self.sems[i], dma=True)
            self.last[i] = tk
            return tk

    ring_in = Ring("rin", 4)
    ring_out = Ring("rout", 8)
    ring_pl = Ring("rpl", 2)

    tiny0 = sb_once("tiny", [128, 8], F32)

    def top_barrier():
        toks = [P.op("act", lambda e: e.activation(out=tiny0[:, 0:1], in_=tiny0[:, 0:1], func=AF.Copy), sig=True),
                P.op("dve", lambda e: e.tensor_copy(out=tiny0[:, 1:2], in_=tiny0[:, 1:2]), sig=True),
                P.op("pool", lambda e: e.memset(tiny0[:, 2:3], 0.0), sig=True),
                P.lastsig.get("pe")]
        for rg in (ring_in, ring_out, ring_pl):
            toks += [t for t in rg.last if t is not None]
        for eng in ("pe", "act", "dve", "pool", "sp"):
            P.op(eng, None, waits=toks)

    def run_pass(sq):
        x_d = x_all[sq]
        xs_d = xs_all[sq]
        sp_d = sp_all[sq]
        p_d = p_all[sq]
        ps_d = ps_all[sq]
        skw_d = skw_all[sq]
        svw_d = svw_all[sq]
        y_o = y_all[sq]
        ys_o = ys_all[sq]
        poolp_o = poolp_all[sq]
        pools_o = pools_all[sq]
        kv_o = [t_[sq] for t_ in kv_all]
        win_o = [t_[sq] for t_ in win_all]
        kvs_o = [t_[sq] for t_ in kvs_all]
        wins_o = [t_[sq] for t_ in wins_all]
        if sq > 0:
            P.strict = set()
            top_barrier()
        setup = []
        setup.append(P.op("pool", lambda e: e.memset(ones_f[:], 1.0)))
        P.op("pool", lambda e: e.memset(ones_bf[:], 1.0))
        P.op("pool", lambda e: e.memset(eps_t[:], EPS))
        P.op("pool", lambda e: e.memset(mhalf[:], -0.5))
        P.op("pool", lambda e: e.memset(halo[:], 0.0))
        P.op("pool", lambda e: e.affine_select(out=ident[:], in_=ones_f[:], pattern=[[1, 128]],
                                                compare_op=ALU.is_equal, fill=freg(e, 0.0), base=0,
                                                channel_multiplier=-1))
        for g, w in enumerate((2, 4, 8, 16)):
            P.op("pool", lambda e, g=g: e.iota(out=invc[:, g, :, :], pattern=[[0, 8], [1, 16]], base=1,
                                               channel_multiplier=0, allow_small_or_imprecise_dtypes=True))
            P.op("pool", lambda e, g=g, w=w: e.tensor_scalar(out=invc[:, g, :, :], in0=invc[:, g, :, :],
                                                            scalar1=float(w), scalar2=None, op0=ALU.min))
        t_const = P.op("pool", lambda e: e.nop() if False else e.memset(junk[:], 0.0), sig=True)
        P.op("dve", lambda e: e.reciprocal(out=invc[:], in_=invc[:]), waits=[t_const])
        t_const2 = P.op("dve", lambda e: e.tensor_copy(out=junk[:, 0:1], in_=junk[:, 0:1]), sig=True)

        vv = vec_d.rearrange("v (kc p) -> (v kc) p", p=128)
        sc_b_free0 = [None]
        for half in range(2):
            tk = ring_in.issue("sp", lambda e, half=half: e.dma_start(out=scB[:, 0, 0:128], in_=vv[half * 128:(half + 1) * 128, :]),
                               waits=[ps_free[4], sc_b_free0[0]])
            tp = P.op("pe", lambda e: e.transpose(out=ps[4][:, 0:128], in_=scB[:, 0, 0:128], identity=ident[:]),
                      waits=[tk, t_const2, t_const], sig=True)
            tc = P.op("act", lambda e, half=half: e.activation(
                out=gcol[:, half * 4:(half + 1) * 4, :],
                in_=ps[4][:, 0:128].rearrange("p (v k) -> p v k", v=4), func=AF.Copy), waits=[tp], sig=True)
            ps_free[4] = tc
            tk2 = tp
            if half == 0:
                P.op("sp", lambda e: e.nop() if hasattr(e, "nop") else None, waits=[tp]) if False else None
            sc_b_free = tp
            sc_b_free0[0] = tp
        GV = dict(g_mix0=0, g_mix1=1, pscale=2, g_kv=3, g_ffn0=4, g_ffn1=5, g_ple0=6, g_ple1=7)
        tk = ring_in.issue("sp", lambda e: e.dma_start(out=hvb[:].rearrange("p v h -> p (v h)"),
                                                       in_=hv_d.rearrange("v h -> (v h)").partition_broadcast(128)))
        t_hvb = tk

        state = dict(scB_free=sc_b_free, scA_free=None, a_free=None, stg_free=[None] * 3, stgT_free=[None] * 2,
                     stg_i=0, stgT_i=0, misc_i=0)

        def misc_bank():
            state["misc_i"] ^= 1
            return 4 + state["misc_i"]

        def load_T(src_rows_ap, rows, ncols, dst_fn, wait_extra=()):
            nchunk = ncols // 128
            stage = scB[:].rearrange("p a b -> p (a b)")
            tk = ring_in.issue("sp", lambda e: e.dma_start(out=stage[0:rows, 0:ncols], in_=src_rows_ap),
                               waits=[state["scB_free"]] + list(wait_extra))
            last = None
            lastpe = None
            for c0 in range(0, nchunk, 4):
                n = min(4, nchunk - c0)
                b = misc_bank()
                for i in range(n):
                    lastpe = P.op("pe", lambda e, b=b, i=i, c=c0 + i: e.transpose(
                        out=ps[b][:, i * rows:(i + 1) * rows], in_=stage[0:rows, c * 128:(c + 1) * 128],
                        identity=ident[0:rows, 0:rows]), waits=[tk, ps_free[b], t_const2], sig=(i == n - 1))
                eng, fn = dst_fn(c0, n, ps[b][:, 0:n * rows], rows)
                last = P.op(eng, fn, waits=[lastpe], sig=True)
                ps_free[b] = last
            state["scB_free"] = lastpe
            return last

        def norm_stats(W, wait=()):
            t1 = P.op("act", lambda e: e.activation(out=a[:, :, 0:W], in_=h[:, :, 0:W], func=AF.Square),
                      waits=list(wait) + [state["a_free"]], sig=True)
            tp = None
            for kc in range(KC):
                tp = P.op("pe", lambda e, kc=kc: e.matmul(ps[6][:, 0:W], lhsT=ones_bf[:], rhs=a[:, kc, 0:W],
                                                           start=(kc == 0), stop=(kc == KC - 1)),
                          waits=[t1, ps_free[6]], sig=(kc == KC - 1))
            t2 = P.op("dve", lambda e: e.tensor_scalar(out=rt[:, 0:W], in0=ps[6][:, 0:W], scalar1=1.0 / D, scalar2=EPS,
                                                       op0=ALU.mult, op1=ALU.add), waits=[tp], sig=True)
            ps_free[6] = t2
            t3 = P.op("pool", lambda e: e.tensor_tensor(out=rstd[:, 0:W], in0=rt[:, 0:W],
                                                        in1=mhalf[:, 0:1].to_broadcast([128, W]), op=ALU.pow),
                      waits=[t2], sig=True)
            state["a_free"] = tp
            return t3

        def normalize(W, gi, wait=()):
            tk = None
            for kc in range(KC):
                tk = P.op("dve", lambda e, kc=kc: e.scalar_tensor_tensor(
                    out=a[:, kc, 0:W], in0=h[:, kc, 0:W], scalar=gcol[:, gi, kc:kc + 1], in1=rstd[:, 0:W],
                    op0=ALU.mult, op1=ALU.mult), waits=list(wait) + [state["a_free"]], sig=(kc == KC - 1))
            return tk

        def wload(src_ap, kcb, nb, waits=()):
            i = wb_ctr[0] % NWB
            wb_ctr[0] += 1
            dst = wb[i][:, 0:kcb * nb].rearrange("p (k n) -> p k n", k=kcb)
            tk = P.op("pool", lambda e: e.dma_start(out=dst, in_=src_ap), waits=[wb_free[i]] + list(waits),
                      sem=wb_sem[i], dma=True)
            return i, dst, tk

        def dense_fm(Wv, k0, nk, c0, ncols, xin, W, epi, banks=(0, 1, 2, 3), waits=()):
            NB = 512
            KB = min(nk, 16)
            blocks = []
            for cb in range(0, ncols, NB):
                nb = min(NB, ncols - cb)
                for kb in range(0, nk, KB):
                    blocks.append((cb, nb, kb, min(KB, nk - kb)))
            loaded = {}
            depth = NWB - 1
            bi = [0]
            last_epi = None
            for i in range(len(blocks) + depth):
                if i < len(blocks):
                    cb, nb, kb, kn = blocks[i]
                    loaded[i] = wload(Wv[:, k0 + kb:k0 + kb + kn, c0 + cb:c0 + cb + nb], kn, nb, ())
                j = i - depth
                if j < 0:
                    continue
                cb, nb, kb, kn = blocks[j]
                slot, wv, tk = loaded.pop(j)
                nfc = nb // 128
                if kb == 0:
                    cur = []
                    for f in range(nfc):
                        cur.append(banks[bi[0] % len(banks)])
                        bi[0] += 1
                    state["cur_banks"] = cur
                cur = state["cur_banks"]
                tp = None
                for f in range(nfc):
                    for k in range(kn):
                        first = (kb == 0 and k == 0)
                        last = (kb + kn == nk and k == kn - 1)
                        tp = P.op("pe", lambda e, b=cur[f], wv=wv, k=k, f=f, kk=kb + k, first=first, last=last:
                                  e.matmul(ps[b][:, 0:W], lhsT=wv[:, k, f * 128:(f + 1) * 128], rhs=xin(kk),
                                           start=first, stop=last),
                                  waits=[tk, ps_free[cur[f]]] + list(waits), sig=(k == kn - 1))
                    if kb + kn == nk:
                        tl = epi((cb // 128) + f, ps[cur[f]][:, 0:W], tp)
                        ps_free[cur[f]] = tl
                        last_epi = tl
                wb_free[slot] = tp
            return last_epi

        def store_rows(dst_ap, src_ap, waits):
            return ring_out.issue("sp", lambda e: e.dma_start(out=dst_ap, in_=src_ap), waits=waits)

        def next_stg():
            i = state["stg_i"] % 3
            state["stg_i"] += 1
            return i

        wpool_v = [wpool_d[g].rearrange("(kc p) f -> p kc f", p=128) for g in range(4)]
        wup_v = [w.rearrange("(kc p) f -> p kc f", p=128) for w in wup_d]
        wdn_v = [w.rearrange("(kc p) f -> p kc f", p=128) for w in wdn_d]
        wgt_v = [w.rearrange("(kc p) f -> p kc f", p=128) for w in wgt_d]
        wple_v = [w.rearrange("(kc p) f -> p kc f", p=128) for w in wple_d]
        wkv_v = wkv_d.rearrange("(kc p) f -> p kc f", p=128)
        POOLW = (2, 4, 8, 16)

        def mlp_and_ple(layer, W, p_rows_ap, rows_list, wait):
            t3 = norm_stats(W, wait=wait)
            tn = normalize(W, GV["g_ffn%d" % layer], wait=[t3])
            u = scA[:].rearrange("p a b -> p (a b)").bitcast(BF16)[:, 0:8 * TW].rearrange("p (k t) -> p k t", k=8)
            tlast = tn
            state["u_free"] = state.get("scA_free")
            for s in range(DFF // 1024):
                def epi_up(fc, pp, tp, s=s):
                    sl = state.get("jslot", 0)
                    state["jslot"] = sl ^ 1
                    jf = state.setdefault("jfree", [None, None])
                    t1 = P.op("act", lambda e: e.activation(out=junk2[:, sl, 0:W], in_=pp, func=AF.Relu),
                              waits=[tp, jf[sl]], sig=True)
                    t2 = P.op("dve", lambda e: e.tensor_tensor(out=u[:, fc, 0:W], in0=junk2[:, sl, 0:W],
                                                               in1=junk2[:, sl, 0:W], op=ALU.mult), waits=[t1], sig=True)
                    jf[sl] = t2
                    state["last_pool_u"] = t2
                    return t1
                tu = dense_fm(wup_v[layer], 0, KC, s * 1024, 1024, lambda kk: a[:, kk, 0:W], W, epi_up,
                              banks=(0, 1, 2, 3), waits=[tn, state["u_free"]])

                def epi_dn(j, pp, tp):
                    return P.op("dve", lambda e: e.tensor_tensor(out=h[:, j, 0:W], in0=pp, in1=h[:, j, 0:W], op=ALU.add),
                                waits=[tp], sig=True)
                td = dense_fm(wdn_v[layer], s * 8, 8, 0, D, lambda kk: u[:, kk, 0:W], W, epi_dn,
                              banks=(4, 5, 6, 7), waits=[tu, state["last_pool_u"]])
                state["u_free"] = td
                tlast = td
            state["scA_free"] = tlast
            if W == NSMP:
                dbg_dump(2, [tlast])
            t3 = norm_stats(W, wait=[tlast])
            tn = normalize(W, GV["g_ple%d" % layer], wait=[t3])
            wple_sb = scA[:].rearrange("p a b -> p (a b)").bitcast(BF16)[:, 0:2 * D].rearrange("p (k f) -> p k f", k=2)
            twp = ring_pl.issue("pool", lambda e: e.dma_start(out=wple_sb, in_=wple_v[layer]), waits=[tlast])
            tpt = None
            off = 0
            for (src, rows) in rows_list:
                def dst(c0, n, pp, rows, off=off):
                    return ("act", lambda e: e.activation(
                        out=pT[:, c0:c0 + n, off:off + rows], in_=pp.rearrange("p (n r) -> p n r", n=n), func=AF.Copy))
                tpt = load_T(src, rows, PLE, dst)
                off += rows

            def epi_gate(j, pp, tp):
                gi = j % 2
                gf = state.setdefault("gfree", [None, None])
                t1 = P.op("act", lambda e: e.activation(out=gate_t[:, gi, 0:W], in_=pp, func=AF.Sigmoid),
                          waits=[tp, gf[gi]], sig=True)
                P.op("pe", lambda e: e.matmul(ps[7][:, 0:W], lhsT=wple_sb[:, 0, j * 128:(j + 1) * 128],
                                              rhs=pT[:, 0, 0:W], start=True, stop=False), waits=[ps_free[7], twp, tpt])
                t2 = P.op("pe", lambda e: e.matmul(ps[7][:, 0:W], lhsT=wple_sb[:, 1, j * 128:(j + 1) * 128],
                                                   rhs=pT[:, 1, 0:W], start=False, stop=True), sig=True)
                t3 = P.op("dve", lambda e: e.tensor_tensor(out=gate_t[:, gi, 0:W], in0=ps[7][:, 0:W],
                                                           in1=gate_t[:, gi, 0:W], op=ALU.mult), waits=[t1, t2], sig=True)
                ps_free[7] = t3
                t4 = P.op("dve", lambda e: e.tensor_tensor(out=h[:, j, 0:W], in0=gate_t[:, gi, 0:W],
                                                           in1=h[:, j, 0:W], op=ALU.add), sig=True)
                gf[gi] = t4
                return t1
            tg = dense_fm(wgt_v[layer], 0, KC, 0, D, lambda kk: a[:, kk, 0:W], W, epi_gate,
                          banks=(0, 1, 2, 3), waits=[tn, twp, tpt])
            gf = state["gfree"]
            tend = gf[0] if (gf[1] is None or (gf[0] is not None and gf[0][1] > gf[1][1])) else gf[1]
            state["scA_free"] = tend
            return tend

        def stage_a_tile(ti):
            sample = (ti == NT)
            W = NSMP if sample else TW
            t0 = 0 if sample else ti * TW
            tl = None
            nblk = 1 if sample else W // 128
            for tb in range(nblk):
                rows = W if sample else 128
                src = xs_d[:, :] if sample else x_d[t0 + tb * 128:t0 + (tb + 1) * 128, :]

                def dst(c0, n, pp, rows, tb=tb):
                    eng = "act" if (c0 // 4) % 2 == 0 else "dve"
                    if eng == "act":
                        return (eng, lambda e: e.activation(out=h[:, c0:c0 + n, tb * 128:tb * 128 + rows],
                                                            in_=pp.rearrange("p (n r) -> p n r", n=n), func=AF.Copy))
                    return (eng, lambda e: e.tensor_copy(out=h[:, c0:c0 + n, tb * 128:tb * 128 + rows],
                                                         in_=pp.rearrange("p (n r) -> p n r", n=n)))
                tl = load_T(src, rows, D, dst, wait_extra=[state.get("h_free"), state.get("h_free2")])
            if sample:
                dbg_dump(0, [tl])
            t3 = norm_stats(W, wait=[tl])
            seq = scA
            sB = scB
            pool_halos = [None] if not sample else list(range(NSMP))
            t_front = None
            def front(hb):
                nonlocal t_front
                if sample:
                    def dsth(c0, n, pp, rows):
                        return ("act", lambda e: e.activation(out=halo[:, c0:c0 + n, 0:15],
                                                              in_=pp.rearrange("p (n r) -> p n r", n=n), func=AF.Copy))
                    th = load_T(sp_d[hb], 15, D, dsth, wait_extra=[state.get("halo_free")])
                    col0, Wf = hb, 1
                else:
                    th = None
                    col0, Wf = 0, W
                L = 15 + Wf
                for g in range(4):
                    w = POOLW[g]
                    tw = [t3, th, state["scA_free"], state["scB_free"], state["a_free"]]
                    for k in range(8):
                        kc = 8 * g + k
                        P.op("dve", lambda e, k=k, kc=kc: e.scalar_tensor_tensor(
                            out=seq[:, k, 15:15 + Wf], in0=h[:, kc, col0:col0 + Wf], scalar=gcol[:, GV["g_mix0"], kc:kc + 1],
                            in1=rstd[:, col0:col0 + Wf], op0=ALU.mult, op1=ALU.mult), waits=tw)
                    P.op("dve", lambda e, g=g: e.tensor_copy(out=seq[:, :, 0:15], in_=halo[:, 8 * g:8 * g + 8, 0:15]),
                         waits=tw)
                    P.op("dve", lambda e: e.tensor_tensor(out=sB[:, :, 15:L], in0=seq[:, :, 15:L], in1=seq[:, :, 14:L - 1],
                                                          op=ALU.add), waits=tw)
                    for j in range(2, w):
                        P.op("dve", lambda e, j=j: e.tensor_tensor(out=sB[:, :, 15:L], in0=sB[:, :, 15:L],
                                                                   in1=seq[:, :, 15 - j:L - j], op=ALU.add))
                    if sample:
                        dsl = gate_t[:].rearrange("p a b -> p (a b)")[:, 0:KC * NSMP].rearrange(
                            "p (k t) -> p k t", k=KC)[:, 8 * g:8 * g + 8, col0:col0 + Wf]
                    else:
                        dsl = a[:, 8 * g:8 * g + 8, col0:col0 + Wf]
                    if (not sample) and ti == 0:
                        P.op("dve", lambda e, g=g: e.tensor_tensor(out=sB[:, :, 15:31], in0=sB[:, :, 15:31],
                                                                   in1=invc[:, g, :, :], op=ALU.mult))
                        P.op("dve", lambda e, g=g: e.tensor_tensor(out=a[:, 8 * g:8 * g + 8, 0:16], in0=sB[:, :, 15:31],
                                                                   in1=seq[:, :, 15:31], op=ALU.subtract))
                        P.op("dve", lambda e, g=g, w=w: e.scalar_tensor_tensor(
                            out=a[:, 8 * g:8 * g + 8, 16:Wf], in0=sB[:, :, 31:L], scalar=1.0 / w, in1=seq[:, :, 31:L],
                            op0=ALU.mult, op1=ALU.subtract))
                    else:
                        P.op("dve", lambda e, w=w, dsl=dsl: e.scalar_tensor_tensor(
                            out=dsl, in0=sB[:, :, 15:L], scalar=1.0 / w, in1=seq[:, :, 15:L],
                            op0=ALU.mult, op1=ALU.subtract))
                    t_front = P.op("dve", lambda e, g=g: e.tensor_copy(out=halo[:, 8 * g:8 * g + 8, 0:15],
                                                                       in_=seq[:, :, L - 15:L]), sig=True)
                    state["scA_free"] = t_front
                    state["scB_free"] = t_front
                if sample or ti == NT - 1:
                    dst_rows = pools_o[hb] if sample else poolp_o
                    si = next_stg()
                    stage = scB[:].rearrange("p a b -> p (a b)")
                    tpe = None
                    for c0 in range(0, KC, 4):
                        b = misc_bank()
                        for i in range(4):
                            tpe = P.op("pe", lambda e, b=b, i=i, kc=c0 + i: e.transpose(
                                out=ps[b][0:15, i * 128:(i + 1) * 128], in_=halo[:, kc, 0:15], identity=ident[:]),
                                waits=[t_front, ps_free[b]], sig=(i == 3))
                        tcp = P.op("act", lambda e, b=b, c0=c0: e.activation(out=stage[0:15, c0 * 128:(c0 + 4) * 128],
                                                                               in_=ps[b][0:15, 0:512], func=AF.Copy),
                                   waits=[tpe, state["scB_free"]], sig=True)
                        ps_free[b] = tcp
                    tst = store_rows(dst_rows, stage[0:15, 0:D], [tcp])
                    state["scB_free"] = tst
                    state["halo_free"] = tpe

            for hb in pool_halos:
                front(hb)
            if sample:
                dtmp = gate_t[:].rearrange("p a b -> p (a b)")[:, 0:KC * NSMP].rearrange("p (k t) -> p k t", k=KC)
                t_front = P.op("dve", lambda e: e.tensor_copy(out=a[:, :, 0:NSMP], in_=dtmp), sig=True)
                if dbg_o is not None:
                    ring_out.issue("sp", lambda e: e.dma_start(out=dbg_o[4].rearrange("p (k t) -> p k t", k=KC), in_=dtmp),
                                   waits=[t_front])
            state["a_free"] = None
            tmix = None
            for g in range(4):
                def epi_pool(ec, pp, tp, g=g):
                    kc = 8 * g + ec
                    return P.op("dve", lambda e: e.scalar_tensor_tensor(
                        out=h[:, kc, 0:W], in0=pp, scalar=gcol[:, GV["pscale"], kc:kc + 1], in1=h[:, kc, 0:W],
                        op0=ALU.mult, op1=ALU.add), waits=[tp], sig=True)
                tmix = dense_fm(wpool_v[g], 0, 8, 0, 1024, lambda kk, g=g: a[:, 8 * g + kk, 0:W], W, epi_pool,
                                banks=(0, 1, 2, 3), waits=[t_front])
            state["a_free"] = tmix
            if sample:
                dbg_dump(1, [tmix])
            if sample:
                rows_list = [(ps_d[0], NSMP)]
            else:
                rows_list = [(p_d[0, t0 + tb * 128:t0 + (tb + 1) * 128, :], 128) for tb in range(4)]
            th1 = mlp_and_ple(0, W, None, rows_list, [tmix])
            if sample:
                dbg_dump(3, [th1])
            tsp = ring_out.issue("sp", lambda e: e.dma_start(
                out=h1_s[ti].rearrange("p (k t) -> p k t", k=KC)[:, :, 0:W], in_=h[:, :, 0:W]), waits=[th1])
            state["h1_tok_%d" % ti] = tsp
            t3b = norm_stats(W, wait=[th1])
            tnb = normalize(W, GV["g_mix1"], wait=[t3b])
            ta1 = ring_out.issue("sp", lambda e: e.dma_start(
                out=a1_s[ti].rearrange("p (k t) -> p k t", k=KC)[:, :, 0:W], in_=a[:, :, 0:W]), waits=[tnb])
            t3 = norm_stats(W, wait=[th1, ta1])
            tn = normalize(W, GV["g_kv"], wait=[t3])
            for n in range(6):
                halves = []
                for hf in range(2):
                    halves.append(wload(wkv_v[:, hf * 16:(hf + 1) * 16, n * 512:(n + 1) * 512], 16, 512, [tn]))
                tps = []
                for hf in range(2):
                    slot, wv, tk = halves[hf]
                    tp = None
                    for tb in range(nblk):
                        rows = W if sample else 128
                        for k in range(16):
                            kk = hf * 16 + k
                            tp = P.op("pe", lambda e, tb=tb, rows=rows, wv=wv, k=k, kk=kk: e.matmul(
                                ps[tb][0:rows, 0:512], lhsT=a[:, kk, tb * 128:tb * 128 + rows], rhs=wv[:, k, :],
                                start=(kk == 0), stop=(kk == KC - 1)), waits=[tk, tn, ps_free[tb]], sig=(k == 15))
                        if hf == 1:
                            tps.append(tp)
                    wb_free[slot] = tp
                for tb in range(nblk):
                    rows = W if sample else 128
                    r0 = t0 + tb * 128
                    si = next_stg()
                    sg = stg[si]
                    wfree = state["stg_free"][si]
                    if n in (2, 4):
                        gi = 1 if n == 2 else 2
                        tq1 = P.op("act", lambda e, tb=tb, rows=rows: e.activation(
                            out=junk[0:rows, 0:512], in_=ps[tb][0:rows, 0:512], func=AF.Square),
                            waits=[tps[tb], state.get("junk_free")], sig=True)
                        tq2 = P.op("dve", lambda e, rows=rows: e.tensor_reduce(
                            out=ss4[0:rows, 0:4], in_=junk[0:rows, 0:512].rearrange("p (g h) -> p g h", g=4),
                            axis=mybir.AxisListType.X, op=ALU.add), waits=[tq1], sig=True)
                        state["junk_free"] = tq2
                        tss = P.op("act", lambda e, rows=rows: e.activation(out=ss4[0:rows, 4:8], in_=ss4[0:rows, 0:4],
                                                                            func=AF.Sqrt, bias=eps_t[0:rows, :], scale=1.0 / HD),
                                   waits=[tq2], sig=True)
                        tq4 = P.op("dve", lambda e, rows=rows: e.reciprocal(out=ss4[0:rows, 4:8], in_=ss4[0:rows, 4:8]),
                                   waits=[tss, wfree, t_hvb], sig=True)
                        state["tq4"] = tq4
                        for g in range(4):
                            P.op("dve", lambda e, tb=tb, g=g, rows=rows, sg=sg, gi=gi: e.scalar_tensor_tensor(
                                out=sg[0:rows, g * 128:(g + 1) * 128], in0=ps[tb][0:rows, g * 128:(g + 1) * 128],
                                scalar=ss4[0:rows, 4 + g:5 + g], in1=hvb[0:rows, gi, :], op0=ALU.mult, op1=ALU.mult),
                                force=[state["tq4"]] if g == 0 else [])
                        crow = cs_d[T:T + 1].partition_broadcast(rows) if False else None
                        if sample:
                            tcs = ring_in.issue("sp", lambda e, rows=rows: e.dma_start(
                                out=cs_t[0:rows].rearrange("p a b -> p (a b)"),
                                in_=cs_d[T].rearrange("a b -> (a b)").partition_broadcast(rows)),
                                waits=[state.get("cs_free")])
                        else:
                            tcs = ring_in.issue("sp", lambda e, r0=r0: e.dma_start(out=cs_t[:], in_=cs_d[r0:r0 + 128]),
                                                waits=[state.get("cs_free")])
                        sgv = sg[0:rows, :].rearrange("p (g h) -> p g h", g=4)
                        x1 = sgv[:, :, 0:64]
                        x2 = sgv[:, :, 64:128]
                        cosb = cs_t[0:rows, 0:1, :].to_broadcast([rows, 4, 64]) if hasattr(cs_t[0:rows, 0:1, :], "to_broadcast") else None
                        sinb = cs_t[0:rows, 1:2, :].to_broadcast([rows, 4, 64]) if hasattr(cs_t[0:rows, 1:2, :], "to_broadcast") else None
                        r = [rp[i][0:rows] for i in range(4)]
                        P.op("dve", lambda e, x1=x1, cosb=cosb, r=r: e.tensor_tensor(out=r[0], in0=x1, in1=cosb, op=ALU.mult),
                             waits=[tcs])
                        P.op("dve", lambda e, x2=x2, sinb=sinb, r=r: e.tensor_tensor(out=r[1], in0=x2, in1=sinb, op=ALU.mult))
                        P.op("dve", lambda e, x2=x2, cosb=cosb, r=r: e.tensor_tensor(out=r[2], in0=x2, in1=cosb, op=ALU.mult))
                        P.op("dve", lambda e, x1=x1, sinb=sinb, r=r: e.tensor_tensor(out=r[3], in0=x1, in1=sinb, op=ALU.mult))
                        P.op("dve", lambda e, x1=x1, r=r: e.tensor_tensor(out=x1, in0=r[0], in1=r[1], op=ALU.subtract))
                        tev = P.op("dve", lambda e, x2=x2, r=r: e.tensor_tensor(out=x2, in0=r[2], in1=r[3], op=ALU.add),
                                   sig=True)
                        state["cs_free"] = tev
                    else:
                        tev = P.op("act", lambda e, tb=tb, rows=rows, sg=sg: e.activation(
                            out=sg[0:rows, :], in_=ps[tb][0:rows, 0:512], func=AF.Copy), waits=[tps[tb], wfree], sig=True)
                    ps_free[tb] = tev
                    toks = []
                    if sample:
                        if n < 4:
                            toks.append(store_rows(kvs_o[n][:, :], sg[0:rows, :], [tev]))
                        else:
                            for b in range(NSMP):
                                toks.append(store_rows(wins_o[n - 4][b, WIN - 1:WIN, :], sg[b:b + 1, :], [tev]))
                    else:
                        if n < 4:
                            toks.append(store_rows(kv_o[n][r0:r0 + 128, :], sg[:, :], [tev]))
                        else:
                            toks.append(store_rows(kw_s[n - 4][r0:r0 + 128, :], sg[:, :], [tev]))
                            if r0 >= T - WIN:
                                toks.append(store_rows(win_o[n - 4][r0 - (T - WIN):r0 - (T - WIN) + 128, :], sg[:, :], [tev]))
                    if toks:
                        state["stg_free"][si] = toks[-1]
                    else:
                        state["stg_free"][si] = tev
            state["a_free"] = tp
            state["h_free"] = tp
            state["h_free2"] = tsp

        if "A" in stages:
            ntiles = NT + 1
            for ti in range(ntiles):
                if isinstance(stages, dict) and ti not in stages["A"]:
                    continue
                P.strict = {"act", "dve", "pool"} if ti == NT else set()
                stage_a_tile(ti)
            P.strict = set()
            for i, (src, dst) in enumerate(((skw_d, wins_o[0]), (svw_d, wins_o[1]))):
                for b in range(NSMP):
                    store_rows(dst[b, 0:WIN - 1, :], src[b, 1:WIN, :], [])


        if "B" in stages:
            SCALE = float(HD) ** -0.5
            NEGF = -1e30
            wqg_d = din_once("w_qg", [D, 4192])
            wo_d = din_once("w_o", [D, D])
            wc1_d = [din_once("w_cmp_k1", [4096, 256]), din_once("w_cmp_v1", [4096, 256])]
            wc2_d = [din_once("w_cmp_k2", [256, 128]), din_once("w_cmp_v2", [256, 128])]
            pe_d = [din_once("pe_cmp_k", [32, 128]), din_once("pe_cmp_v", [32, 128])]
            wqg_v = wqg_d.rearrange("(kc p) f -> p kc f", p=128)
            wo_v = wo_d.rearrange("(kc p) f -> p kc f", p=128)

            vsel_g = sb_once("vsel_g", [128, 16, 128], BF16)
            vwin_g = sb_once("vwin_g", [128, 16, 128], BF16)
            identb = sb_once("identb", [128, 128], BF16)
            tiny = sb_once("tiny", [128, 8], F32)
            bias_sb = sb_once("bias_sb", [128, 4], F32)
            gk_col = sb_once("gk_col", [128, 1], F32)
            m8a = sb_once("m8a", [128, 8], F32)
            m8b = sb_once("m8b", [128, 8], F32)

            def barrier():
                toks = [P.op("act", lambda e: e.activation(out=tiny[:, 0:1], in_=tiny[:, 0:1], func=AF.Copy), sig=True),
                        P.op("dve", lambda e: e.tensor_copy(out=tiny[:, 1:2], in_=tiny[:, 1:2]), sig=True),
                        P.op("pool", lambda e: e.memset(tiny[:, 2:3], 0.0), sig=True),
                        P.lastsig.get("pe")]
                for rg in (ring_in, ring_out, ring_pl):
                    toks += [t for t in rg.last if t is not None]
                for eng in ("pe", "act", "dve", "pool", "sp"):
                    P.op(eng, None, waits=toks)

            P.op("pool", lambda e: e.memset(tiny[:], 0.0))
            barrier()
            P.strict = {"act", "dve", "pool"}

            hf = h[:].rearrange("p a b -> p (a b)")
            hbv = hf.bitcast(BF16)
            kselT = hbv[:, 0:8192].rearrange("p (g t) -> p g t", g=4)
            kwinT = hbv[:, 8192:16384].rearrange("p (g t) -> p g t", g=4)
            maskbuf = hbv[:, 16384:24576].rearrange("p (k q) -> p k q", k=16)
            acc = hf[:, 12288:16384].rearrange("p (r q) -> p r q", r=8)
            w1_sb = hbv[:, 16384:24576].rearrange("p (r h) -> p r h", r=32)
            sT = hbv[:, 24576:25600].rearrange("p (c n) -> p c n", c=2)
            w2_sb = hbv[:, 25600:25856].rearrange("p (c h) -> p c h", c=2)
            peT = hbv[:, 25856:25888]
            sqb = hbv[:, 26112:26624]
            scAb = scA[:].rearrange("p a b -> p (a b)").bitcast(BF16)
            qT = scAb[:, 0:4096].rearrange("p (r q) -> p r q", r=8)
            qrT = scAb[:, 4096:8192].rearrange("p (r q) -> p r q", r=8)
            scBb = scB[:].rearrange("p a b -> p (a b)").bitcast(BF16)
            mc = [scBb[:, d_ * 512:(d_ + 1) * 512] for d_ in range(4)]
            mw = [scBb[:, 2048 + d_ * 512:2048 + (d_ + 1) * 512] for d_ in range(4)]
            e_sb = [scBb[:, 4096:4608], scBb[:, 4608:5120]]
            pt_sb = [scBb[:, 5120:5632], scBb[:, 5632:6144]]
            cmask = scBb[:, 6144:6656]
            pn_sb = scBb[:, 6656:7168]
            Asel = scBb[:, 7168:7200]
            Asel2 = scBb[:, 7200:7232]
            selT = scBb[:, 7232:7744]
            Eexp = gate_t[:].rearrange("p a b -> p (a b)").bitcast(BF16).rearrange("p (k m) -> p k m", k=16)
            gatesT = rt[:].bitcast(BF16)[:, 0:512]
            pTb = pT[:]
            ckT = pTb[:, 0, :]
            cv_sb = pTb[:, 1, :].rearrange("p (g h) -> p g h", g=4)
            pslc_sb = rstd
            rec_sb = stg[2]
            coef_sb = junk
            rowst = junk2

            P.op("dve", lambda e: e.tensor_copy(out=identb[:], in_=ident[:]))
            for d_ in range(4):
                P.op("pool", lambda e, d_=d_: e.memset(mc[d_], 1.0))
                P.op("pool", lambda e, d_=d_: e.affine_select(out=mc[d_], in_=mc[d_], pattern=[[1, 512]], compare_op=ALU.is_ge,
                                                              fill=freg(e, 0.0), base=-128 * d_, channel_multiplier=-1))
                P.op("pool", lambda e, d_=d_: e.memset(mw[d_], 1.0))
                P.op("pool", lambda e, d_=d_: e.affine_select(out=mw[d_], in_=mw[d_], pattern=[[-1, 512]], compare_op=ALU.is_gt,
                                                              fill=freg(e, 0.0), base=128 * d_, channel_multiplier=1))
            for (A_, lo, hi) in ((Asel, 1, 3), (Asel2, 0, 2)):
                P.op("pool", lambda e, A_=A_: e.memset(A_, 1.0))
                P.op("pool", lambda e, A_=A_, lo=lo: e.affine_select(out=A_, in_=A_, pattern=[[-4, 32]], compare_op=ALU.is_ge,
                                                                     fill=freg(e, 0.0), base=lo, channel_multiplier=1))
                P.op("pool", lambda e, A_=A_, hi=hi: e.affine_select(out=A_, in_=A_, pattern=[[4, 32]], compare_op=ALU.is_ge,
                                                                     fill=freg(e, 0.0), base=hi, channel_multiplier=-1))
            P.op("pool", lambda e: e.tensor_tensor(out=Asel, in0=Asel, in1=Asel2, op=ALU.add))
            P.op("pool", lambda e: e.memset(Eexp[0:32], 1.0))
            t_cst = P.op("pool", lambda e: e.affine_select(
                out=Eexp[0:32], in_=Eexp[0:32], pattern=[[-2, 16], [-1, 2], [0, 64]], compare_op=ALU.is_equal, fill=freg(e, 0.0),
                base=0, channel_multiplier=1), sig=True)
            t_gk = ring_in.issue("sp", lambda e: e.dma_start(out=gk_col[:], in_=hv_d[0].rearrange("(h o) -> h o", o=1)))

            def rowsT(src_fn, nblk, dst, waits=()):
                last = None
                rf = state.setdefault("rows_free", [None, None])
                for tb in range(nblk):
                    sl = tb % 2
                    tk = ring_in.issue("sp", lambda e, tb=tb, sl=sl: e.dma_start(out=rowst[:, sl, :], in_=src_fn(tb)),
                                       waits=[rf[sl]] + list(waits))
                    b = misc_bank()
                    tp = None
                    for g in range(4):
                        tp = P.op("pe", lambda e, b=b, g=g, sl=sl: e.transpose(
                            out=ps[b][:, g * 128:(g + 1) * 128], in_=rowst[:, sl, g * 128:(g + 1) * 128], identity=ident[:]),
                            waits=[tk, ps_free[b]], sig=(g == 3))
                    rf[sl] = tp
                    last = P.op("act", lambda e, b=b, tb=tb: e.activation(
                        out=dst[:, :, tb * 128:(tb + 1) * 128], in_=ps[b][:, :].rearrange("p (g t) -> p g t", g=4),
                        func=AF.Copy), waits=[tp], sig=True)
                    ps_free[b] = last
                return last

            t_rk = rowsT(lambda tb: kv_o[0][tb * 128:(tb + 1) * 128, :], 16, kselT)
            t_rv = rowsT(lambda tb: kv_o[1][tb * 128:(tb + 1) * 128, :], 16, kwinT)
            raws = [kselT, kwinT]
            t_cmp_done = None
            for kvi in range(2):
                tw1 = ring_pl.issue("pool", lambda e, kvi=kvi: e.dma_start(
                    out=w1_sb, in_=wc1_d[kvi].rearrange("(r p) h -> p r h", p=128)), waits=[t_cmp_done, t_cst])
                tw2 = ring_pl.issue("pool", lambda e, kvi=kvi: e.dma_start(
                    out=w2_sb, in_=wc2_d[kvi].rearrange("(c p) h -> p c h", p=128)), waits=[t_cmp_done])
                tk = ring_in.issue("sp", lambda e, kvi=kvi: e.dma_start(out=rowst[0:32, 0, 0:128], in_=pe_d[kvi]),
                                   waits=[state["rows_free"][0], t_cmp_done])
                b = misc_bank()
                tp = P.op("pe", lambda e, b=b: e.transpose(out=ps[b][:, 0:32], in_=rowst[0:32, 0, 0:128],
                                                            identity=ident[0:32, 0:32]), waits=[tk, ps_free[b]], sig=True)
                state["rows_free"][0] = tp
                tpe = P.op("act", lambda e, b=b: e.activation(out=peT, in_=ps[b][:, 0:32], func=AF.Copy), waits=[tp], sig=True)
                ps_free[b] = tpe
                tb_ = None
                for hc in range(2):
                    for r in range(32):
                        tb_ = P.op("pe", lambda e, hc=hc, r=r: e.matmul(
                            ps[6][:, hc:hc + 1], lhsT=w1_sb[:, r, hc * 128:(hc + 1) * 128], rhs=peT[:, r:r + 1],
                            start=(r == 0), stop=(r == 31)), waits=[tw1, tpe, ps_free[6]], sig=(r == 31))
                tbias = P.op("dve", lambda e, kvi=kvi: e.tensor_copy(out=bias_sb[:, 2 * kvi:2 * kvi + 2], in_=ps[6][:, 0:2]),
                             waits=[tb_], sig=True)
                ps_free[6] = tbias
                raw = raws[kvi]
                tsil = None
                for hc in range(2):
                    b = hc
                    tm = None
                    for r in range(32):
                        tm = P.op("pe", lambda e, b=b, hc=hc, r=r, raw=raw: e.matmul(
                            ps[b][:, 0:508].rearrange("p (g n) -> p g n", g=4), lhsT=w1_sb[:, r, hc * 128:(hc + 1) * 128],
                            rhs=raw[:, :, r:r + 16 * 126 + 1:16], start=(r == 0), stop=(r == 31)),
                            waits=[tw1, t_rk, t_rv, ps_free[b]], sig=(r == 31))
                    tsil = P.op("act", lambda e, b=b, hc=hc, kvi=kvi: e.activation(
                        out=sT[:, hc, 0:508], in_=ps[b][:, 0:508], func=AF.Silu,
                        bias=bias_sb[:, 2 * kvi + hc:2 * kvi + hc + 1]), waits=[tm, tbias], sig=True)
                    ps_free[b] = tsil
                if kvi == 0:
                    tm = None
                    for hc in range(2):
                        tm = P.op("pe", lambda e, hc=hc: e.matmul(ps[2][:, 0:508], lhsT=w2_sb[:, hc, :], rhs=sT[:, hc, 0:508],
                                                                   start=(hc == 0), stop=(hc == 1)),
                                  waits=[tsil, tw2, ps_free[2]], sig=(hc == 1))
                    tsq = P.op("act", lambda e: e.activation(out=sqb[:, 0:508], in_=ps[2][:, 0:508], func=AF.Square),
                               waits=[tm], sig=True)
                    tss = P.op("pe", lambda e: e.matmul(ps[3][:, 0:508], lhsT=ones_bf[:], rhs=sqb[:, 0:508], start=True, stop=True),
                               waits=[tsq, ps_free[3]], sig=True)
                    t1 = P.op("dve", lambda e: e.tensor_scalar(out=rec_sb[:, 0:508], in0=ps[3][:, 0:508], scalar1=1.0 / HD,
                                                               scalar2=EPS, op0=ALU.mult, op1=ALU.add), waits=[tss], sig=True)
                    ps_free[3] = t1
                    t2 = P.op("pool", lambda e: e.tensor_tensor(out=rec_sb[:, 0:508], in0=rec_sb[:, 0:508],
                                                                in1=mhalf[:, 0:1].to_broadcast([128, 508]), op=ALU.pow),
                              waits=[t1], sig=True)
                    t3_ = P.op("dve", lambda e: e.scalar_tensor_tensor(out=ckT[:, 0:508], in0=ps[2][:, 0:508], scalar=gk_col[:, 0:1],
                                                                        in1=rec_sb[:, 0:508], op0=ALU.mult, op1=ALU.mult),
                               waits=[t2, t_gk], sig=True)
                    ps_free[2] = t3_
                    t_cmp_done = t3_
                else:
                    tcv = None
                    for g in range(4):
                        tm = None
                        for hc in range(2):
                            tm = P.op("pe", lambda e, g=g, hc=hc: e.matmul(
                                ps[2][0:127, g * 128:(g + 1) * 128], lhsT=sT[:, hc, g * 127:(g + 1) * 127], rhs=w2_sb[:, hc, :],
                                start=(hc == 0), stop=(hc == 1)), waits=[tsil, tw2, ps_free[2]], sig=(hc == 1))
                    tcv = P.op("act", lambda e: e.activation(out=cv_sb[0:127], in_=ps[2][0:127, :].rearrange("p (g h) -> p g h", g=4),
                                                             func=AF.Copy), waits=[tm], sig=True)
                    ps_free[2] = tcv
                    t_cmp_done = tcv

            t_ks = rowsT(lambda tb: kv_o[2][tb * 128:(tb + 1) * 128, :], 16, kselT, waits=[t_cmp_done])
            t_kw = rowsT(lambda tb: kw_s[0][tb * 128:(tb + 1) * 128, :], 16, kwinT, waits=[t_cmp_done])
            wg_sb = hx[:].bitcast(BF16).rearrange("p (k c) -> p k c", k=KC)
            t_wg = ring_pl.issue("pool", lambda e: e.dma_start(out=wg_sb[:], in_=wqg_v[:, :, 4096:4192]))

            fin = dict(acc_free=None, e_free=[None, None], pt_free=[None, None], step=0, last=None)

            def block_step(qv, kT_ap, nk, v_ap, mask_ap, first, last, extra_waits=()):
                i = fin["step"] % 2
                fin["step"] += 1
                tS = P.op("pe", lambda e: e.matmul(ps[i][0:nk, :], lhsT=kT_ap, rhs=qv, start=True, stop=True),
                          waits=[ps_free[i]] + list(extra_waits), sig=True)
                tE = P.op("act", lambda e: e.activation(out=e_sb[i][0:nk], in_=ps[i][0:nk, :], func=AF.Exp, scale=SCALE),
                          waits=[tS, fin["e_free"][i]], sig=True)
                ps_free[i] = tE
                if mask_ap is not None:
                    eng = "pool" if (fin["step"] % 4) < 2 else "dve"
                    tM = P.op(eng, lambda e: e.tensor_tensor(out=pt_sb[i][0:nk], in0=e_sb[i][0:nk], in1=mask_ap, op=ALU.mult),
                              waits=[tE, fin["pt_free"][i]], sig=True)
                    fin["e_free"][i] = tM
                    src = pt_sb[i]
                else:
                    tM = tE
                    src = e_sb[i]
                P.op("pe", lambda e: e.matmul(ps[2][:, :], lhsT=v_ap, rhs=src[0:nk], start=first, stop=last),
                     waits=[tM] + ([ps_free[2]] if first else []))
                tD = P.op("pe", lambda e: e.matmul(ps[3][:, :], lhsT=ones_bf[0:nk, :], rhs=src[0:nk], start=first, stop=last),
                          waits=([ps_free[3]] if first else []), sig=True)
                if mask_ap is not None:
                    fin["pt_free"][i] = tD
                else:
                    fin["e_free"][i] = tD
                return tD, src, i

            def finish(tD, r, idx, first_branch, t_gT, cmp_extra=None):
                tG = P.op("pe", lambda e: e.matmul(ps[6][:, :], lhsT=identb[0:96, idx:idx + 1].to_broadcast([96, 128]),
                                                   rhs=gatesT[0:96, :], start=True, stop=True), waits=[ps_free[6], t_gT], sig=True)
                t1 = P.op("dve", lambda e: e.tensor_scalar(out=rec_sb[:, :], in0=ps[3][:, :], scalar1=1e-30, scalar2=None,
                                                           op0=ALU.add), waits=[tD], sig=True)
                ps_free[3] = t1
                t2 = P.op("dve", lambda e: e.reciprocal(out=rec_sb[:, :], in_=rec_sb[:, :]), sig=True)
                if cmp_extra is not None:
                    cmp_extra(t2)
                t3_ = P.op("dve", lambda e: e.tensor_tensor(out=coef_sb[:, :], in0=rec_sb[:, :], in1=ps[6][:, :], op=ALU.mult),
                           waits=[tG], sig=True)
                ps_free[6] = t3_
                if first_branch:
                    t4 = P.op("dve", lambda e: e.tensor_tensor(out=acc[:, r, :], in0=ps[2][:, :], in1=coef_sb[:, :], op=ALU.mult),
                              waits=[fin["acc_free"]], sig=True)
                    ps_free[2] = t4
                else:
                    t4a = P.op("dve", lambda e: e.tensor_tensor(out=coef_sb[:, :], in0=ps[2][:, :], in1=coef_sb[:, :], op=ALU.mult),
                               sig=True)
                    ps_free[2] = t4a
                    t4 = P.op("dve", lambda e: e.tensor_tensor(out=acc[:, r, :], in0=acc[:, r, :], in1=coef_sb[:, :], op=ALU.add),
                              sig=True)
                fin["last"] = t4
                return t4

            for qt in range(NT):
                ta = ring_in.issue("sp", lambda e, qt=qt: e.dma_start(
                    out=a[:, :, :], in_=a1_s[qt].rearrange("p (k t) -> p k t", k=KC)), waits=[state.get("a_free"), fin["last"]])
                t_gT = None
                for tb in range(4):
                    b = misc_bank()
                    tp = None
                    for kc in range(KC):
                        tp = P.op("pe", lambda e, b=b, kc=kc, tb=tb: e.matmul(
                            ps[b][:, 0:96], lhsT=a[:, kc, tb * 128:(tb + 1) * 128], rhs=wg_sb[:, kc, :],
                            start=(kc == 0), stop=(kc == KC - 1)), waits=[ta, t_wg, ps_free[b]], sig=(kc == KC - 1))
                    tsg = P.op("act", lambda e, b=b: e.activation(out=stg[0][:, 0:96], in_=ps[b][:, 0:96], func=AF.Sigmoid),
                               waits=[tp, state.get("stg0_free")], sig=True)
                    ps_free[b] = tsg
                    b2 = misc_bank()
                    tp2 = P.op("pe", lambda e, b2=b2: e.transpose(out=ps[b2][0:96, 0:128], in_=stg[0][:, 0:96], identity=ident[:]),
                               waits=[tsg, ps_free[b2]], sig=True)
                    state["stg0_free"] = tp2
                    t_gT = P.op("act", lambda e, b2=b2, tb=tb: e.activation(out=gatesT[0:96, tb * 128:(tb + 1) * 128],
                                                                             in_=ps[b2][0:96, 0:128], func=AF.Copy),
                                waits=[tp2, fin["last"]], sig=True)
                    ps_free[b2] = t_gT
                P.op("pool", lambda e: e.memset(cmask, 1.0), waits=[fin["last"]])
                t_cm = P.op("pool", lambda e, qt=qt: e.affine_select(out=cmask, in_=cmask, pattern=[[1, 512]], compare_op=ALU.is_ge,
                                                                     fill=freg(e, 0.0), base=512 * qt - 31, channel_multiplier=-16), sig=True)
                for g in range(4):
                    tvs = ring_pl.issue("pool", lambda e, g=g: e.dma_start(
                        out=vsel_g[:], in_=kv_o[3].rearrange("(k p) c -> p k c", p=128)[:, :, g * 128:(g + 1) * 128]),
                        waits=[fin["last"], P.lastsig.get("pe")])
                    tvw = ring_pl.issue("pool", lambda e, g=g: e.dma_start(
                        out=vwin_g[:], in_=kw_s[1].rearrange("(k p) c -> p k c", p=128)[:, :, g * 128:(g + 1) * 128]),
                        waits=[fin["last"], P.lastsig.get("pe")])
                    for cbk in range(2):
                        c0 = g * 1024 + cbk * 512
                        halves = [wload(wqg_v[:, hf_ * 16:(hf_ + 1) * 16, c0:c0 + 512], 16, 512, [ta]) for hf_ in range(2)]
                        tps = []
                        for hf_ in range(2):
                            slot, wv, tk = halves[hf_]
                            tp = None
                            for tb in range(4):
                                bq = 4 + (tb % 2) if False else tb
                                for k in range(16):
                                    kk = hf_ * 16 + k
                                    tp = P.op("pe", lambda e, tb=tb, wv=wv, k=k, kk=kk: e.matmul(
                                        ps[tb][:, 0:512], lhsT=a[:, kk, tb * 128:(tb + 1) * 128], rhs=wv[:, k, :],
                                        start=(kk == 0), stop=(kk == KC - 1)), waits=[tk, ta, ps_free[tb]], sig=(k == 15))
                                if hf_ == 1:
                                    tps.append(tp)
                            wb_free[slot] = tp
                        for tb in range(4):
                            qn = stg[0]
                            qr = stg[1]
                            tq1 = P.op("act", lambda e, tb=tb: e.activation(out=junk[:, 0:512], in_=ps[tb][:, 0:512], func=AF.Square),
                                       waits=[tps[tb], fin["last"]], sig=True)
                            P.op("dve", lambda e: e.tensor_reduce(out=ss4[:, 0:4], in_=junk[:, 0:512].rearrange("p (g h) -> p g h", g=4),
                                                                  axis=mybir.AxisListType.X, op=ALU.add), waits=[tq1])
                            tq3 = P.op("dve", lambda e: e.tensor_scalar(out=ss4[:, 0:4], in0=ss4[:, 0:4], scalar1=1.0 / HD, scalar2=EPS,
                                                                        op0=ALU.mult, op1=ALU.add), sig=True)
                            tss = P.op("act", lambda e: e.activation(out=ss4[:, 4:8], in_=ss4[:, 0:4], func=AF.Sqrt), waits=[tq3], sig=True)
                            P.op("dve", lambda e: e.reciprocal(out=ss4[:, 4:8], in_=ss4[:, 4:8]), waits=[tss, state.get("stg0_free"),
                                                                                                         state.get("stg1_free")])
                            for hh in range(4):
                                P.op("dve", lambda e, tb=tb, hh=hh: e.scalar_tensor_tensor(
                                    out=qn[:, hh * 128:(hh + 1) * 128], in0=ps[tb][:, hh * 128:(hh + 1) * 128],
                                    scalar=ss4[:, 4 + hh:5 + hh], in1=hvb[:, 3, :], op0=ALU.mult, op1=ALU.mult))
                            tqn = P.op("dve", lambda e: e.tensor_copy(out=tiny[:, 3:4], in_=tiny[:, 3:4]), sig=True)
                            ps_free[tb] = tqn
                            r0 = qt * TW + tb * 128
                            tcs = ring_in.issue("sp", lambda e, r0=r0: e.dma_start(out=cs_t[:], in_=cs_d[r0:r0 + 128]),
                                                waits=[state.get("cs_free")])
                            qnv = qn[:, :].rearrange("p (g h) -> p g h", g=4)
                            qrv = qr[:, :].rearrange("p (g h) -> p g h", g=4)
                            cosb = cs_t[:, 0:1, :].to_broadcast([128, 4, 64])
                            sinb = cs_t[:, 1:2, :].to_broadcast([128, 4, 64])
                            P.op("dve", lambda e, qnv=qnv, cosb=cosb: e.tensor_tensor(out=rp[0][:], in0=qnv[:, :, 0:64], in1=cosb, op=ALU.mult),
                                 waits=[tcs])
                            P.op("dve", lambda e, qnv=qnv, sinb=sinb: e.tensor_tensor(out=rp[1][:], in0=qnv[:, :, 64:128], in1=sinb, op=ALU.mult))
                            P.op("dve", lambda e, qnv=qnv, cosb=cosb: e.tensor_tensor(out=rp[2][:], in0=qnv[:, :, 64:128], in1=cosb, op=ALU.mult))
                            P.op("dve", lambda e, qnv=qnv, sinb=sinb: e.tensor_tensor(out=rp[3][:], in0=qnv[:, :, 0:64], in1=sinb, op=ALU.mult))
                            P.op("dve", lambda e, qrv=qrv: e.tensor_tensor(out=qrv[:, :, 0:64], in0=rp[0][:], in1=rp[1][:], op=ALU.subtract))
                            tqr = P.op("dve", lambda e, qrv=qrv: e.tensor_tensor(out=qrv[:, :, 64:128], in0=rp[2][:], in1=rp[3][:], op=ALU.add),
                                       sig=True)
                            state["cs_free"] = tqr
                            for (src_, dstT, key) in ((qn, qT, "stg0_free"), (qr, qrT, "stg1_free")):
                                b = misc_bank()
                                tp = None
                                for hh in range(4):
                                    tp = P.op("pe", lambda e, b=b, hh=hh, src_=src_: e.transpose(
                                        out=ps[b][:, hh * 128:(hh + 1) * 128], in_=src_[:, hh * 128:(hh + 1) * 128], identity=ident[:]),
                                        waits=[tqr, ps_free[b]], sig=(hh == 3))
                                state[key] = tp
                                tev = P.op("act", lambda e, b=b, dstT=dstT, cbk=cbk, tb=tb: e.activation(
                                    out=dstT[:, cbk * 4:cbk * 4 + 4, tb * 128:(tb + 1) * 128],
                                    in_=ps[b][:, :].rearrange("p (r t) -> p r t", r=4), func=AF.Copy),
                                    waits=[tp, fin["last"]], sig=True)
                                ps_free[b] = tev
                                state["q_ready"] = tev
                    tq = state["q_ready"]
                    for r in range(8):
                        tD, src, i = block_step(qT[:, r, :], ckT[:, g * 127:(g + 1) * 127], 127, cv_sb[0:127, g, :], cmask[0:127, :],
                                                True, True, extra_waits=[tq, t_cm, t_cmp_done, t_ks, t_kw])

                        def cmp_extra(t2, r=r, src=src, i=i):
                            tpn = P.op("dve", lambda e: e.tensor_tensor(out=pn_sb[0:127, :], in0=src[0:127, :], in1=rec_sb[0:127, :],
                                                                         op=ALU.mult), waits=[fin.get("pn_free")], sig=True)
                            tps_ = P.op("pe", lambda e: e.matmul(ps[7][0:32, :], lhsT=Asel[0:127, 0:32], rhs=pn_sb[0:127, :],
                                                                 start=(r == 0), stop=(r == 7)),
                                        waits=[tpn, t_cst] + ([ps_free[7]] if r == 0 else []), sig=True)
                            fin["pn_free"] = tps_
                            fin["pt_free"][i] = tps_
                            fin["slc_done"] = tps_
                        finish(tD, r, g * 24 + r * 3 + 0, True, t_gT, cmp_extra)
                    tcp = P.op("act", lambda e: e.activation(out=pslc_sb[0:32, :], in_=ps[7][0:32, :], func=AF.Copy),
                               waits=[fin["slc_done"]], sig=True)
                    ps_free[7] = tcp
                    tsel = None
                    for tb in range(4):
                        pos0 = qt * TW + tb * 128
                        c0b = pos0 // 64
                        b = misc_bank()
                        tp = P.op("pe", lambda e, b=b, tb=tb: e.transpose(out=ps[b][:, 0:32], in_=pslc_sb[0:32, tb * 128:(tb + 1) * 128],
                                                                            identity=ident[0:32, 0:32]), waits=[tcp, ps_free[b]], sig=True)
                        sc = stg[0][:, 0:32]
                        sc2 = stg[0][:, 32:64]
                        s1 = stg[0][:, 64:96]
                        selq = stg[0][:, 96:128]
                        tsc = P.op("act", lambda e, b=b, sc=sc: e.activation(out=sc, in_=ps[b][:, 0:32], func=AF.Copy),
                                   waits=[tp, state.get("stg0_free")], sig=True)
                        ps_free[b] = tsc
                        P.op("pool", lambda e, sc=sc, pos0=pos0: e.affine_select(out=sc, in_=sc, pattern=[[-64, 32]], compare_op=ALU.is_ge,
                                                                                 fill=freg(e, NEGF), base=pos0, channel_multiplier=1), waits=[tsc])
                        P.op("pool", lambda e, sc=sc, c0b=c0b: e.memset(sc[0:64, c0b:c0b + 1], 2e9))
                        if c0b >= 1:
                            P.op("pool", lambda e, sc=sc, c0b=c0b: e.memset(sc[0:64, c0b - 1:c0b], 1e9))
                        P.op("pool", lambda e, sc=sc, c0b=c0b: e.memset(sc[64:128, c0b + 1:c0b + 2], 2e9))
                        P.op("pool", lambda e, sc=sc, c0b=c0b: e.memset(sc[64:128, c0b:c0b + 1], 1e9))
                        tfz = P.op("pool", lambda e, sc=sc: e.memset(sc[:, 0:1], 3e9), sig=True)
                        P.op("dve", lambda e, sc=sc: e.max(out=m8a[:], in_=sc), waits=[tfz])
                        P.op("dve", lambda e, sc=sc, sc2=sc2: e.match_replace(out=sc2, in_to_replace=m8a[:], in_values=sc, imm_value=-3e38))
                        P.op("dve", lambda e, sc2=sc2: e.max(out=m8b[:], in_=sc2))
                        P.op("dve", lambda e, sc=sc, s1=s1: e.tensor_scalar(out=s1, in0=sc, scalar1=m8b[:, 7:8], scalar2=None, op0=ALU.is_ge))
                        tsq_ = P.op("dve", lambda e, sc=sc, s1=s1, selq=selq: e.scalar_tensor_tensor(
                            out=selq, in0=sc, scalar=-5e29, in1=s1, op0=ALU.is_gt, op1=ALU.mult), sig=True)
                        b2 = misc_bank()
                        tp2 = P.op("pe", lambda e, b2=b2, selq=selq: e.transpose(out=ps[b2][0:32, 0:128], in_=selq, identity=ident[:]),
                                   waits=[tsq_, ps_free[b2]], sig=True)
                        state["stg0_free"] = tp2
                        tsel = P.op("act", lambda e, b2=b2, tb=tb: e.activation(out=selT[0:32, tb * 128:(tb + 1) * 128],
                                                                                 in_=ps[b2][0:32, 0:128], func=AF.Copy),
                                    waits=[tp2, fin.get("selT_free")], sig=True)
                        ps_free[b2] = tsel
                    nkb = 4 * qt + 4
                    tmk = None
                    for kb in range(nkb):
                        tx = P.op("pe", lambda e, kb=kb: e.matmul(ps[6][:, :], lhsT=Eexp[0:32, kb, :], rhs=selT[0:32, :],
                                                                  start=True, stop=True), waits=[tsel, ps_free[6], t_cst], sig=True)
                        if kb >= 4 * qt:
                            tmk = P.op("dve", lambda e, kb=kb, qt=qt: e.tensor_tensor(out=maskbuf[:, kb, :], in0=ps[6][:, :],
                                                                                      in1=mc[kb - 4 * qt], op=ALU.mult),
                                       waits=[tx, fin["last"]], sig=True)
                        else:
                            tmk = P.op("act", lambda e, kb=kb: e.activation(out=maskbuf[:, kb, :], in_=ps[6][:, :], func=AF.Copy),
                                       waits=[tx, fin["last"]], sig=True)
                        ps_free[6] = tmk
                        fin["mk_last"] = tmk
                    fin["selT_free"] = tx
                    tmk_all = [P.lastsig.get("dve"), P.lastsig.get("act")]
                    for r in range(8):
                        tD = None
                        for kb in range(nkb):
                            tD, _, _ = block_step(qrT[:, r, :], kselT[:, g, kb * 128:(kb + 1) * 128], 128, vsel_g[:, kb, :],
                                                  maskbuf[:, kb, :], kb == 0, kb == nkb - 1, extra_waits=tmk_all + [tvs, tq])
                        finish(tD, r, g * 24 + r * 3 + 1, False, t_gT)
                    kb0 = max(0, 4 * qt - 4)
                    for r in range(8):
                        tD = None
                        for kb in range(kb0, nkb):
                            m_ap = mc[kb - 4 * qt] if kb >= 4 * qt else mw[kb - 4 * qt + 4]
                            tD, _, _ = block_step(qrT[:, r, :], kwinT[:, g, kb * 128:(kb + 1) * 128], 128, vwin_g[:, kb, :],
                                                  m_ap, kb == kb0, kb == nkb - 1, extra_waits=[tvw, tq])
                        finish(tD, r, g * 24 + r * 3 + 2, False, t_gT)
                    to = ring_pl.issue("pool", lambda e, g=g, qt=qt: e.dma_start(
                        out=oT_s[qt].rearrange("p (k t) -> p k t", k=KC)[:, 8 * g:8 * g + 8, :], in_=acc), waits=[fin["last"]])
                    fin["acc_free"] = to
                state["a_free"] = P.lastsig.get("pe")


            P.strict = set()
            barrier()
            P.strict = {"act", "dve", "pool"}
            pt_d = din_once("ptab", [NSEQ, NSMP, 128], I32)[sq]
            cache_d = [din_once("cache%d" % n, [1280 * 128, 512]) for n in range(4)]
            zeros_bf = sb_once("zeros_bf", [128, 128], BF16)
            acc_s = sb_once("acc_s", [128, NSMP, 32], F32)
            vsb = vsel_g[:].rearrange("p a b -> p (a b)")
            vwb = vwin_g[:].rearrange("p a b -> p (a b)")
            maskT = vsb[:, 0:516].rearrange("p (t g) -> p t g", g=4)
            PTall = vsb[:, 516:772].rearrange("p (t c) -> p t c", t=8)
            pn_all = vsb[:, 772:1028].rearrange("p (t c) -> p t c", t=8)
            pcol_bf = vsb[:, 1028:1060].rearrange("p (t g) -> p t g", t=8)
            A128 = vsb[:, 1060:1092]
            A128b = vsb[:, 1092:1124]
            Aprev = vsb[:, 1124:1156]
            qT_s = vsb[:, 1156:1220].rearrange("p (b c) -> p b c", b=NSMP)
            qrT_s = vsb[:, 1220:1284].rearrange("p (b c) -> p b c", b=NSMP)
            gs2 = vsb[:, 1284:1380]
            sel_bf = vsb[:, 1380:1640]
            es_sb = [vsb[:, 1640:1672], vsb[:, 1672:1704]]
            pts_sb = [vsb[:, 1704:1736], vsb[:, 1736:1768]]
            vwf = vwb.bitcast(F32)
            idx_all = vwb.bitcast(I32)[:, 0:256].rearrange("p (b g) -> p b g", b=NSMP)
            Grep = vwf[:, 256:448].rearrange("p (b c) -> p b c", b=NSMP)
            rec_s = vwf[:, 448:480]
            coef_s = vwf[:, 480:512]
            pcol_f = vwf[:, 512:544].rearrange("p (t g) -> p t g", t=8)
            sc_s = vwf[:, 544:808]
            sc2_s = stg[1][:, 0:264]
            pcolp = tiny[:, 4:5]
            Vp = [scBb[:, 4096:4608], scBb[:, 4608:5120]]
            kTp = [scBb[:, 5120:5632].rearrange("p (g t) -> p g t", g=4), scBb[:, 5632:6144].rearrange("p (g t) -> p g t", g=4)]
            ckT_s = hbv[:, 26624:30720].rearrange("p (g t) -> p g t", g=4)
            cv_s = scAb[:, 0:4096].rearrange("p (t g h) -> p t g h", t=8, g=4)
            rawk = hbv[:, 0:4608].rearrange("p (g t) -> p g t", g=4)
            rawv = hbv[:, 8192:12800].rearrange("p (g t) -> p g t", g=4)

            P.op("pool", lambda e: e.memset(zeros_bf[:], 0.0))
            for (A_, lo, hi) in ((A128, 1, 3), (A128b, 0, 2)):
                P.op("pool", lambda e, A_=A_: e.memset(A_, 1.0))
                P.op("pool", lambda e, A_=A_, lo=lo: e.affine_select(out=A_, in_=A_, pattern=[[-4, 32]], compare_op=ALU.is_ge,
                                                                     fill=freg(e, 0.0), base=lo, channel_multiplier=1))
                P.op("pool", lambda e, A_=A_, hi=hi: e.affine_select(out=A_, in_=A_, pattern=[[4, 32]], compare_op=ALU.is_ge,
                                                                     fill=freg(e, 0.0), base=hi, channel_multiplier=-1))
            P.op("pool", lambda e: e.tensor_tensor(out=A128, in0=A128, in1=A128b, op=ALU.add))
            P.op("pool", lambda e: e.memset(Aprev, 1.0))
            P.op("pool", lambda e: e.affine_select(out=Aprev, in_=Aprev, pattern=[[128, 32]], compare_op=ALU.is_equal,
                                                   fill=freg(e, 0.0), base=-127, channel_multiplier=1))
            P.op("pool", lambda e: e.iota(out=pcolp, pattern=[[0, 1]], base=0, channel_multiplier=1,
                                          allow_small_or_imprecise_dtypes=True))
            t_sc = P.op("pool", lambda e: e.memset(tiny[:, 5:6], 0.0), sig=True)
            t_pt = ring_in.issue("sp", lambda e: e.dma_start(out=idx_all.rearrange("p b g -> p (b g)"),
                                                             in_=pt_d.rearrange("b g -> (b g)").partition_broadcast(128)))
            t_idx = P.op("dve", lambda e: e.tensor_scalar(out=idx_all, in0=idx_all, scalar1=128.0, scalar2=pcolp,
                                                          op0=ALU.mult, op1=ALU.add), waits=[t_pt, t_sc], sig=True)

            def page_rows(n, b, pg, dst, waits):
                return ring_pl.issue("pool", lambda e: e.indirect_dma_start(
                    out=dst, out_offset=None, in_=cache_d[n][:, :],
                    in_offset=bass.IndirectOffsetOnAxis(ap=idx_all[:, b, pg:pg + 1], axis=0)), waits=[t_idx] + list(waits))

            ta = ring_in.issue("sp", lambda e: e.dma_start(
                out=a[:, :, 0:NSMP], in_=a1_s[NT].rearrange("p (k t) -> p k t", k=KC)[:, :, 0:NSMP]),
                waits=[P.lastsig.get("pe")])
            tcs = ring_in.issue("sp", lambda e: e.dma_start(
                out=cs_t[0:NSMP].rearrange("p a b -> p (a b)"),
                in_=cs_d[T].rearrange("a b -> (a b)").partition_broadcast(NSMP)), waits=[P.lastsig.get("dve")])
            for cbk in range(8):
                c0 = cbk * 512
                halves = [wload(wqg_v[:, hf_ * 16:(hf_ + 1) * 16, c0:c0 + 512], 16, 512, [ta]) for hf_ in range(2)]
                bq = cbk % 2
                tp = None
                for hf_ in range(2):
                    slot, wv, tk = halves[hf_]
                    for k in range(16):
                        kk = hf_ * 16 + k
                        tp = P.op("pe", lambda e, bq=bq, wv=wv, k=k, kk=kk: e.matmul(
                            ps[bq][0:NSMP, 0:512], lhsT=a[:, kk, 0:NSMP], rhs=wv[:, k, :],
                            start=(kk == 0), stop=(kk == KC - 1)), waits=[tk, ta, ps_free[bq]], sig=(k == 15))
                    wb_free[slot] = tp
                qn = stg[0]
                qr = stg[1]
                tq1 = P.op("act", lambda e, bq=bq: e.activation(out=junk[0:NSMP, 0:512], in_=ps[bq][0:NSMP, 0:512], func=AF.Square),
                           waits=[tp], sig=True)
                P.op("dve", lambda e: e.tensor_reduce(out=ss4[0:NSMP, 0:4], in_=junk[0:NSMP, 0:512].rearrange("p (g h) -> p g h", g=4),
                                                      axis=mybir.AxisListType.X, op=ALU.add), waits=[tq1])
                tq3 = P.op("dve", lambda e: e.tensor_scalar(out=ss4[0:NSMP, 0:4], in0=ss4[0:NSMP, 0:4], scalar1=1.0 / HD, scalar2=EPS,
                                                            op0=ALU.mult, op1=ALU.add), sig=True)
                tss = P.op("act", lambda e: e.activation(out=ss4[0:NSMP, 4:8], in_=ss4[0:NSMP, 0:4], func=AF.Sqrt), waits=[tq3], sig=True)
                P.op("dve", lambda e: e.reciprocal(out=ss4[0:NSMP, 4:8], in_=ss4[0:NSMP, 4:8]),
                     waits=[tss, state.get("stg0_free"), state.get("stg1_free")])
                for hh in range(4):
                    P.op("dve", lambda e, bq=bq, hh=hh: e.scalar_tensor_tensor(
                        out=qn[0:NSMP, hh * 128:(hh + 1) * 128], in0=ps[bq][0:NSMP, hh * 128:(hh + 1) * 128],
                        scalar=ss4[0:NSMP, 4 + hh:5 + hh], in1=hvb[0:NSMP, 3, :], op0=ALU.mult, op1=ALU.mult))
                tqn = P.op("dve", lambda e: e.tensor_copy(out=tiny[:, 3:4], in_=tiny[:, 3:4]), sig=True)
                ps_free[bq] = tqn
                qnv = qn[0:NSMP, :].rearrange("p (g h) -> p g h", g=4)
                qrv = qr[0:NSMP, :].rearrange("p (g h) -> p g h", g=4)
                cosb = cs_t[0:NSMP, 0:1, :].to_broadcast([NSMP, 4, 64])
                sinb = cs_t[0:NSMP, 1:2, :].to_broadcast([NSMP, 4, 64])
                rr = [rp[i_][0:NSMP] for i_ in range(4)]
                P.op("dve", lambda e, qnv=qnv, cosb=cosb, rr=rr: e.tensor_tensor(out=rr[0], in0=qnv[:, :, 0:64], in1=cosb, op=ALU.mult), waits=[tcs])
                P.op("dve", lambda e, qnv=qnv, sinb=sinb, rr=rr: e.tensor_tensor(out=rr[1], in0=qnv[:, :, 64:128], in1=sinb, op=ALU.mult))
                P.op("dve", lambda e, qnv=qnv, cosb=cosb, rr=rr: e.tensor_tensor(out=rr[2], in0=qnv[:, :, 64:128], in1=cosb, op=ALU.mult))
                P.op("dve", lambda e, qnv=qnv, sinb=sinb, rr=rr: e.tensor_tensor(out=rr[3], in0=qnv[:, :, 0:64], in1=sinb, op=ALU.mult))
                P.op("dve", lambda e, qrv=qrv, rr=rr: e.tensor_tensor(out=qrv[:, :, 0:64], in0=rr[0], in1=rr[1], op=ALU.subtract))
                tqr = P.op("dve", lambda e, qrv=qrv, rr=rr: e.tensor_tensor(out=qrv[:, :, 64:128], in0=rr[2], in1=rr[3], op=ALU.add), sig=True)
                for (src_, dstT, key) in ((qn, qT_s, "stg0_free"), (qr, qrT_s, "stg1_free")):
                    b_ = misc_bank()
                    tpx = None
                    for hh in range(4):
                        tpx = P.op("pe", lambda e, b_=b_, hh=hh, src_=src_: e.transpose(
                            out=ps[b_][:, hh * NSMP:(hh + 1) * NSMP], in_=src_[0:NSMP, hh * 128:(hh + 1) * 128],
                            identity=ident[0:NSMP, 0:NSMP]), waits=[tqr, ps_free[b_]], sig=(hh == 3))
                    state[key] = tpx
                    tev = P.op("act", lambda e, b_=b_, dstT=dstT, cbk=cbk: e.activation(
                        out=dstT[:, :, cbk * 4:cbk * 4 + 4], in_=ps[b_][:, 0:4 * NSMP].rearrange("p (h b) -> p b h", b=NSMP),
                        func=AF.Copy), waits=[tpx], sig=True)
                    ps_free[b_] = tev
            b_ = misc_bank()
            tp = None
            for kc in range(KC):
                tp = P.op("pe", lambda e, b_=b_, kc=kc: e.matmul(ps[b_][0:NSMP, 0:96], lhsT=a[:, kc, 0:NSMP], rhs=wg_sb[:, kc, :],
                                                                 start=(kc == 0), stop=(kc == KC - 1)),
                          waits=[ta, t_wg, ps_free[b_]], sig=(kc == KC - 1))
            tsg = P.op("act", lambda e, b_=b_: e.activation(out=gs2[0:NSMP, 0:96], in_=ps[b_][0:NSMP, 0:96], func=AF.Sigmoid),
                       waits=[tp], sig=True)
            ps_free[b_] = tsg
            state["a_free"] = tp
            tgr = None
            for b in range(NSMP):
                tg_ = P.op("pe", lambda e, b=b: e.matmul(ps[6][:, 0:96], lhsT=identb[0:NSMP, b:b + 1].to_broadcast([NSMP, 128]),
                                                         rhs=gs2[0:NSMP, 0:96], start=True, stop=True), waits=[tsg, ps_free[6]], sig=True)
                tgr = P.op("act", lambda e, b=b: e.activation(out=Grep[:, b, :], in_=ps[6][:, 0:96], func=AF.Copy), waits=[tg_], sig=True)
                ps_free[6] = tgr

            sfin = dict(step=0, e_free=[None, None], pt_free=[None, None])

            def s_init():
                return P.op("pe", lambda e: e.matmul(ps[2][:, 0:64], lhsT=zeros_bf[:, :], rhs=zeros_bf[:, 0:64], start=True, stop=False),
                            waits=[ps_free[2]], sig=True)

            def s_step(b, kT_fn, nk, v_fn, qsrc, mask_ap, e_dst=None, waits=()):
                i = sfin["step"] % 2
                sfin["step"] += 1
                tS = None
                for g in range(4):
                    tS = P.op("pe", lambda e, g=g: e.matmul(ps[i][0:nk, g * 8:(g + 1) * 8], lhsT=kT_fn(g), rhs=qsrc[:, b, g * 8:(g + 1) * 8],
                                                            start=True, stop=True), waits=[ps_free[i]] + list(waits), sig=(g == 3))
                ed = e_dst if e_dst is not None else es_sb[i]
                tE = P.op("act", lambda e: e.activation(out=ed[0:nk], in_=ps[i][0:nk, 0:32], func=AF.Exp, scale=SCALE),
                          waits=[tS, sfin["e_free"][i]], sig=True)
                ps_free[i] = tE
                if mask_ap is not None:
                    tM = P.op("dve", lambda e: e.tensor_tensor(
                        out=pts_sb[i][0:nk].rearrange("p (g r) -> p g r", g=4), in0=ed[0:nk].rearrange("p (g r) -> p g r", g=4),
                        in1=mask_ap, op=ALU.mult), waits=[tE, sfin["pt_free"][i]], sig=True)
                    src = pts_sb[i]
                    sfin["e_free"][i] = tM
                else:
                    tM = tE
                    src = ed
                tO = None
                for g in range(4):
                    P.op("pe", lambda e, g=g: e.matmul(ps[2][:, g * 8:(g + 1) * 8], lhsT=v_fn(g), rhs=src[0:nk, g * 8:(g + 1) * 8],
                                                       start=False, stop=False, skip_group_check=True), waits=[tM])
                tO = P.op("pe", lambda e: e.matmul(ps[2][:, 32:64], lhsT=ones_bf[0:nk, :], rhs=src[0:nk, 0:32],
                                                   start=False, stop=False, skip_group_check=True), sig=True)
                if mask_ap is not None:
                    sfin["pt_free"][i] = tO
                elif e_dst is None:
                    sfin["e_free"][i] = tO
                return tO

            def s_finish(b, tO, br, first):
                t1 = P.op("dve", lambda e: e.tensor_scalar(out=rec_s, in0=ps[2][:, 32:64], scalar1=1e-30, scalar2=None, op0=ALU.add),
                          waits=[tO], sig=True)
                P.op("dve", lambda e: e.reciprocal(out=rec_s, in_=rec_s))
                P.op("dve", lambda e: e.tensor_tensor(out=coef_s, in0=rec_s, in1=Grep[:, b, br:96:3], op=ALU.mult), waits=[tgr])
                if first:
                    t4 = P.op("dve", lambda e: e.tensor_tensor(out=acc_s[:, b, :], in0=ps[2][:, 0:32], in1=coef_s, op=ALU.mult), sig=True)
                else:
                    P.op("dve", lambda e: e.tensor_tensor(out=coef_s, in0=ps[2][:, 0:32], in1=coef_s, op=ALU.mult))
                    t4 = P.op("dve", lambda e: e.tensor_tensor(out=acc_s[:, b, :], in0=acc_s[:, b, :], in1=coef_s, op=ALU.add), sig=True)
                ps_free[2] = t4
                return t4

            e0col = ident[:, 0:1]
            for b in range(NSMP):
                t_seg_free = None
                for kvi in range(2):
                    tw1 = ring_pl.issue("pool", lambda e, kvi=kvi: e.dma_start(
                        out=w1_sb, in_=wc1_d[kvi].rearrange("(r p) h -> p r h", p=128)), waits=[P.lastsig.get("pe")])
                    tw2 = ring_pl.issue("pool", lambda e, kvi=kvi: e.dma_start(
                        out=w2_sb, in_=wc2_d[kvi].rearrange("(c p) h -> p c h", p=128)), waits=[P.lastsig.get("pe")])
                    raw = rawk if kvi == 0 else rawv
                    for sg_ in range(16):
                        nblk = 64 if sg_ < 15 else 63
                        npg = 9 if sg_ < 15 else 8
                        tl = None
                        rf = state["rows_free"]
                        for pg_ in range(npg):
                            sl = pg_ % 2
                            tk = page_rows(kvi, b, 8 * sg_ + pg_, rowst[:, sl, :], [rf[sl], t_seg_free])
                            b_ = misc_bank()
                            tpx = None
                            for g in range(4):
                                tpx = P.op("pe", lambda e, b_=b_, g=g, sl=sl: e.transpose(
                                    out=ps[b_][:, g * 128:(g + 1) * 128], in_=rowst[:, sl, g * 128:(g + 1) * 128], identity=ident[:]),
                                    waits=[tk, ps_free[b_]], sig=(g == 3))
                            rf[sl] = tpx
                            tl = P.op("act", lambda e, b_=b_, pg_=pg_, raw=raw: e.activation(
                                out=raw[:, :, pg_ * 128:(pg_ + 1) * 128], in_=ps[b_][:, :].rearrange("p (g t) -> p g t", g=4),
                                func=AF.Copy), waits=[tpx], sig=True)
                            ps_free[b_] = tl
                        tsil = None
                        for hc in range(2):
                            bb = hc
                            tm = None
                            for r in range(32):
                                tm = P.op("pe", lambda e, bb=bb, hc=hc, r=r, raw=raw, nblk=nblk: e.matmul(
                                    ps[bb][:, 0:4 * nblk].rearrange("p (g n) -> p g n", g=4), lhsT=w1_sb[:, r, hc * 128:(hc + 1) * 128],
                                    rhs=raw[:, :, r:r + 16 * (nblk - 1) + 1:16], start=(r == 0), stop=(r == 31)),
                                    waits=[tw1, tl, ps_free[bb]], sig=(r == 31))
                            tsil = P.op("act", lambda e, bb=bb, hc=hc, kvi=kvi, nblk=nblk: e.activation(
                                out=sT[:, hc, 0:4 * nblk], in_=ps[bb][:, 0:4 * nblk], func=AF.Silu,
                                bias=bias_sb[:, 2 * kvi + hc:2 * kvi + hc + 1]), waits=[tm, state.get("sT_free")], sig=True)
                            ps_free[bb] = tsil
                        t_seg_free = tm
                        if kvi == 0:
                            tm = None
                            for hc in range(2):
                                tm = P.op("pe", lambda e, hc=hc, nblk=nblk: e.matmul(ps[3][:, 0:4 * nblk], lhsT=w2_sb[:, hc, :],
                                                                                     rhs=sT[:, hc, 0:4 * nblk], start=(hc == 0), stop=(hc == 1)),
                                          waits=[tsil, tw2, ps_free[3]], sig=(hc == 1))
                            state["sT_free"] = tm
                            tsq = P.op("act", lambda e, nblk=nblk: e.activation(out=sqb[:, 0:4 * nblk], in_=ps[3][:, 0:4 * nblk], func=AF.Square),
                                       waits=[tm], sig=True)
                            tss = P.op("pe", lambda e, nblk=nblk: e.matmul(ps[6][:, 0:4 * nblk], lhsT=ones_bf[:], rhs=sqb[:, 0:4 * nblk],
                                                                            start=True, stop=True), waits=[tsq, ps_free[6]], sig=True)
                            t1 = P.op("dve", lambda e, nblk=nblk: e.tensor_scalar(out=rec_sb[:, 0:4 * nblk], in0=ps[6][:, 0:4 * nblk],
                                                                                  scalar1=1.0 / HD, scalar2=EPS, op0=ALU.mult, op1=ALU.add),
                                      waits=[tss], sig=True)
                            ps_free[6] = t1
                            t2 = P.op("pool", lambda e, nblk=nblk: e.tensor_tensor(out=rec_sb[:, 0:4 * nblk], in0=rec_sb[:, 0:4 * nblk],
                                                                                   in1=mhalf[:, 0:1].to_broadcast([128, 4 * nblk]), op=ALU.pow),
                                      waits=[t1], sig=True)
                            t3_ = P.op("dve", lambda e, nblk=nblk, sg_=sg_: e.scalar_tensor_tensor(
                                out=ckT_s[:, :, sg_ * 64:sg_ * 64 + nblk], in0=ps[3][:, 0:4 * nblk].rearrange("p (g n) -> p g n", g=4),
                                scalar=gk_col[:, 0:1], in1=rec_sb[:, 0:4 * nblk].rearrange("p (g n) -> p g n", g=4),
                                op0=ALU.mult, op1=ALU.mult), waits=[t2], sig=True)
                            ps_free[3] = t3_
                        else:
                            tm = None
                            p0 = (sg_ % 2) * 64
                            for g in range(4):
                                for hc in range(2):
                                    tm = P.op("pe", lambda e, g=g, hc=hc, nblk=nblk, p0=p0: e.matmul(
                                        ps[3][p0:p0 + nblk, g * 128:(g + 1) * 128], lhsT=sT[:, hc, g * nblk:(g + 1) * nblk], rhs=w2_sb[:, hc, :],
                                        start=(hc == 0), stop=(hc == 1)), waits=[tsil, tw2, ps_free[3]], sig=(hc == 1))
                            state["sT_free"] = tm
                            tcv = P.op("act", lambda e, sg_=sg_, nblk=nblk, p0=p0: e.activation(
                                out=cv_s[p0:p0 + nblk, sg_ // 2, :, :], in_=ps[3][p0:p0 + nblk, :].rearrange("p (g h) -> p g h", g=4),
                                func=AF.Copy), waits=[tm], sig=True)
                            ps_free[3] = tcv
                t_cmp_s = [P.lastsig.get("dve"), P.lastsig.get("act")]
                P.op("dve", lambda e: e.memset(PTall, 0.0) if hasattr(e, "memset") else None) if False else None
                tz = P.op("pool", lambda e: e.memset(PTall, 0.0), waits=[P.lastsig.get("pe"), P.lastsig.get("dve")], sig=True)
                s_init()
                tO = None
                for t_ in range(8):
                    nk = 128 if t_ < 7 else 127
                    tO = s_step(b, lambda g, t_=t_, nk=nk: ckT_s[:, g, t_ * 128:t_ * 128 + nk], nk,
                                lambda g, t_=t_, nk=nk: cv_s[0:nk, t_, g, :], qT_s, None, e_dst=PTall[:, t_, :],
                                waits=t_cmp_s + [tz])
                s_finish(b, tO, 0, True)
                P.op("dve", lambda e: e.tensor_tensor(out=pn_all, in0=PTall, in1=rec_s.unsqueeze(1).to_broadcast([128, 8, 32]), op=ALU.mult))
                P.op("dve", lambda e: e.tensor_reduce(out=pcol_f, in_=pn_all.rearrange("p t (g r) -> p t g r", g=4),
                                                      axis=mybir.AxisListType.X, op=ALU.add))
                tpc = P.op("dve", lambda e: e.tensor_copy(out=pcol_bf, in_=pcol_f), sig=True)
                tsl = None
                for t_ in range(8):
                    tsl = P.op("pe", lambda e, t_=t_: e.matmul(ps[7][0:4, t_ * 32:(t_ + 1) * 32], lhsT=pcol_bf[:, t_, :], rhs=A128,
                                                               start=True, stop=(t_ == 0)), waits=[tpc, ps_free[7]], sig=True)
                    if t_ >= 1:
                        tsl = P.op("pe", lambda e, t_=t_: e.matmul(ps[7][0:4, t_ * 32:(t_ + 1) * 32], lhsT=pcol_bf[:, t_ - 1, :], rhs=Aprev,
                                                                   start=False, stop=True), sig=True)
                P.op("pool", lambda e: e.memset(sc_s[0:4, 0:264], 0.0), waits=[P.lastsig.get("dve")])
                tcp = P.op("act", lambda e: e.activation(out=sc_s[0:4, 0:256], in_=ps[7][0:4, 0:256], func=AF.Copy), waits=[tsl, P.lastsig.get("pool")], sig=True)
                ps_free[7] = tcp
                P.op("pool", lambda e: e.memset(sc_s[0:4, 255:256], 1e9), waits=[tcp])
                P.op("pool", lambda e: e.memset(sc_s[0:4, 256:257], 2e9))
                tfz = P.op("pool", lambda e: e.memset(sc_s[0:4, 0:1], 3e9), sig=True)
                P.op("dve", lambda e: e.max(out=m8a[0:4], in_=sc_s[0:4, 0:257]), waits=[tfz])
                P.op("dve", lambda e: e.match_replace(out=sc2_s[0:4, 0:257], in_to_replace=m8a[0:4], in_values=sc_s[0:4, 0:257], imm_value=-3e38))
                P.op("dve", lambda e: e.max(out=m8b[0:4], in_=sc2_s[0:4, 0:257]))
                tse = P.op("dve", lambda e: e.tensor_scalar(out=sel_bf[0:4, 0:257], in0=sc_s[0:4, 0:257], scalar1=m8b[0:4, 7:8], scalar2=None,
                                                            op0=ALU.is_ge), sig=True)
                tmk = None
                for g in range(4):
                    tx = P.op("pe", lambda e, g=g: e.matmul(ps[6][:, 0:257], lhsT=identb[0:4, g:g + 1].to_broadcast([4, 128]),
                                                            rhs=sel_bf[0:4, 0:257], start=True, stop=True), waits=[tse, ps_free[6]], sig=True)
                    P.op("act", lambda e, g=g: e.activation(out=maskT[0:64, 0:128, g], in_=ps[6][0:64, 0:256:2], func=AF.Copy), waits=[tx])
                    tmk = P.op("act", lambda e, g=g: e.activation(out=maskT[64:128, 0:128, g], in_=ps[6][64:128, 1:256:2], func=AF.Copy), sig=True)
                    ps_free[6] = tmk
                tmk = P.op("act", lambda e: e.activation(out=maskT[:, 128, :], in_=ident[:, 0:1].to_broadcast([128, 4]), func=AF.Copy), sig=True)
                s_init()
                tO = None
                kfree = [None, None]
                vfree = [None, None]
                for t_ in range(129):
                    sl = t_ % 2
                    if t_ < 128:
                        tk = page_rows(2, b, t_, rowst[:, sl, :], [state["rows_free"][sl]])
                        tv = page_rows(3, b, t_, Vp[sl], [vfree[sl]])
                    else:
                        tzk = P.op("pool", lambda e, sl=sl: e.memset(rowst[:, sl, :], 0.0), waits=[state["rows_free"][sl]], sig=True)
                        tzv = P.op("pool", lambda e, sl=sl: e.memset(Vp[sl], 0.0), waits=[vfree[sl]], sig=True)
                        tk = ring_in.issue("sp", lambda e, sl=sl, b=b: e.dma_start(out=rowst[0:1, sl, :], in_=kvs_o[2][b:b + 1, :]), waits=[tzk])
                        tv = ring_pl.issue("pool", lambda e, sl=sl, b=b: e.dma_start(out=Vp[sl][0:1, :], in_=kvs_o[3][b:b + 1, :]), waits=[tzv])
                    b_ = misc_bank()
                    tpx = None
                    for g in range(4):
                        tpx = P.op("pe", lambda e, b_=b_, g=g, sl=sl: e.transpose(
                            out=ps[b_][:, g * 128:(g + 1) * 128], in_=rowst[:, sl, g * 128:(g + 1) * 128], identity=ident[:]),
                            waits=[tk, ps_free[b_]], sig=(g == 3))
                    state["rows_free"][sl] = tpx
                    tkt = P.op("act", lambda e, b_=b_, sl=sl: e.activation(out=kTp[sl], in_=ps[b_][:, :].rearrange("p (g t) -> p g t", g=4),
                                                                            func=AF.Copy), waits=[tpx, kfree[sl]], sig=True)
                    ps_free[b_] = tkt
                    tO = s_step(b, lambda g, sl=sl: kTp[sl][:, g, :], 128, lambda g, sl=sl: Vp[sl][:, g * 128:(g + 1) * 128], qrT_s,
                                maskT[:, t_, :].unsqueeze(2).to_broadcast([128, 4, 8]), waits=[tkt, tv, tmk])
                    kfree[sl] = tO
                    vfree[sl] = tO
                s_finish(b, tO, 1, False)
                s_init()
                for t_ in range(4):
                    sl = t_ % 2
                    tk = ring_in.issue("sp", lambda e, sl=sl, b=b, t_=t_: e.dma_start(out=rowst[:, sl, :], in_=wins_o[0][b, t_ * 128:(t_ + 1) * 128, :]),
                                       waits=[state["rows_free"][sl]])
                    tv = ring_pl.issue("pool", lambda e, sl=sl, b=b, t_=t_: e.dma_start(out=Vp[sl], in_=wins_o[1][b, t_ * 128:(t_ + 1) * 128, :]),
                                       waits=[vfree[sl]])
                    b_ = misc_bank()
                    tpx = None
                    for g in range(4):
                        tpx = P.op("pe", lambda e, b_=b_, g=g, sl=sl: e.transpose(
                            out=ps[b_][:, g * 128:(g + 1) * 128], in_=rowst[:, sl, g * 128:(g + 1) * 128], identity=ident[:]),
                            waits=[tk, ps_free[b_]], sig=(g == 3))
                    state["rows_free"][sl] = tpx
                    tkt = P.op("act", lambda e, b_=b_, sl=sl: e.activation(out=kTp[sl], in_=ps[b_][:, :].rearrange("p (g t) -> p g t", g=4),
                                                                            func=AF.Copy), waits=[tpx, kfree[sl]], sig=True)
                    ps_free[b_] = tkt
                    tO = s_step(b, lambda g, sl=sl: kTp[sl][:, g, :], 128, lambda g, sl=sl: Vp[sl][:, g * 128:(g + 1) * 128], qrT_s,
                                None, waits=[tkt, tv])
                    kfree[sl] = tO
                    vfree[sl] = tO
                s_finish(b, tO, 2, False)

            P.strict = set()
            barrier()
            state["scA_free"] = None
            state["scB_free"] = None
            state["a_free"] = None
            state["gfree"] = [None, None]
            state["jfree"] = [None, None]
            def phase2_tile(qt):
                smp = (qt == NT)
                W = NSMP if smp else TW
                t0 = qt * TW
                P.strict = {"act", "dve", "pool"} if smp else set()
                th = ring_in.issue("sp", lambda e, qt=qt, W=W: e.dma_start(
                    out=h[:, :, 0:W], in_=h1_s[qt].rearrange("p (k t) -> p k t", k=KC)[:, :, 0:W]), waits=[state.get("y_done")])
                if smp:
                    to_ = P.op("dve", lambda e: e.tensor_copy(out=a[:, :, 0:NSMP], in_=acc_s[:].rearrange("p b c -> p c b")),
                               waits=[state.get("y_done")], sig=True)
                else:
                    to_ = ring_in.issue("sp", lambda e, qt=qt: e.dma_start(out=a[:, :, :], in_=oT_s[qt].rearrange("p (k t) -> p k t", k=KC)),
                                        waits=[state.get("y_done")])

                def epi_o(j, pp, tp):
                    return P.op("dve", lambda e: e.tensor_tensor(out=h[:, j, 0:W], in0=pp, in1=h[:, j, 0:W], op=ALU.add),
                                waits=[tp, th], sig=True)
                tmix = dense_fm(wo_v, 0, KC, 0, D, lambda kk: a[:, kk, 0:W], W, epi_o, banks=(0, 1, 2, 3), waits=[to_])
                state["a_free"] = tmix
                if smp:
                    rows_list = [(ps_d[1], NSMP)]
                else:
                    rows_list = [(p_d[1, t0 + tb * 128:t0 + (tb + 1) * 128, :], 128) for tb in range(4)]
                th2 = mlp_and_ple(1, W, None, rows_list, [tmix])
                stage = scB[:].rearrange("p a b -> p (a b)")
                ty = None
                if smp:
                    tcp = None
                    for c0 in range(0, KC, 4):
                        b = misc_bank()
                        tpe = None
                        for i_ in range(4):
                            tpe = P.op("pe", lambda e, b=b, i_=i_, kc=c0 + i_: e.transpose(
                                out=ps[b][0:NSMP, i_ * 128:(i_ + 1) * 128], in_=h[:, kc, 0:NSMP], identity=ident[:]),
                                waits=[th2, ps_free[b]], sig=(i_ == 3))
                        tcp = P.op("act", lambda e, b=b, c0=c0: e.activation(out=stage[0:NSMP, c0 * 128:(c0 + 4) * 128],
                                                                               in_=ps[b][0:NSMP, :], func=AF.Copy),
                                   waits=[tpe, state["scB_free"]], sig=True)
                        ps_free[b] = tcp
                    ty = store_rows(ys_o[:, :], stage[0:NSMP, 0:D], [tcp])
                    return
                for tb in range(4):
                    tcp = None
                    for c0 in range(0, KC, 4):
                        b = misc_bank()
                        tpe = None
                        for i_ in range(4):
                            tpe = P.op("pe", lambda e, b=b, i_=i_, kc=c0 + i_, tb=tb: e.transpose(
                                out=ps[b][:, i_ * 128:(i_ + 1) * 128], in_=h[:, kc, tb * 128:(tb + 1) * 128], identity=ident[:]),
                                waits=[th2, ps_free[b]], sig=(i_ == 3))
                        eng = "act" if (c0 // 4) % 2 == 0 else "dve"
                        if eng == "act":
                            tcp = P.op("act", lambda e, b=b, c0=c0: e.activation(out=stage[:, c0 * 128:(c0 + 4) * 128], in_=ps[b][:, :],
                                                                                   func=AF.Copy), waits=[tpe, state["scB_free"]], sig=True)
                        else:
                            tcp = P.op("dve", lambda e, b=b, c0=c0: e.tensor_copy(out=stage[:, c0 * 128:(c0 + 4) * 128], in_=ps[b][:, :]),
                                       waits=[tpe, state["scB_free"]], sig=True)
                        ps_free[b] = tcp
                        state["y_pe"] = tpe
                    ty = store_rows(y_o[t0 + tb * 128:t0 + (tb + 1) * 128, :], stage[:, 0:D],
                                    [P.lastsig.get("act"), P.lastsig.get("dve")])
                    state["scB_free"] = ty
                state["y_done"] = state["y_pe"]
                state["h_free"] = state["y_pe"]

            for qt in range(NT + 1):
                phase2_tile(qt)
            P.strict = set()


    for sq in range(NSEQ):
        run_pass(sq)

    P.op("sp", lambda e: e.nop(), waits=[t for t in ring_out.last + ring_in.last + ring_pl.last if t is not None])

    with nc.Block() as block:
        @block.tensor
        def _(e):
            P.replay("pe", e)

        @block.scalar
        def _(e):
            P.replay("act", e)

        @block.vector
        def _(e):
            P.replay("dve", e)

        @block.gpsimd
        def _(e):
            P.replay("pool", e)

        @block.sync
        def _(e):
            P.replay("sp", e)
    stack.close()
    return nc


def rope_tables():
    half = 64
    inv = (10000.0 ** (-np.arange(half, dtype=np.float32) / half)).astype(np.float32)
    pos = np.concatenate([np.arange(T, dtype=np.float32), np.array([PAST], np.float32)])
    ang = pos[:, None].astype(np.float32) * inv[None, :]
    return np.stack([np.cos(ang), np.sin(ang)], axis=1).astype(np.float32)


def make_in_maps(inp, ncores):
    f = np.ascontiguousarray
    vecs = f(np.stack([inp["g_mix"][0], inp["g_mix"][1], inp["pool_scale"][0], inp["g_kv"],
                       inp["g_ffn"][0], inp["g_ffn"][1], inp["g_ple"][0], inp["g_ple"][1]]))
    hvecs = f(np.stack([inp["g_k_cmp"], inp["g_k_sel"], inp["g_k_win"], inp["g_q"][0]]))
    cs = rope_tables()
    NS = NSEQ * NSMP
    maps = []
    for c in range(ncores):
        sq = slice(NSEQ * c, NSEQ * (c + 1))
        sl = slice(NS * c, NS * (c + 1))
        m = dict(
            x=f(inp["x_prompt"][sq]), xs=f(inp["x_sample"][sl, 0].reshape(NSEQ, NSMP, D)),
            spool=f(inp["state_pool"][0, sl].reshape(NSEQ, NSMP, 15, D)),
            p=f(inp["p_prompt"][:, sq].transpose(1, 0, 2, 3)),
            psm=f(inp["p_sample"][:, sl, 0].reshape(2, NSEQ, NSMP, PLE).transpose(1, 0, 2, 3)),
            vecs=vecs, hvecs=hvecs, cossin=cs,
            w_pool=f(inp["w_pool"][0]), w_up0=inp["w_up"][0], w_up1=inp["w_up"][1],
            w_down0=inp["w_down"][0], w_down1=inp["w_down"][1],
            w_gate0=inp["w_ple_gate"][0], w_gate1=inp["w_ple_gate"][1],
            w_ple0=inp["w_ple"][0], w_ple1=inp["w_ple"][1],
            w_kv=f(inp["w_kv"].reshape(D, 6 * 512)),
            st_kwin=f(inp["state_k_win"][sl].reshape(NSEQ, NSMP, WIN, 512)),
            st_vwin=f(inp["state_v_win"][sl].reshape(NSEQ, NSMP, WIN, 512)),
            w_qg=inp["w_qg"][0], w_o=inp["w_o"][0],
            w_cmp_k1=inp["w_cmp_k1"], w_cmp_v1=inp["w_cmp_v1"], w_cmp_k2=inp["w_cmp_k2"], w_cmp_v2=inp["w_cmp_v2"],
            pe_cmp_k=inp["pe_cmp_k"], pe_cmp_v=inp["pe_cmp_v"],
            ptab=f(inp["page_table"][sl].reshape(NSEQ, NSMP, 128)),
            cache0=inp["cache_k_cmp"].reshape(1280 * 128, 512), cache1=inp["cache_v_cmp"].reshape(1280 * 128, 512),
            cache2=inp["cache_k_sel"].reshape(1280 * 128, 512), cache3=inp["cache_v_sel"].reshape(1280 * 128, 512),
        )
        maps.append(m)
    return maps


def kernel(**inp):
    ncores = 4 // NSEQ
    nc = build(stages=("A", "B"))
    maps = make_in_maps(inp, ncores)
    res = run_bass_kernel_spmd(nc, maps, core_ids=list(range(ncores)))
    R = res.results
    B = 4

    def cat(k):
        return np.concatenate([R[c][k] for c in range(ncores)], axis=0)
    y_p = cat("y")
    y_s = cat("ysm").reshape(8, 1, D)
    pool_p = cat("pool_p")[None]
    pool_s = cat("pool_s").reshape(8, 15, D)[None]
    kv_p = [cat("kv%d_p" % n).reshape(B, T, 4, 128) for n in range(4)]
    win_p = [cat(k).reshape(B, WIN, 4, 128) for k in ("kwin_p", "vwin_p")]
    kv_s = [cat("kv%d_s" % n).reshape(8, 1, 4, 128) for n in range(4)]
    win_s = [cat(k).reshape(8, WIN, 4, 128) for k in ("kwin_s", "vwin_s")]
    return (y_p, y_s, pool_p, pool_s, *kv_p, *win_p, *kv_s, *win_s)
```

```python
import contextlib
import numpy as np
import concourse.bass as bass
import concourse.mybir as mybir
from concourse.bass_utils import run_bass_kernel_spmd

AF = mybir.ActivationFunctionType
ALU = mybir.AluOpType
F32, BF16, I32 = mybir.dt.float32, mybir.dt.bfloat16, mybir.dt.int32

D = 4096
KC = 32
DFF = 16384
T = 2048
TW = 512
NT = T // TW
PLE = 256
HD = 128
G = 4
EPS = 1e-6
NSMP = 2
NSEQ = 1
PAST = 16384
WIN = 512


class Sem:
    def __init__(self, h):
        self.h = h
        self.n = 0


class Prog:
    ENG = ("pe", "act", "dve", "pool", "sp")

    def __init__(self, nc, stack):
        self.nc = nc
        self.stack = stack
        self.q = {e: [] for e in self.ENG}
        self.nsem = 0
        self.esem = {e: self.newsem("e_" + e) for e in self.ENG}
        self.strict = set()
        self.last = {}
        self.lastsig = {}

    def newsem(self, name):
        self.nsem += 1
        return Sem(self.stack.enter_context(self.nc.semaphore(name)))

    def op(self, eng, fn, waits=(), sig=False, sem=None, dma=False, force=()):
        tok = None
        inc = None
        force = list(force)
        if eng in self.strict and sem is None:
            sig = True
            force.append(self.last.get(eng))
        if sig or sem is not None:
            sm = sem if sem is not None else self.esem[eng]
            k = 16 if dma else 1
            sm.n += k
            tok = (sm, sm.n)
            inc = (sm.h, k)
        self.q[eng].append((fn, [w for w in waits if w is not None], inc, [w for w in force if w is not None]))
        if sem is None and tok is not None:
            self.last[eng] = tok
            self.lastsig[eng] = tok
        return tok

    def replay(self, eng, e):
        seen = {}
        own = self.esem[eng]
        for fn, waits, inc, force in self.q[eng]:
            for (sm, v) in force:
                e.wait_ge(sm.h, v)
            for (sm, v) in waits:
                if sm is own:
                    continue
                if seen.get(id(sm), -1) >= v:
                    continue
                e.wait_ge(sm.h, v)
                seen[id(sm)] = v
            if fn is None:
                continue
            r = fn(e)
            if inc is not None:
                r.then_inc(inc[0], inc[1])


def build(stages=("A",), debug=False):
    nc = bass.Bass("TRN2", target_bir_lowering=False)
    stack = contextlib.ExitStack()
    P = Prog(nc, stack)

    def din(name, shape, dt=F32):
        return nc.dram_tensor(name, list(shape), dt, kind="ExternalInput").ap()

    def dout(name, shape, dt=F32):
        return nc.dram_tensor(name, list(shape), dt, kind="ExternalOutput").ap()

    def dint(name, shape, dt):
        return nc.dram_tensor(name, list(shape), dt, kind="Internal").ap()

    _fregs = {}

    def freg(e, v):
        if v not in _fregs:
            _fregs[v] = e.to_reg(v)
        return _fregs[v]

    def sb(name, shape, dt):
        return stack.enter_context(nc.sbuf_tensor(name, list(shape), dt))

    x_all = din("x", [NSEQ, T, D])
    xs_all = din("xs", [NSEQ, NSMP, D])
    sp_all = din("spool", [NSEQ, NSMP, 15, D])
    p_all = din("p", [NSEQ, 2, T, PLE])
    ps_all = din("psm", [NSEQ, 2, NSMP, PLE])
    vec_d = din("vecs", [8, D])
    hv_d = din("hvecs", [4, HD])
    cs_d = din("cossin", [T + 1, 2, 64])
    wpool_d = din("w_pool", [4, 1024, 1024])
    NL = 2 if "B" in stages else 1
    wup_d = [din("w_up%d" % l, [D, DFF]) for l in range(NL)]
    wdn_d = [din("w_down%d" % l, [DFF, D]) for l in range(NL)]
    wgt_d = [din("w_gate%d" % l, [D, D]) for l in range(NL)]
    wple_d = [din("w_ple%d" % l, [PLE, D]) for l in range(NL)]
    wkv_d = din("w_kv", [D, 6 * 512])
    skw_all = din("st_kwin", [NSEQ, NSMP, WIN, 512])
    svw_all = din("st_vwin", [NSEQ, NSMP, WIN, 512])

    y_all = dout("y", [NSEQ, T, D])
    ys_all = dout("ysm", [NSEQ, NSMP, D])
    poolp_all = dout("pool_p", [NSEQ, 15, D])
    pools_all = dout("pool_s", [NSEQ, NSMP, 15, D])
    kv_all = [dout("kv%d_p" % n, [NSEQ, T, 512]) for n in range(4)]
    win_all = [dout("kwin_p", [NSEQ, WIN, 512]), dout("vwin_p", [NSEQ, WIN, 512])]
    kvs_all = [dout("kv%d_s" % n, [NSEQ, NSMP, 512]) for n in range(4)]
    wins_all = [dout("kwin_s", [NSEQ, NSMP, WIN, 512]), dout("vwin_s", [NSEQ, NSMP, WIN, 512])]

    _once = {}

    def din_once(name, shape, dt=F32):
        if name not in _once:
            _once[name] = din(name, shape, dt)
        return _once[name]

    def sb_once(name, shape, dt):
        if name not in _once:
            _once[name] = sb(name, shape, dt)
        return _once[name]

    h1_s = dint("h1_scr", [NT + 1, 128, KC * TW], F32)
    a1_s = dint("a1_scr", [NT + 1, 128, KC * TW], BF16)
    oT_s = dint("oT_scr", [NT + 1, 128, KC * TW], BF16)
    kw_s = [dint("kwin_scr", [T, 512], F32), dint("vwin_scr", [T, 512], F32)]
    dbg_o = dout("dbg", [6, 128, KC * 2]) if debug else None

    def dbg_dump(i, waits):
        if dbg_o is None:
            return
        ring_out.issue("sp", lambda e: e.dma_start(out=dbg_o[i].rearrange("p (k t) -> p k t", k=KC), in_=h[:, :, 0:2]),
                       waits=waits)

    h = sb("h", [128, KC, TW], F32)
    a = sb("a", [128, KC, TW], BF16)
    scA = sb("scA", [128, 8, TW + 16], F32)
    scB = sb("scB", [128, 8, TW + 16], F32)
    NWB = 2
    wb = [sb("wb%d" % i, [128, 8192], BF16) for i in range(NWB)]
    hx = sb("hx", [128, 1536], F32)
    halo = hx[:, 0:512].rearrange("p (k t) -> p k t", k=KC)
    rstd = sb("rstd", [128, TW], F32)
    rt = sb("rt", [128, TW], F32)
    gate_t = sb("gate_t", [128, 2, TW], F32)
    gcol = sb("gcol", [128, 8, KC], F32)
    ident = sb("ident", [128, 128], F32)
    ones_bf = sb("ones_bf", [128, 128], BF16)
    ones_f = sb("ones_f", [128, 128], F32)
    eps_t = sb("eps_t", [128, 1], F32)
    mhalf = sb("mhalf", [128, 1], F32)
    invc = hx[:, 512:1024].rearrange("p (g k t) -> p g k t", g=4, k=8)
    pT = sb("pT", [128, 2, TW], BF16)
    stg = [sb("stg%d" % i, [128, 512], F32) for i in range(3)]
    stgT = []
    cs_t = sb("cs_t", [128, 2, 64], F32)
    hvb = sb("hvb", [128, 4, HD], F32)
    ss4 = sb("ss4", [128, 8], F32)
    rp = [sb("rp%d" % i, [128, 4, 64], F32) for i in range(4)]
    junk = sb("junk", [128, 512], F32)
    junk2 = sb("junk2", [128, 2, 512], F32)

    u_bf = scA[:].bitcast(BF16) if hasattr(scA[:], "bitcast") else None

    ps = [stack.enter_context(nc.psum_tensor("ps%d" % i, [128, 512], F32)) for i in range(8)]

    ps_free = [None] * 8
    wb_free = [None] * NWB
    wb_sem = [P.newsem("wbs%d" % i) for i in range(NWB)]
    wb_ctr = [0]
    wb.append(scB[:].rearrange("p a b -> p (a b)").bitcast(BF16)[:, 0:8192])
    wb_free.append(None)
    wb_sem.append(P.newsem("wbs_x"))
    out_tokens = []

    class Ring:
        def __init__(self, name, k):
            self.sems = [P.newsem("%s%d" % (name, i)) for i in range(k)]
            self.last = [None] * k
            self.i = 0

        def issue(self, eng, fn, waits=()):
            i = self.i % len(self.sems)
            self.i += 1
            tk = P.op(eng, fn, waits=list(waits) + [self.last[i]], sem=self.sems[i], dma=True)
            self.last[i] = tk
            return tk

    ring_in = Ring("rin", 4)
    ring_out = Ring("rout", 8)
    ring_pl = Ring("rpl", 2)

    tiny0 = sb_once("tiny", [128, 8], F32)

    def top_barrier():
        toks = [P.op("act", lambda e: e.activation(out=tiny0[:, 0:1], in_=tiny0[:, 0:1], func=AF.Copy), sig=True),
                P.op("dve", lambda e: e.tensor_copy(out=tiny0[:, 1:2], in_=tiny0[:, 1:2]), sig=True),
                P.op("pool", lambda e: e.memset(tiny0[:, 2:3], 0.0), sig=True),
                P.lastsig.get("pe")]
        for rg in (ring_in, ring_out, ring_pl):
            toks += [t for t in rg.last if t is not None]
        for eng in ("pe", "act", "dve", "pool", "sp"):
            P.op(eng, None, waits=toks)

    def run_pass(sq):
        x_d = x_all[sq]
        xs_d = xs_all[sq]
        sp_d = sp_all[sq]
        p_d = p_all[sq]
        ps_d = ps_all[sq]
        skw_d = skw_all[sq]
        svw_d = svw_all[sq]
        y_o = y_all[sq]
        ys_o = ys_all[sq]
        poolp_o = poolp_all[sq]
        pools_o = pools_all[sq]
        kv_o = [t_[sq] for t_ in kv_all]
        win_o = [t_[sq] for t_ in win_all]
        kvs_o = [t_[sq] for t_ in kvs_all]
        wins_o = [t_[sq] for t_ in wins_all]
        if sq > 0:
            P.strict = set()
            top_barrier()
        setup = []
        setup.append(P.op("pool", lambda e: e.memset(ones_f[:], 1.0)))
        P.op("pool", lambda e: e.memset(ones_bf[:], 1.0))
        P.op("pool", lambda e: e.memset(eps_t[:], EPS))
        P.op("pool", lambda e: e.memset(mhalf[:], -0.5))
        P.op("pool", lambda e: e.memset(halo[:], 0.0))
        P.op("pool", lambda e: e.affine_select(out=ident[:], in_=ones_f[:], pattern=[[1, 128]],
                                                compare_op=ALU.is_equal, fill=freg(e, 0.0), base=0,
                                                channel_multiplier=-1))
        for g, w in enumerate((2, 4, 8, 16)):
            P.op("pool", lambda e, g=g: e.iota(out=invc[:, g, :, :], pattern=[[0, 8], [1, 16]], base=1,
                                               channel_multiplier=0, allow_small_or_imprecise_dtypes=True))
            P.op("pool", lambda e, g=g, w=w: e.tensor_scalar(out=invc[:, g, :, :], in0=invc[:, g, :, :],
                                                            scalar1=float(w), scalar2=None, op0=ALU.min))
        t_const = P.op("pool", lambda e: e.nop() if False else e.memset(junk[:], 0.0), sig=True)
        P.op("dve", lambda e: e.reciprocal(out=invc[:], in_=invc[:]), waits=[t_const])
        t_const2 = P.op("dve", lambda e: e.tensor_copy(out=junk[:, 0:1], in_=junk[:, 0:1]), sig=True)

        vv = vec_d.rearrange("v (kc p) -> (v kc) p", p=128)
        sc_b_free0 = [None]
        for half in range(2):
            tk = ring_in.issue("sp", lambda e, half=half: e.dma_start(out=scB[:, 0, 0:128], in_=vv[half * 128:(half + 1) * 128, :]),
                               waits=[ps_free[4], sc_b_free0[0]])
            tp = P.op("pe", lambda e: e.transpose(out=ps[4][:, 0:128], in_=scB[:, 0, 0:128], identity=ident[:]),
                      waits=[tk, t_const2, t_const], sig=True)
            tc = P.op("act", lambda e, half=half: e.activation(
                out=gcol[:, half * 4:(half + 1) * 4, :],
                in_=ps[4][:, 0:128].rearrange("p (v k) -> p v k", v=4), func=AF.Copy), waits=[tp], sig=True)
            ps_free[4] = tc
            tk2 = tp
            if half == 0:
                P.op("sp", lambda e: e.nop() if hasattr(e, "nop") else None, waits=[tp]) if False else None
            sc_b_free = tp
            sc_b_free0[0] = tp
        GV = dict(g_mix0=0, g_mix1=1, pscale=2, g_kv=3, g_ffn0=4, g_ffn1=5, g_ple0=6, g_ple1=7)
        tk = ring_in.issue("sp", lambda e: e.dma_start(out=hvb[:].rearrange("p v h -> p (v h)"),
                                                       in_=hv_d.rearrange("v h -> (v h)").partition_broadcast(128)))
        t_hvb = tk

        state = dict(scB_free=sc_b_free, scA_free=None, a_free=None, stg_free=[None] * 3, stgT_free=[None] * 2,
                     stg_i=0, stgT_i=0, misc_i=0)

        def misc_bank():
            state["misc_i"] ^= 1
            return 4 + state["misc_i"]

        def load_T(src_rows_ap, rows, ncols, dst_fn, wait_extra=()):
            nchunk = ncols // 128
            stage = scB[:].rearrange("p a b -> p (a b)")
            tk = ring_in.issue("sp", lambda e: e.dma_start(out=stage[0:rows, 0:ncols], in_=src_rows_ap),
                               waits=[state["scB_free"]] + list(wait_extra))
            last = None
            lastpe = None
            for c0 in range(0, nchunk, 4):
                n = min(4, nchunk - c0)
                b = misc_bank()
                for i in range(n):
                    lastpe = P.op("pe", lambda e, b=b, i=i, c=c0 + i: e.transpose(
                        out=ps[b][:, i * rows:(i + 1) * rows], in_=stage[0:rows, c * 128:(c + 1) * 128],
                        identity=ident[0:rows, 0:rows]), waits=[tk, ps_free[b], t_const2], sig=(i == n - 1))
                eng, fn = dst_fn(c0, n, ps[b][:, 0:n * rows], rows)
                last = P.op(eng, fn, waits=[lastpe], sig=True)
                ps_free[b] = last
            state["scB_free"] = lastpe
            return last

        def norm_stats(W, wait=()):
            t1 = P.op("act", lambda e: e.activation(out=a[:, :, 0:W], in_=h[:, :, 0:W], func=AF.Square),
                      waits=list(wait) + [state["a_free"]], sig=True)
            tp = None
            for kc in range(KC):
                tp = P.op("pe", lambda e, kc=kc: e.matmul(ps[6][:, 0:W], lhsT=ones_bf[:], rhs=a[:, kc, 0:W],
                                                           start=(kc == 0), stop=(kc == KC - 1)),
                          waits=[t1, ps_free[6]], sig=(kc == KC - 1))
            t2 = P.op("dve", lambda e: e.tensor_scalar(out=rt[:, 0:W], in0=ps[6][:, 0:W], scalar1=1.0 / D, scalar2=EPS,
                                                       op0=ALU.mult, op1=ALU.add), waits=[tp], sig=True)
            ps_free[6] = t2
            t3 = P.op("pool", lambda e: e.tensor_tensor(out=rstd[:, 0:W], in0=rt[:, 0:W],
                                                        in1=mhalf[:, 0:1].to_broadcast([128, W]), op=ALU.pow),
                      waits=[t2], sig=True)
            state["a_free"] = tp
            return t3

        def normalize(W, gi, wait=()):
            tk = None
            for kc in range(KC):
                tk = P.op("dve", lambda e, kc=kc: e.scalar_tensor_tensor(
                    out=a[:, kc, 0:W], in0=h[:, kc, 0:W], scalar=gcol[:, gi, kc:kc + 1], in1=rstd[:, 0:W],
                    op0=ALU.mult, op1=ALU.mult), waits=list(wait) + [state["a_free"]], sig=(kc == KC - 1))
            return tk

        def wload(src_ap, kcb, nb, waits=(), nbuf=NWB):
            i = wb_ctr[0] % nbuf
            wb_ctr[0] += 1
            dst = wb[i][:, 0:kcb * nb].rearrange("p (k n) -> p k n", k=kcb)
            tk = P.op("pool", lambda e: e.dma_start(out=dst, in_=src_ap), waits=[wb_free[i]] + list(waits),
                      sem=wb_sem[i], dma=True)
            return i, dst, tk

        def dense_fm(Wv, k0, nk, c0, ncols, xin, W, epi, banks=(0, 1, 2, 3), waits=(), nbuf=NWB):
            NB = 512
            KB = min(nk, 16)
            blocks = []
            for cb in range(0, ncols, NB):
                nb = min(NB, ncols - cb)
                for kb in range(0, nk, KB):
                    blocks.append((cb, nb, kb, min(KB, nk - kb)))
            loaded = {}
            depth = nbuf - 1
            bi = [0]
            last_epi = None
            for i in range(len(blocks) + depth):
                if i < len(blocks):
                    cb, nb, kb, kn = blocks[i]
                    loaded[i] = wload(Wv[:, k0 + kb:k0 + kb + kn, c0 + cb:c0 + cb + nb], kn, nb, (), nbuf)
                j = i - depth
                if j < 0:
                    continue
                cb, nb, kb, kn = blocks[j]
                slot, wv, tk = loaded.pop(j)
                nfc = nb // 128
                if kb == 0:
                    cur = []
                    for f in range(nfc):
                        cur.append(banks[bi[0] % len(banks)])
                        bi[0] += 1
                    state["cur_banks"] = cur
                cur = state["cur_banks"]
                tp = None
                for f in range(nfc):
                    for k in range(kn):
                        first = (kb == 0 and k == 0)
                        last = (kb + kn == nk and k == kn - 1)
                        tp = P.op("pe", lambda e, b=cur[f], wv=wv, k=k, f=f, kk=kb + k, first=first, last=last:
                                  e.matmul(ps[b][:, 0:W], lhsT=wv[:, k, f * 128:(f + 1) * 128], rhs=xin(kk),
                                           start=first, stop=last),
                                  waits=[tk, ps_free[cur[f]]] + list(waits), sig=(k == kn - 1))
                    if kb + kn == nk:
                        tl = epi((cb // 128) + f, ps[cur[f]][:, 0:W], tp)
                        ps_free[cur[f]] = tl
                        last_epi = tl
                wb_free[slot] = tp
            return last_epi

        def store_rows(dst_ap, src_ap, waits):
            return ring_out.issue("sp", lambda e: e.dma_start(out=dst_ap, in_=src_ap), waits=waits)

        def next_stg():
            i = state["stg_i"] % 3
            state["stg_i"] += 1
            return i

        wpool_v = [wpool_d[g].rearrange("(kc p) f -> p kc f", p=128) for g in range(4)]
        wup_v = [w.rearrange("(kc p) f -> p kc f", p=128) for w in wup_d]
        wdn_v = [w.rearrange("(kc p) f -> p kc f", p=128) for w in wdn_d]
        wgt_v = [w.rearrange("(kc p) f -> p kc f", p=128) for w in wgt_d]
        wple_v = [w.rearrange("(kc p) f -> p kc f", p=128) for w in wple_d]
        wkv_v = wkv_d.rearrange("(kc p) f -> p kc f", p=128)
        POOLW = (2, 4, 8, 16)

        def mlp_and_ple(layer, W, p_rows_ap, rows_list, wait):
            t3 = norm_stats(W, wait=wait)
            tn = normalize(W, GV["g_ffn%d" % layer], wait=[t3])
            u = scA[:].rearrange("p a b -> p (a b)").bitcast(BF16)[:, 0:8 * TW].rearrange("p (k t) -> p k t", k=8)
            tlast = tn
            state["u_free"] = state.get("scA_free")
            wb_free[2] = state.get("scB_free")
            for s in range(DFF // 1024):
                def epi_up(fc, pp, tp, s=s):
                    sl = state.get("jslot", 0)
                    state["jslot"] = sl ^ 1
                    jf = state.setdefault("jfree", [None, None])
                    t1 = P.op("act", lambda e: e.activation(out=junk2[:, sl, 0:W], in_=pp, func=AF.Relu),
                              waits=[tp, jf[sl]], sig=True)
                    t2 = P.op("dve", lambda e: e.tensor_tensor(out=u[:, fc, 0:W], in0=junk2[:, sl, 0:W],
                                                               in1=junk2[:, sl, 0:W], op=ALU.mult), waits=[t1], sig=True)
                    jf[sl] = t2
                    state["last_pool_u"] = t2
                    return t1
                tu = dense_fm(wup_v[layer], 0, KC, s * 1024, 1024, lambda kk: a[:, kk, 0:W], W, epi_up,
                              banks=(0, 1, 2, 3), waits=[tn, state["u_free"]], nbuf=3)

                def epi_dn(j, pp, tp):
                    return P.op("dve", lambda e: e.tensor_tensor(out=h[:, j, 0:W], in0=pp, in1=h[:, j, 0:W], op=ALU.add),
                                waits=[tp], sig=True)
                td = dense_fm(wdn_v[layer], s * 8, 8, 0, D, lambda kk: u[:, kk, 0:W], W, epi_dn,
                              banks=(4, 5, 6, 7), waits=[tu, state["last_pool_u"]], nbuf=3)
                state["u_free"] = td
                tlast = td
            state["scA_free"] = tlast
            if wb_free[2] is not None:
                state["scB_free"] = wb_free[2]
            if W == NSMP:
                dbg_dump(2, [tlast])
            t3 = norm_stats(W, wait=[tlast])
            tn = normalize(W, GV["g_ple%d" % layer], wait=[t3])
            wple_sb = scA[:].rearrange("p a b -> p (a b)").bitcast(BF16)[:, 0:2 * D].rearrange("p (k f) -> p k f", k=2)
            twp = ring_pl.issue("pool", lambda e: e.dma_start(out=wple_sb, in_=wple_v[layer]), waits=[tlast])
            tpt = None
            off = 0
            for (src, rows) in rows_list:
                def dst(c0, n, pp, rows, off=off):
                    return ("act", lambda e: e.activation(
                        out=pT[:, c0:c0 + n, off:off + rows], in_=pp.rearrange("p (n r) -> p n r", n=n), func=AF.Copy))
                tpt = load_T(src, rows, PLE, dst)
                off += rows

            def epi_gate(j, pp, tp):
                gi = j % 2
                gf = state.setdefault("gfree", [None, None])
                t1 = P.op("act", lambda e: e.activation(out=gate_t[:, gi, 0:W], in_=pp, func=AF.Sigmoid),
                          waits=[tp, gf[gi]], sig=True)
                P.op("pe", lambda e: e.matmul(ps[7][:, 0:W], lhsT=wple_sb[:, 0, j * 128:(j + 1) * 128],
                                              rhs=pT[:, 0, 0:W], start=True, stop=False), waits=[ps_free[7], twp, tpt])
                t2 = P.op("pe", lambda e: e.matmul(ps[7][:, 0:W], lhsT=wple_sb[:, 1, j * 128:(j + 1) * 128],
                                                   rhs=pT[:, 1, 0:W], start=False, stop=True), sig=True)
                t3 = P.op("dve", lambda e: e.tensor_tensor(out=gate_t[:, gi, 0:W], in0=ps[7][:, 0:W],
                                                           in1=gate_t[:, gi, 0:W], op=ALU.mult), waits=[t1, t2], sig=True)
                ps_free[7] = t3
                t4 = P.op("dve", lambda e: e.tensor_tensor(out=h[:, j, 0:W], in0=gate_t[:, gi, 0:W],
                                                           in1=h[:, j, 0:W], op=ALU.add), sig=True)
                gf[gi] = t4
                return t1
            tg = dense_fm(wgt_v[layer], 0, KC, 0, D, lambda kk: a[:, kk, 0:W], W, epi_gate,
                          banks=(0, 1, 2, 3), waits=[tn, twp, tpt])
            gf = state["gfree"]
            tend = gf[0] if (gf[1] is None or (gf[0] is not None and gf[0][1] > gf[1][1])) else gf[1]
            state["scA_free"] = tend
            return tend

        def stage_a_tile(ti):
            sample = (ti == NT)
            W = NSMP if sample else TW
            t0 = 0 if sample else ti * TW
            tl = None
            nblk = 1 if sample else W // 128
            for tb in range(nblk):
                rows = W if sample else 128
                src = xs_d[:, :] if sample else x_d[t0 + tb * 128:t0 + (tb + 1) * 128, :]

                def dst(c0, n, pp, rows, tb=tb):
                    eng = "act" if (c0 // 4) % 2 == 0 else "dve"
                    if eng == "act":
                        return (eng, lambda e: e.activation(out=h[:, c0:c0 + n, tb * 128:tb * 128 + rows],
                                                            in_=pp.rearrange("p (n r) -> p n r", n=n), func=AF.Copy))
                    return (eng, lambda e: e.tensor_copy(out=h[:, c0:c0 + n, tb * 128:tb * 128 + rows],
                                                         in_=pp.rearrange("p (n r) -> p n r", n=n)))
                tl = load_T(src, rows, D, dst, wait_extra=[state.get("h_free"), state.get("h_free2")])
            if sample:
                dbg_dump(0, [tl])
            t3 = norm_stats(W, wait=[tl])
            seq = scA
            sB = scB
            pool_halos = [None] if not sample else list(range(NSMP))
            t_front = None
            def front(hb):
                nonlocal t_front
                if sample:
                    def dsth(c0, n, pp, rows):
                        return ("act", lambda e: e.activation(out=halo[:, c0:c0 + n, 0:15],
                                                              in_=pp.rearrange("p (n r) -> p n r", n=n), func=AF.Copy))
                    th = load_T(sp_d[hb], 15, D, dsth, wait_extra=[state.get("halo_free")])
                    col0, Wf = hb, 1
                else:
                    th = None
                    col0, Wf = 0, W
                L = 15 + Wf
                for g in range(4):
                    w = POOLW[g]
                    tw = [t3, th, state["scA_free"], state["scB_free"], state["a_free"]]
                    for k in range(8):
                        kc = 8 * g + k
                        P.op("dve", lambda e, k=k, kc=kc: e.scalar_tensor_tensor(
                            out=seq[:, k, 15:15 + Wf], in0=h[:, kc, col0:col0 + Wf], scalar=gcol[:, GV["g_mix0"], kc:kc + 1],
                            in1=rstd[:, col0:col0 + Wf], op0=ALU.mult, op1=ALU.mult), waits=tw)
                    P.op("dve", lambda e, g=g: e.tensor_copy(out=seq[:, :, 0:15], in_=halo[:, 8 * g:8 * g + 8, 0:15]),
                         waits=tw)
                    P.op("dve", lambda e: e.tensor_tensor(out=sB[:, :, 15:L], in0=seq[:, :, 15:L], in1=seq[:, :, 14:L - 1],
                                                          op=ALU.add), waits=tw)
                    for j in range(2, w):
                        P.op("dve", lambda e, j=j: e.tensor_tensor(out=sB[:, :, 15:L], in0=sB[:, :, 15:L],
                                                                   in1=seq[:, :, 15 - j:L - j], op=ALU.add))
                    if sample:
                        dsl = gate_t[:].rearrange("p a b -> p (a b)")[:, 0:KC * NSMP].rearrange(
                            "p (k t) -> p k t", k=KC)[:, 8 * g:8 * g + 8, col0:col0 + Wf]
                    else:
                        dsl = a[:, 8 * g:8 * g + 8, col0:col0 + Wf]
                    if (not sample) and ti == 0:
                        P.op("dve", lambda e, g=g: e.tensor_tensor(out=sB[:, :, 15:31], in0=sB[:, :, 15:31],
                                                                   in1=invc[:, g, :, :], op=ALU.mult))
                        P.op("dve", lambda e, g=g: e.tensor_tensor(out=a[:, 8 * g:8 * g + 8, 0:16], in0=sB[:, :, 15:31],
                                                                   in1=seq[:, :, 15:31], op=ALU.subtract))
                        P.op("dve", lambda e, g=g, w=w: e.scalar_tensor_tensor(
                            out=a[:, 8 * g:8 * g + 8, 16:Wf], in0=sB[:, :, 31:L], scalar=1.0 / w, in1=seq[:, :, 31:L],
                            op0=ALU.mult, op1=ALU.subtract))
                    else:
                        P.op("dve", lambda e, w=w, dsl=dsl: e.scalar_tensor_tensor(
                            out=dsl, in0=sB[:, :, 15:L], scalar=1.0 / w, in1=seq[:, :, 15:L],
                            op0=ALU.mult, op1=ALU.subtract))
                    t_front = P.op("dve", lambda e, g=g: e.tensor_copy(out=halo[:, 8 * g:8 * g + 8, 0:15],
                                                                       in_=seq[:, :, L - 15:L]), sig=True)
                    state["scA_free"] = t_front
                    state["scB_free"] = t_front
                if sample or ti == NT - 1:
                    dst_rows = pools_o[hb] if sample else poolp_o
                    si = next_stg()
                    stage = scB[:].rearrange("p a b -> p (a b)")
                    tpe = None
                    for c0 in range(0, KC, 4):
                        b = misc_bank()
                        for i in range(4):
                            tpe = P.op("pe", lambda e, b=b, i=i, kc=c0 + i: e.transpose(
                                out=ps[b][0:15, i * 128:(i + 1) * 128], in_=halo[:, kc, 0:15], identity=ident[:]),
                                waits=[t_front, ps_free[b]], sig=(i == 3))
                        tcp = P.op("act", lambda e, b=b, c0=c0: e.activation(out=stage[0:15, c0 * 128:(c0 + 4) * 128],
                                                                               in_=ps[b][0:15, 0:512], func=AF.Copy),
                                   waits=[tpe, state["scB_free"]], sig=True)
                        ps_free[b] = tcp
                    tst = store_rows(dst_rows, stage[0:15, 0:D], [tcp])
                    state["scB_free"] = tst
                    state["halo_free"] = tpe

            for hb in pool_halos:
                front(hb)
            if sample:
                dtmp = gate_t[:].rearrange("p a b -> p (a b)")[:, 0:KC * NSMP].rearrange("p (k t) -> p k t", k=KC)
                t_front = P.op("dve", lambda e: e.tensor_copy(out=a[:, :, 0:NSMP], in_=dtmp), sig=True)
                if dbg_o is not None:
                    ring_out.issue("sp", lambda e: e.dma_start(out=dbg_o[4].rearrange("p (k t) -> p k t", k=KC), in_=dtmp),
                                   waits=[t_front])
            state["a_free"] = None
            tmix = None
            for g in range(4):
                def epi_pool(ec, pp, tp, g=g):
                    kc = 8 * g + ec
                    return P.op("dve", lambda e: e.scalar_tensor_tensor(
                        out=h[:, kc, 0:W], in0=pp, scalar=gcol[:, GV["pscale"], kc:kc + 1], in1=h[:, kc, 0:W],
                        op0=ALU.mult, op1=ALU.add), waits=[tp], sig=True)
                tmix = dense_fm(wpool_v[g], 0, 8, 0, 1024, lambda kk, g=g: a[:, 8 * g + kk, 0:W], W, epi_pool,
                                banks=(0, 1, 2, 3), waits=[t_front])
            state["a_free"] = tmix
            if sample:
                dbg_dump(1, [tmix])
            if sample:
                rows_list = [(ps_d[0], NSMP)]
            else:
                rows_list = [(p_d[0, t0 + tb * 128:t0 + (tb + 1) * 128, :], 128) for tb in range(4)]
            th1 = mlp_and_ple(0, W, None, rows_list, [tmix])
            if sample:
                dbg_dump(3, [th1])
            tsp = ring_out.issue("sp", lambda e: e.dma_start(
                out=h1_s[ti].rearrange("p (k t) -> p k t", k=KC)[:, :, 0:W], in_=h[:, :, 0:W]), waits=[th1])
            state["h1_tok_%d" % ti] = tsp
            t3b = norm_stats(W, wait=[th1])
            tnb = normalize(W, GV["g_mix1"], wait=[t3b])
            ta1 = ring_out.issue("sp", lambda e: e.dma_start(
                out=a1_s[ti].rearrange("p (k t) -> p k t", k=KC)[:, :, 0:W], in_=a[:, :, 0:W]), waits=[tnb])
            t3 = norm_stats(W, wait=[th1, ta1])
            tn = normalize(W, GV["g_kv"], wait=[t3])
            for n in range(6):
                halves = []
                for hf in range(2):
                    halves.append(wload(wkv_v[:, hf * 16:(hf + 1) * 16, n * 512:(n + 1) * 512], 16, 512, [tn]))
                tps = []
                for hf in range(2):
                    slot, wv, tk = halves[hf]
                    tp = None
                    for tb in range(nblk):
                        rows = W if sample else 128
                        for k in range(16):
                            kk = hf * 16 + k
                            tp = P.op("pe", lambda e, tb=tb, rows=rows, wv=wv, k=k, kk=kk: e.matmul(
                                ps[tb][0:rows, 0:512], lhsT=a[:, kk, tb * 128:tb * 128 + rows], rhs=wv[:, k, :],
                                start=(kk == 0), stop=(kk == KC - 1)), waits=[tk, tn, ps_free[tb]], sig=(k == 15))
                        if hf == 1:
                            tps.append(tp)
                    wb_free[slot] = tp
                for tb in range(nblk):
                    rows = W if sample else 128
                    r0 = t0 + tb * 128
                    si = next_stg()
                    sg = stg[si]
                    wfree = state["stg_free"][si]
                    if n in (2, 4):
                        gi = 1 if n == 2 else 2
                        tq1 = P.op("act", lambda e, tb=tb, rows=rows: e.activation(
                            out=junk[0:rows, 0:512], in_=ps[tb][0:rows, 0:512], func=AF.Square),
                            waits=[tps[tb], state.get("junk_free")], sig=True)
                        tq2 = P.op("dve", lambda e, rows=rows: e.tensor_reduce(
                            out=ss4[0:rows, 0:4], in_=junk[0:rows, 0:512].rearrange("p (g h) -> p g h", g=4),
                            axis=mybir.AxisListType.X, op=ALU.add), waits=[tq1], sig=True)
                        state["junk_free"] = tq2
                        tss = P.op("act", lambda e, rows=rows: e.activation(out=ss4[0:rows, 4:8], in_=ss4[0:rows, 0:4],
                                                                            func=AF.Sqrt, bias=eps_t[0:rows, :], scale=1.0 / HD),
                                   waits=[tq2], sig=True)
                        tq4 = P.op("dve", lambda e, rows=rows: e.reciprocal(out=ss4[0:rows, 4:8], in_=ss4[0:rows, 4:8]),
                                   waits=[tss, wfree, t_hvb], sig=True)
                        state["tq4"] = tq4
                        for g in range(4):
                            P.op("dve", lambda e, tb=tb, g=g, rows=rows, sg=sg, gi=gi: e.scalar_tensor_tensor(
                                out=sg[0:rows, g * 128:(g + 1) * 128], in0=ps[tb][0:rows, g * 128:(g + 1) * 128],
                                scalar=ss4[0:rows, 4 + g:5 + g], in1=hvb[0:rows, gi, :], op0=ALU.mult, op1=ALU.mult),
                                force=[state["tq4"]] if g == 0 else [])
                        crow = cs_d[T:T + 1].partition_broadcast(rows) if False else None
                        if sample:
                            tcs = ring_in.issue("sp", lambda e, rows=rows: e.dma_start(
                                out=cs_t[0:rows].rearrange("p a b -> p (a b)"),
                                in_=cs_d[T].rearrange("a b -> (a b)").partition_broadcast(rows)),
                                waits=[state.get("cs_free")])
                        else:
                            tcs = ring_in.issue("sp", lambda e, r0=r0: e.dma_start(out=cs_t[:], in_=cs_d[r0:r0 + 128]),
                                                waits=[state.get("cs_free")])
                        sgv = sg[0:rows, :].rearrange("p (g h) -> p g h", g=4)
                        x1 = sgv[:, :, 0:64]
                        x2 = sgv[:, :, 64:128]
                        cosb = cs_t[0:rows, 0:1, :].to_broadcast([rows, 4, 64]) if hasattr(cs_t[0:rows, 0:1, :], "to_broadcast") else None
                        sinb = cs_t[0:rows, 1:2, :].to_broadcast([rows, 4, 64]) if hasattr(cs_t[0:rows, 1:2, :], "to_broadcast") else None
                        r = [rp[i][0:rows] for i in range(4)]
                        P.op("dve", lambda e, x1=x1, cosb=cosb, r=r: e.tensor_tensor(out=r[0], in0=x1, in1=cosb, op=ALU.mult),
                             waits=[tcs])
                        P.op("dve", lambda e, x2=x2, sinb=sinb, r=r: e.tensor_tensor(out=r[1], in0=x2, in1=sinb, op=ALU.mult))
                        P.op("dve", lambda e, x2=x2, cosb=cosb, r=r: e.tensor_tensor(out=r[2], in0=x2, in1=cosb, op=ALU.mult))
                        P.op("dve", lambda e, x1=x1, sinb=sinb, r=r: e.tensor_tensor(out=r[3], in0=x1, in1=sinb, op=ALU.mult))
                        P.op("dve", lambda e, x1=x1, r=r: e.tensor_tensor(out=x1, in0=r[0], in1=r[1], op=ALU.subtract))
                        tev = P.op("dve", lambda e, x2=x2, r=r: e.tensor_tensor(out=x2, in0=r[2], in1=r[3], op=ALU.add),
                                   sig=True)
                        state["cs_free"] = tev
                    else:
                        tev = P.op("act", lambda e, tb=tb, rows=rows, sg=sg: e.activation(
                            out=sg[0:rows, :], in_=ps[tb][0:rows, 0:512], func=AF.Copy), waits=[tps[tb], wfree], sig=True)
                    ps_free[tb] = tev
                    toks = []
                    if sample:
                        if n < 4:
                            toks.append(store_rows(kvs_o[n][:, :], sg[0:rows, :], [tev]))
                        else:
                            for b in range(NSMP):
                                toks.append(store_rows(wins_o[n - 4][b, WIN - 1:WIN, :], sg[b:b + 1, :], [tev]))
                    else:
                        if n < 4:
                            toks.append(store_rows(kv_o[n][r0:r0 + 128, :], sg[:, :], [tev]))
                        else:
                            toks.append(store_rows(kw_s[n - 4][r0:r0 + 128, :], sg[:, :], [tev]))
                            if r0 >= T - WIN:
                                toks.append(store_rows(win_o[n - 4][r0 - (T - WIN):r0 - (T - WIN) + 128, :], sg[:, :], [tev]))
                    if toks:
                        state["stg_free"][si] = toks[-1]
                    else:
                        state["stg_free"][si] = tev
            state["a_free"] = tp
            state["h_free"] = tp
            state["h_free2"] = tsp

        if "A" in stages:
            ntiles = NT + 1
            for ti in range(ntiles):
                if isinstance(stages, dict) and ti not in stages["A"]:
                    continue
                P.strict = {"act", "dve", "pool"} if ti == NT else set()
                stage_a_tile(ti)
            P.strict = set()
            for i, (src, dst) in enumerate(((skw_d, wins_o[0]), (svw_d, wins_o[1]))):
                for b in range(NSMP):
                    store_rows(dst[b, 0:WIN - 1, :], src[b, 1:WIN, :], [])


        if "B" in stages:
            SCALE = float(HD) ** -0.5
            NEGF = -1e30
            wqg_d = din_once("w_qg", [D, 4192])
            wo_d = din_once("w_o", [D, D])
            wc1_d = [din_once("w_cmp_k1", [4096, 256]), din_once("w_cmp_v1", [4096, 256])]
            wc2_d = [din_once("w_cmp_k2", [256, 128]), din_once("w_cmp_v2", [256, 128])]
            pe_d = [din_once("pe_cmp_k", [32, 128]), din_once("pe_cmp_v", [32, 128])]
            wqg_v = wqg_d.rearrange("(kc p) f -> p kc f", p=128)
            wo_v = wo_d.rearrange("(kc p) f -> p kc f", p=128)

            vsel_g = sb_once("vsel_g", [128, 16, 128], BF16)
            vwin_g = sb_once("vwin_g", [128, 16, 128], BF16)
            identb = sb_once("identb", [128, 128], BF16)
            tiny = sb_once("tiny", [128, 8], F32)
            bias_sb = sb_once("bias_sb", [128, 4], F32)
            gk_col = sb_once("gk_col", [128, 1], F32)
            m8a = sb_once("m8a", [128, 8], F32)
            m8b = sb_once("m8b", [128, 8], F32)

            def barrier():
                toks = [P.op("act", lambda e: e.activation(out=tiny[:, 0:1], in_=tiny[:, 0:1], func=AF.Copy), sig=True),
                        P.op("dve", lambda e: e.tensor_copy(out=tiny[:, 1:2], in_=tiny[:, 1:2]), sig=True),
                        P.op("pool", lambda e: e.memset(tiny[:, 2:3], 0.0), sig=True),
                        P.lastsig.get("pe")]
                for rg in (ring_in, ring_out, ring_pl):
                    toks += [t for t in rg.last if t is not None]
                for eng in ("pe", "act", "dve", "pool", "sp"):
                    P.op(eng, None, waits=toks)

            P.op("pool", lambda e: e.memset(tiny[:], 0.0))
            barrier()
            P.strict = {"act", "dve", "pool"}

            hf = h[:].rearrange("p a b -> p (a b)")
            hbv = hf.bitcast(BF16)
            kselT = hbv[:, 0:8192].rearrange("p (g t) -> p g t", g=4)
            kwinT = hbv[:, 8192:16384].rearrange("p (g t) -> p g t", g=4)
            maskbuf = hbv[:, 16384:24576].rearrange("p (k q) -> p k q", k=16)
            acc = hf[:, 12288:16384].rearrange("p (r q) -> p r q", r=8)
            w1_sb = hbv[:, 16384:24576].rearrange("p (r h) -> p r h", r=32)
            sT = hbv[:, 24576:25600].rearrange("p (c n) -> p c n", c=2)
            w2_sb = hbv[:, 25600:25856].rearrange("p (c h) -> p c h", c=2)
            peT = hbv[:, 25856:25888]
            sqb = hbv[:, 26112:26624]
            scAb = scA[:].rearrange("p a b -> p (a b)").bitcast(BF16)
            qT = scAb[:, 0:4096].rearrange("p (r q) -> p r q", r=8)
            qrT = scAb[:, 4096:8192].rearrange("p (r q) -> p r q", r=8)
            scBb = scB[:].rearrange("p a b -> p (a b)").bitcast(BF16)
            mc = [scBb[:, d_ * 512:(d_ + 1) * 512] for d_ in range(4)]
            mw = [scBb[:, 2048 + d_ * 512:2048 + (d_ + 1) * 512] for d_ in range(4)]
            e_sb = [scBb[:, 4096:4608], scBb[:, 4608:5120]]
            pt_sb = [scBb[:, 5120:5632], scBb[:, 5632:6144]]
            cmask = scBb[:, 6144:6656]
            pn_sb = scBb[:, 6656:7168]
            Asel = scBb[:, 7168:7200]
            Asel2 = scBb[:, 7200:7232]
            selT = scBb[:, 7232:7744]
            Eexp = gate_t[:].rearrange("p a b -> p (a b)").bitcast(BF16).rearrange("p (k m) -> p k m", k=16)
            gatesT = rt[:].bitcast(BF16)[:, 0:512]
            pTb = pT[:]
            ckT = pTb[:, 0, :]
            cv_sb = pTb[:, 1, :].rearrange("p (g h) -> p g h", g=4)
            pslc_sb = rstd
            rec_sb = stg[2]
            coef_sb = junk
            rowst = junk2

            P.op("dve", lambda e: e.tensor_copy(out=identb[:], in_=ident[:]))
            for d_ in range(4):
                P.op("pool", lambda e, d_=d_: e.memset(mc[d_], 1.0))
                P.op("pool", lambda e, d_=d_: e.affine_select(out=mc[d_], in_=mc[d_], pattern=[[1, 512]], compare_op=ALU.is_ge,
                                                              fill=freg(e, 0.0), base=-128 * d_, channel_multiplier=-1))
                P.op("pool", lambda e, d_=d_: e.memset(mw[d_], 1.0))
                P.op("pool", lambda e, d_=d_: e.affine_select(out=mw[d_], in_=mw[d_], pattern=[[-1, 512]], compare_op=ALU.is_gt,
                                                              fill=freg(e, 0.0), base=128 * d_, channel_multiplier=1))
            for (A_, lo, hi) in ((Asel, 1, 3), (Asel2, 0, 2)):
                P.op("pool", lambda e, A_=A_: e.memset(A_, 1.0))
                P.op("pool", lambda e, A_=A_, lo=lo: e.affine_select(out=A_, in_=A_, pattern=[[-4, 32]], compare_op=ALU.is_ge,
                                                                     fill=freg(e, 0.0), base=lo, channel_multiplier=1))
                P.op("pool", lambda e, A_=A_, hi=hi: e.affine_select(out=A_, in_=A_, pattern=[[4, 32]], compare_op=ALU.is_ge,
                                                                     fill=freg(e, 0.0), base=hi, channel_multiplier=-1))
            P.op("pool", lambda e: e.tensor_tensor(out=Asel, in0=Asel, in1=Asel2, op=ALU.add))
            P.op("pool", lambda e: e.memset(Eexp[0:32], 1.0))
            t_cst = P.op("pool", lambda e: e.affine_select(
                out=Eexp[0:32], in_=Eexp[0:32], pattern=[[-2, 16], [-1, 2], [0, 64]], compare_op=ALU.is_equal, fill=freg(e, 0.0),
                base=0, channel_multiplier=1), sig=True)
            t_gk = ring_in.issue("sp", lambda e: e.dma_start(out=gk_col[:], in_=hv_d[0].rearrange("(h o) -> h o", o=1)))

            def rowsT(src_fn, nblk, dst, waits=()):
                last = None
                rf = state.setdefault("rows_free", [None, None])
                for tb in range(nblk):
                    sl = tb % 2
                    tk = ring_in.issue("sp", lambda e, tb=tb, sl=sl: e.dma_start(out=rowst[:, sl, :], in_=src_fn(tb)),
                                       waits=[rf[sl]] + list(waits))
                    b = misc_bank()
                    tp = None
                    for g in range(4):
                        tp = P.op("pe", lambda e, b=b, g=g, sl=sl: e.transpose(
                            out=ps[b][:, g * 128:(g + 1) * 128], in_=rowst[:, sl, g * 128:(g + 1) * 128], identity=ident[:]),
                            waits=[tk, ps_free[b]], sig=(g == 3))
                    rf[sl] = tp
                    last = P.op("act", lambda e, b=b, tb=tb: e.activation(
                        out=dst[:, :, tb * 128:(tb + 1) * 128], in_=ps[b][:, :].rearrange("p (g t) -> p g t", g=4),
                        func=AF.Copy), waits=[tp], sig=True)
                    ps_free[b] = last
                return last

            t_rk = rowsT(lambda tb: kv_o[0][tb * 128:(tb + 1) * 128, :], 16, kselT)
            t_rv = rowsT(lambda tb: kv_o[1][tb * 128:(tb + 1) * 128, :], 16, kwinT)
            raws = [kselT, kwinT]
            t_cmp_done = None
            for kvi in range(2):
                tw1 = ring_pl.issue("pool", lambda e, kvi=kvi: e.dma_start(
                    out=w1_sb, in_=wc1_d[kvi].rearrange("(r p) h -> p r h", p=128)), waits=[t_cmp_done, t_cst])
                tw2 = ring_pl.issue("pool", lambda e, kvi=kvi: e.dma_start(
                    out=w2_sb, in_=wc2_d[kvi].rearrange("(c p) h -> p c h", p=128)), waits=[t_cmp_done])
                tk = ring_in.issue("sp", lambda e, kvi=kvi: e.dma_start(out=rowst[0:32, 0, 0:128], in_=pe_d[kvi]),
                                   waits=[state["rows_free"][0], t_cmp_done])
                b = misc_bank()
                tp = P.op("pe", lambda e, b=b: e.transpose(out=ps[b][:, 0:32], in_=rowst[0:32, 0, 0:128],
                                                            identity=ident[0:32, 0:32]), waits=[tk, ps_free[b]], sig=True)
                state["rows_free"][0] = tp
                tpe = P.op("act", lambda e, b=b: e.activation(out=peT, in_=ps[b][:, 0:32], func=AF.Copy), waits=[tp], sig=True)
                ps_free[b] = tpe
                tb_ = None
                for hc in range(2):
                    for r in range(32):
                        tb_ = P.op("pe", lambda e, hc=hc, r=r: e.matmul(
                            ps[6][:, hc:hc + 1], lhsT=w1_sb[:, r, hc * 128:(hc + 1) * 128], rhs=peT[:, r:r + 1],
                            start=(r == 0), stop=(r == 31)), waits=[tw1, tpe, ps_free[6]], sig=(r == 31))
                tbias = P.op("dve", lambda e, kvi=kvi: e.tensor_copy(out=bias_sb[:, 2 * kvi:2 * kvi + 2], in_=ps[6][:, 0:2]),
                             waits=[tb_], sig=True)
                ps_free[6] = tbias
                raw = raws[kvi]
                tsil = None
                for hc in range(2):
                    b = hc
                    tm = None
                    for r in range(32):
                        tm = P.op("pe", lambda e, b=b, hc=hc, r=r, raw=raw: e.matmul(
                            ps[b][:, 0:508].rearrange("p (g n) -> p g n", g=4), lhsT=w1_sb[:, r, hc * 128:(hc + 1) * 128],
                            rhs=raw[:, :, r:r + 16 * 126 + 1:16], start=(r == 0), stop=(r == 31)),
                            waits=[tw1, t_rk, t_rv, ps_free[b]], sig=(r == 31))
                    tsil = P.op("act", lambda e, b=b, hc=hc, kvi=kvi: e.activation(
                        out=sT[:, hc, 0:508], in_=ps[b][:, 0:508], func=AF.Silu,
                        bias=bias_sb[:, 2 * kvi + hc:2 * kvi + hc + 1]), waits=[tm, tbias], sig=True)
                    ps_free[b] = tsil
                if kvi == 0:
                    tm = None
                    for hc in range(2):
                        tm = P.op("pe", lambda e, hc=hc: e.matmul(ps[2][:, 0:508], lhsT=w2_sb[:, hc, :], rhs=sT[:, hc, 0:508],
                                                                   start=(hc == 0), stop=(hc == 1)),
                                  waits=[tsil, tw2, ps_free[2]], sig=(hc == 1))
                    tsq = P.op("act", lambda e: e.activation(out=sqb[:, 0:508], in_=ps[2][:, 0:508], func=AF.Square),
                               waits=[tm], sig=True)
                    tss = P.op("pe", lambda e: e.matmul(ps[3][:, 0:508], lhsT=ones_bf[:], rhs=sqb[:, 0:508], start=True, stop=True),
                               waits=[tsq, ps_free[3]], sig=True)
                    t1 = P.op("dve", lambda e: e.tensor_scalar(out=rec_sb[:, 0:508], in0=ps[3][:, 0:508], scalar1=1.0 / HD,
                                                               scalar2=EPS, op0=ALU.mult, op1=ALU.add), waits=[tss], sig=True)
                    ps_free[3] = t1
                    t2 = P.op("pool", lambda e: e.tensor_tensor(out=rec_sb[:, 0:508], in0=rec_sb[:, 0:508],
                                                                in1=mhalf[:, 0:1].to_broadcast([128, 508]), op=ALU.pow),
                              waits=[t1], sig=True)
                    t3_ = P.op("dve", lambda e: e.scalar_tensor_tensor(out=ckT[:, 0:508], in0=ps[2][:, 0:508], scalar=gk_col[:, 0:1],
                                                                        in1=rec_sb[:, 0:508], op0=ALU.mult, op1=ALU.mult),
                               waits=[t2, t_gk], sig=True)
                    ps_free[2] = t3_
                    t_cmp_done = t3_
                else:
                    tcv = None
                    for g in range(4):
                        tm = None
                        for hc in range(2):
                            tm = P.op("pe", lambda e, g=g, hc=hc: e.matmul(
                                ps[2][0:127, g * 128:(g + 1) * 128], lhsT=sT[:, hc, g * 127:(g + 1) * 127], rhs=w2_sb[:, hc, :],
                                start=(hc == 0), stop=(hc == 1)), waits=[tsil, tw2, ps_free[2]], sig=(hc == 1))
                    tcv = P.op("act", lambda e: e.activation(out=cv_sb[0:127], in_=ps[2][0:127, :].rearrange("p (g h) -> p g h", g=4),
                                                             func=AF.Copy), waits=[tm], sig=True)
                    ps_free[2] = tcv
                    t_cmp_done = tcv

            t_ks = rowsT(lambda tb: kv_o[2][tb * 128:(tb + 1) * 128, :], 16, kselT, waits=[t_cmp_done])
            t_kw = rowsT(lambda tb: kw_s[0][tb * 128:(tb + 1) * 128, :], 16, kwinT, waits=[t_cmp_done])
            wg_sb = hx[:].bitcast(BF16).rearrange("p (k c) -> p k c", k=KC)
            t_wg = ring_pl.issue("pool", lambda e: e.dma_start(out=wg_sb[:], in_=wqg_v[:, :, 4096:4192]))

            fin = dict(acc_free=None, e_free=[None, None], pt_free=[None, None], step=0, last=None)

            def block_step(qv, kT_ap, nk, v_ap, mask_ap, first, last, extra_waits=()):
                i = fin["step"] % 2
                fin["step"] += 1
                tS = P.op("pe", lambda e: e.matmul(ps[i][0:nk, :], lhsT=kT_ap, rhs=qv, start=True, stop=True),
                          waits=[ps_free[i]] + list(extra_waits), sig=True)
                tE = P.op("act", lambda e: e.activation(out=e_sb[i][0:nk], in_=ps[i][0:nk, :], func=AF.Exp, scale=SCALE),
                          waits=[tS, fin["e_free"][i]], sig=True)
                ps_free[i] = tE
                if mask_ap is not None:
                    eng = "pool" if (fin["step"] % 4) < 2 else "dve"
                    tM = P.op(eng, lambda e: e.tensor_tensor(out=pt_sb[i][0:nk], in0=e_sb[i][0:nk], in1=mask_ap, op=ALU.mult),
                              waits=[tE, fin["pt_free"][i]], sig=True)
                    fin["e_free"][i] = tM
                    src = pt_sb[i]
                else:
                    tM = tE
                    src = e_sb[i]
                P.op("pe", lambda e: e.matmul(ps[2][:, :], lhsT=v_ap, rhs=src[0:nk], start=first, stop=last),
                     waits=[tM] + ([ps_free[2]] if first else []))
                tD = P.op("pe", lambda e: e.matmul(ps[3][:, :], lhsT=ones_bf[0:nk, :], rhs=src[0:nk], start=first, stop=last),
                          waits=([ps_free[3]] if first else []), sig=True)
                if mask_ap is not None:
                    fin["pt_free"][i] = tD
                else:
                    fin["e_free"][i] = tD
                return tD, src, i

            def finish(tD, r, idx, first_branch, t_gT, cmp_extra=None):
                tG = P.op("pe", lambda e: e.matmul(ps[6][:, :], lhsT=identb[0:96, idx:idx + 1].to_broadcast([96, 128]),
                                                   rhs=gatesT[0:96, :], start=True, stop=True), waits=[ps_free[6], t_gT], sig=True)
                t1 = P.op("dve", lambda e: e.tensor_scalar(out=rec_sb[:, :], in0=ps[3][:, :], scalar1=1e-30, scalar2=None,
                                                           op0=ALU.add), waits=[tD], sig=True)
                ps_free[3] = t1
                t2 = P.op("dve", lambda e: e.reciprocal(out=rec_sb[:, :], in_=rec_sb[:, :]), sig=True)
                if cmp_extra is not None:
                    cmp_extra(t2)
                t3_ = P.op("dve", lambda e: e.tensor_tensor(out=coef_sb[:, :], in0=rec_sb[:, :], in1=ps[6][:, :], op=ALU.mult),
                           waits=[tG], sig=True)
                ps_free[6] = t3_
                if first_branch:
                    t4 = P.op("dve", lambda e: e.tensor_tensor(out=acc[:, r, :], in0=ps[2][:, :], in1=coef_sb[:, :], op=ALU.mult),
                              waits=[fin["acc_free"]], sig=True)
                    ps_free[2] = t4
                else:
                    t4a = P.op("dve", lambda e: e.tensor_tensor(out=coef_sb[:, :], in0=ps[2][:, :], in1=coef_sb[:, :], op=ALU.mult),
                               sig=True)
                    ps_free[2] = t4a
                    t4 = P.op("dve", lambda e: e.tensor_tensor(out=acc[:, r, :], in0=acc[:, r, :], in1=coef_sb[:, :], op=ALU.add),
                              sig=True)
                fin["last"] = t4
                return t4

            for qt in range(NT):
                ta = ring_in.issue("sp", lambda e, qt=qt: e.dma_start(
                    out=a[:, :, :], in_=a1_s[qt].rearrange("p (k t) -> p k t", k=KC)), waits=[state.get("a_free"), fin["last"]])
                t_gT = None
                for tb in range(4):
                    b = misc_bank()
                    tp = None
                    for kc in range(KC):
                        tp = P.op("pe", lambda e, b=b, kc=kc, tb=tb: e.matmul(
                            ps[b][:, 0:96], lhsT=a[:, kc, tb * 128:(tb + 1) * 128], rhs=wg_sb[:, kc, :],
                            start=(kc == 0), stop=(kc == KC - 1)), waits=[ta, t_wg, ps_free[b]], sig=(kc == KC - 1))
                    tsg = P.op("act", lambda e, b=b: e.activation(out=stg[0][:, 0:96], in_=ps[b][:, 0:96], func=AF.Sigmoid),
                               waits=[tp, state.get("stg0_free")], sig=True)
                    ps_free[b] = tsg
                    b2 = misc_bank()
                    tp2 = P.op("pe", lambda e, b2=b2: e.transpose(out=ps[b2][0:96, 0:128], in_=stg[0][:, 0:96], identity=ident[:]),
                               waits=[tsg, ps_free[b2]], sig=True)
                    state["stg0_free"] = tp2
                    t_gT = P.op("act", lambda e, b2=b2, tb=tb: e.activation(out=gatesT[0:96, tb * 128:(tb + 1) * 128],
                                                                             in_=ps[b2][0:96, 0:128], func=AF.Copy),
                                waits=[tp2, fin["last"]], sig=True)
                    ps_free[b2] = t_gT
                P.op("pool", lambda e: e.memset(cmask, 1.0), waits=[fin["last"]])
                t_cm = P.op("pool", lambda e, qt=qt: e.affine_select(out=cmask, in_=cmask, pattern=[[1, 512]], compare_op=ALU.is_ge,
                                                                     fill=freg(e, 0.0), base=512 * qt - 31, channel_multiplier=-16), sig=True)
                for g in range(4):
                    tvs = ring_pl.issue("pool", lambda e, g=g: e.dma_start(
                        out=vsel_g[:], in_=kv_o[3].rearrange("(k p) c -> p k c", p=128)[:, :, g * 128:(g + 1) * 128]),
                        waits=[fin["last"], P.lastsig.get("pe")])
                    tvw = ring_pl.issue("pool", lambda e, g=g: e.dma_start(
                        out=vwin_g[:], in_=kw_s[1].rearrange("(k p) c -> p k c", p=128)[:, :, g * 128:(g + 1) * 128]),
                        waits=[fin["last"], P.lastsig.get("pe")])
                    for cbk in range(2):
                        c0 = g * 1024 + cbk * 512
                        halves = [wload(wqg_v[:, hf_ * 16:(hf_ + 1) * 16, c0:c0 + 512], 16, 512, [ta]) for hf_ in range(2)]
                        tps = []
                        for hf_ in range(2):
                            slot, wv, tk = halves[hf_]
                            tp = None
                            for tb in range(4):
                                bq = 4 + (tb % 2) if False else tb
                                for k in range(16):
                                    kk = hf_ * 16 + k
                                    tp = P.op("pe", lambda e, tb=tb, wv=wv, k=k, kk=kk: e.matmul(
                                        ps[tb][:, 0:512], lhsT=a[:, kk, tb * 128:(tb + 1) * 128], rhs=wv[:, k, :],
                                        start=(kk == 0), stop=(kk == KC - 1)), waits=[tk, ta, ps_free[tb]], sig=(k == 15))
                                if hf_ == 1:
                                    tps.append(tp)
                            wb_free[slot] = tp
                        for tb in range(4):
                            qn = stg[0]
                            qr = stg[1]
                            tq1 = P.op("act", lambda e, tb=tb: e.activation(out=junk[:, 0:512], in_=ps[tb][:, 0:512], func=AF.Square),
                                       waits=[tps[tb], fin["last"]], sig=True)
                            P.op("dve", lambda e: e.tensor_reduce(out=ss4[:, 0:4], in_=junk[:, 0:512].rearrange("p (g h) -> p g h", g=4),
                                                                  axis=mybir.AxisListType.X, op=ALU.add), waits=[tq1])
                            tq3 = P.op("dve", lambda e: e.tensor_scalar(out=ss4[:, 0:4], in0=ss4[:, 0:4], scalar1=1.0 / HD, scalar2=EPS,
                                                                        op0=ALU.mult, op1=ALU.add), sig=True)
                            tss = P.op("act", lambda e: e.activation(out=ss4[:, 4:8], in_=ss4[:, 0:4], func=AF.Sqrt), waits=[tq3], sig=True)
                            P.op("dve", lambda e: e.reciprocal(out=ss4[:, 4:8], in_=ss4[:, 4:8]), waits=[tss, state.get("stg0_free"),
                                                                                                         state.get("stg1_free")])
                            for hh in range(4):
                                P.op("dve", lambda e, tb=tb, hh=hh: e.scalar_tensor_tensor(
                                    out=qn[:, hh * 128:(hh + 1) * 128], in0=ps[tb][:, hh * 128:(hh + 1) * 128],
                                    scalar=ss4[:, 4 + hh:5 + hh], in1=hvb[:, 3, :], op0=ALU.mult, op1=ALU.mult))
                            tqn = P.op("dve", lambda e: e.tensor_copy(out=tiny[:, 3:4], in_=tiny[:, 3:4]), sig=True)
                            ps_free[tb] = tqn
                            r0 = qt * TW + tb * 128
                            tcs = ring_in.issue("sp", lambda e, r0=r0: e.dma_start(out=cs_t[:], in_=cs_d[r0:r0 + 128]),
                                                waits=[state.get("cs_free")])
                            qnv = qn[:, :].rearrange("p (g h) -> p g h", g=4)
                            qrv = qr[:, :].rearrange("p (g h) -> p g h", g=4)
                            cosb = cs_t[:, 0:1, :].to_broadcast([128, 4, 64])
                            sinb = cs_t[:, 1:2, :].to_broadcast([128, 4, 64])
                            P.op("dve", lambda e, qnv=qnv, cosb=cosb: e.tensor_tensor(out=rp[0][:], in0=qnv[:, :, 0:64], in1=cosb, op=ALU.mult),
                                 waits=[tcs])
                            P.op("dve", lambda e, qnv=qnv, sinb=sinb: e.tensor_tensor(out=rp[1][:], in0=qnv[:, :, 64:128], in1=sinb, op=ALU.mult))
                            P.op("dve", lambda e, qnv=qnv, cosb=cosb: e.tensor_tensor(out=rp[2][:], in0=qnv[:, :, 64:128], in1=cosb, op=ALU.mult))
                            P.op("dve", lambda e, qnv=qnv, sinb=sinb: e.tensor_tensor(out=rp[3][:], in0=qnv[:, :, 0:64], in1=sinb, op=ALU.mult))
                            P.op("dve", lambda e, qrv=qrv: e.tensor_tensor(out=qrv[:, :, 0:64], in0=rp[0][:], in1=rp[1][:], op=ALU.subtract))
                            tqr = P.op("dve", lambda e, qrv=qrv: e.tensor_tensor(out=qrv[:, :, 64:128], in0=rp[2][:], in1=rp[3][:], op=ALU.add),
                                       sig=True)
                            state["cs_free"] = tqr
                            for (src_, dstT, key) in ((qn, qT, "stg0_free"), (qr, qrT, "stg1_free")):
                                b = misc_bank()
                                tp = None
                                for hh in range(4):
                                    tp = P.op("pe", lambda e, b=b, hh=hh, src_=src_: e.transpose(
                                        out=ps[b][:, hh * 128:(hh + 1) * 128], in_=src_[:, hh * 128:(hh + 1) * 128], identity=ident[:]),
                                        waits=[tqr, ps_free[b]], sig=(hh == 3))
                                state[key] = tp
                                tev = P.op("act", lambda e, b=b, dstT=dstT, cbk=cbk, tb=tb: e.activation(
                                    out=dstT[:, cbk * 4:cbk * 4 + 4, tb * 128:(tb + 1) * 128],
                                    in_=ps[b][:, :].rearrange("p (r t) -> p r t", r=4), func=AF.Copy),
                                    waits=[tp, fin["last"]], sig=True)
                                ps_free[b] = tev
                                state["q_ready"] = tev
                    tq = state["q_ready"]
                    for r in range(8):
                        tD, src, i = block_step(qT[:, r, :], ckT[:, g * 127:(g + 1) * 127], 127, cv_sb[0:127, g, :], cmask[0:127, :],
                                                True, True, extra_waits=[tq, t_cm, t_cmp_done, t_ks, t_kw])

                        def cmp_extra(t2, r=r, src=src, i=i):
                            tpn = P.op("dve", lambda e: e.tensor_tensor(out=pn_sb[0:127, :], in0=src[0:127, :], in1=rec_sb[0:127, :],
                                                                         op=ALU.mult), waits=[fin.get("pn_free")], sig=True)
                            tps_ = P.op("pe", lambda e: e.matmul(ps[7][0:32, :], lhsT=Asel[0:127, 0:32], rhs=pn_sb[0:127, :],
                                                                 start=(r == 0), stop=(r == 7)),
                                        waits=[tpn, t_cst] + ([ps_free[7]] if r == 0 else []), sig=True)
                            fin["pn_free"] = tps_
                            fin["pt_free"][i] = tps_
                            fin["slc_done"] = tps_
                        finish(tD, r, g * 24 + r * 3 + 0, True, t_gT, cmp_extra)
                    tcp = P.op("act", lambda e: e.activation(out=pslc_sb[0:32, :], in_=ps[7][0:32, :], func=AF.Copy),
                               waits=[fin["slc_done"]], sig=True)
                    ps_free[7] = tcp
                    tsel = None
                    for tb in range(4):
                        pos0 = qt * TW + tb * 128
                        c0b = pos0 // 64
                        b = misc_bank()
                        tp = P.op("pe", lambda e, b=b, tb=tb: e.transpose(out=ps[b][:, 0:32], in_=pslc_sb[0:32, tb * 128:(tb + 1) * 128],
                                                                            identity=ident[0:32, 0:32]), waits=[tcp, ps_free[b]], sig=True)
                        sc = stg[0][:, 0:32]
                        sc2 = stg[0][:, 32:64]
                        s1 = stg[0][:, 64:96]
                        selq = stg[0][:, 96:128]
                        tsc = P.op("act", lambda e, b=b, sc=sc: e.activation(out=sc, in_=ps[b][:, 0:32], func=AF.Copy),
                                   waits=[tp, state.get("stg0_free")], sig=True)
                        ps_free[b] = tsc
                        P.op("pool", lambda e, sc=sc, pos0=pos0: e.affine_select(out=sc, in_=sc, pattern=[[-64, 32]], compare_op=ALU.is_ge,
                                                                                 fill=freg(e, NEGF), base=pos0, channel_multiplier=1), waits=[tsc])
                        P.op("pool", lambda e, sc=sc, c0b=c0b: e.memset(sc[0:64, c0b:c0b + 1], 2e9))
                        if c0b >= 1:
                            P.op("pool", lambda e, sc=sc, c0b=c0b: e.memset(sc[0:64, c0b - 1:c0b], 1e9))
                        P.op("pool", lambda e, sc=sc, c0b=c0b: e.memset(sc[64:128, c0b + 1:c0b + 2], 2e9))
                        P.op("pool", lambda e, sc=sc, c0b=c0b: e.memset(sc[64:128, c0b:c0b + 1], 1e9))
                        tfz = P.op("pool", lambda e, sc=sc: e.memset(sc[:, 0:1], 3e9), sig=True)
                        P.op("dve", lambda e, sc=sc: e.max(out=m8a[:], in_=sc), waits=[tfz])
                        P.op("dve", lambda e, sc=sc, sc2=sc2: e.match_replace(out=sc2, in_to_replace=m8a[:], in_values=sc, imm_value=-3e38))
                        P.op("dve", lambda e, sc2=sc2: e.max(out=m8b[:], in_=sc2))
                        P.op("dve", lambda e, sc=sc, s1=s1: e.tensor_scalar(out=s1, in0=sc, scalar1=m8b[:, 7:8], scalar2=None, op0=ALU.is_ge))
                        tsq_ = P.op("dve", lambda e, sc=sc, s1=s1, selq=selq: e.scalar_tensor_tensor(
                            out=selq, in0=sc, scalar=-5e29, in1=s1, op0=ALU.is_gt, op1=ALU.mult), sig=True)
                        b2 = misc_bank()
                        tp2 = P.op("pe", lambda e, b2=b2, selq=selq: e.transpose(out=ps[b2][0:32, 0:128], in_=selq, identity=ident[:]),
                                   waits=[tsq_, ps_free[b2]], sig=True)
                        state["stg0_free"] = tp2
                        tsel = P.op("act", lambda e, b2=b2, tb=tb: e.activation(out=selT[0:32, tb * 128:(tb + 1) * 128],
                                                                                 in_=ps[b2][0:32, 0:128], func=AF.Copy),
                                    waits=[tp2, fin.get("selT_free")], sig=True)
                        ps_free[b2] = tsel
                    nkb = 4 * qt + 4
                    tmk = None
                    for kb in range(nkb):
                        tx = P.op("pe", lambda e, kb=kb: e.matmul(ps[6][:, :], lhsT=Eexp[0:32, kb, :], rhs=selT[0:32, :],
                                                                  start=True, stop=True), waits=[tsel, ps_free[6], t_cst], sig=True)
                        if kb >= 4 * qt:
                            tmk = P.op("dve", lambda e, kb=kb, qt=qt: e.tensor_tensor(out=maskbuf[:, kb, :], in0=ps[6][:, :],
                                                                                      in1=mc[kb - 4 * qt], op=ALU.mult),
                                       waits=[tx, fin["last"]], sig=True)
                        else:
                            tmk = P.op("act", lambda e, kb=kb: e.activation(out=maskbuf[:, kb, :], in_=ps[6][:, :], func=AF.Copy),
                                       waits=[tx, fin["last"]], sig=True)
                        ps_free[6] = tmk
                        fin["mk_last"] = tmk
                    fin["selT_free"] = tx
                    tmk_all = [P.lastsig.get("dve"), P.lastsig.get("act")]
                    for r in range(8):
                        tD = None
                        for kb in range(nkb):
                            tD, _, _ = block_step(qrT[:, r, :], kselT[:, g, kb * 128:(kb + 1) * 128], 128, vsel_g[:, kb, :],
                                                  maskbuf[:, kb, :], kb == 0, kb == nkb - 1, extra_waits=tmk_all + [tvs, tq])
                        finish(tD, r, g * 24 + r * 3 + 1, False, t_gT)
                    kb0 = max(0, 4 * qt - 4)
                    for r in range(8):
                        tD = None
                        for kb in range(kb0, nkb):
                            m_ap = mc[kb - 4 * qt] if kb >= 4 * qt else mw[kb - 4 * qt + 4]
                            tD, _, _ = block_step(qrT[:, r, :], kwinT[:, g, kb * 128:(kb + 1) * 128], 128, vwin_g[:, kb, :],
                                                  m_ap, kb == kb0, kb == nkb - 1, extra_waits=[tvw, tq])
                        finish(tD, r, g * 24 + r * 3 + 2, False, t_gT)
                    to = ring_pl.issue("pool", lambda e, g=g, qt=qt: e.dma_start(
                        out=oT_s[qt].rearrange("p (k t) -> p k t", k=KC)[:, 8 * g:8 * g + 8, :], in_=acc), waits=[fin["last"]])
                    fin["acc_free"] = to
                state["a_free"] = P.lastsig.get("pe")


            P.strict = set()
            barrier()
            P.strict = {"act", "dve", "pool"}
            pt_d = din_once("ptab", [NSEQ, NSMP, 128], I32)[sq]
            cache_d = [din_once("cache%d" % n, [1280 * 128, 512]) for n in range(4)]
            zeros_bf = sb_once("zeros_bf", [128, 128], BF16)
            acc_s = sb_once("acc_s", [128, NSMP, 32], F32)
            vsb = vsel_g[:].rearrange("p a b -> p (a b)")
            vwb = vwin_g[:].rearrange("p a b -> p (a b)")
            maskT = vsb[:, 0:516].rearrange("p (t g) -> p t g", g=4)
            PTall = vsb[:, 516:772].rearrange("p (t c) -> p t c", t=8)
            pn_all = vsb[:, 772:1028].rearrange("p (t c) -> p t c", t=8)
            pcol_bf = vsb[:, 1028:1060].rearrange("p (t g) -> p t g", t=8)
            A128 = vsb[:, 1060:1092]
            A128b = vsb[:, 1092:1124]
            Aprev = vsb[:, 1124:1156]
            qT_s = vsb[:, 1156:1220].rearrange("p (b c) -> p b c", b=NSMP)
            qrT_s = vsb[:, 1220:1284].rearrange("p (b c) -> p b c", b=NSMP)
            gs2 = vsb[:, 1284:1380]
            sel_bf = vsb[:, 1380:1640]
            es_sb = [vsb[:, 1640:1672], vsb[:, 1672:1704]]
            pts_sb = [vsb[:, 1704:1736], vsb[:, 1736:1768]]
            vwf = vwb.bitcast(F32)
            idx_all = vwb.bitcast(I32)[:, 0:256].rearrange("p (b g) -> p b g", b=NSMP)
            Grep = vwf[:, 256:448].rearrange("p (b c) -> p b c", b=NSMP)
            rec_s = vwf[:, 448:480]
            coef_s = vwf[:, 480:512]
            pcol_f = vwf[:, 512:544].rearrange("p (t g) -> p t g", t=8)
            sc_s = vwf[:, 544:808]
            sc2_s = stg[1][:, 0:264]
            pcolp = tiny[:, 4:5]
            Vp = [scBb[:, 4096:4608], scBb[:, 4608:5120]]
            kTp = [scBb[:, 5120:5632].rearrange("p (g t) -> p g t", g=4), scBb[:, 5632:6144].rearrange("p (g t) -> p g t", g=4)]
            ckT_s = hbv[:, 26624:30720].rearrange("p (g t) -> p g t", g=4)
            cv_s = scAb[:, 0:4096].rearrange("p (t g h) -> p t g h", t=8, g=4)
            rawk = hbv[:, 0:4608].rearrange("p (g t) -> p g t", g=4)
            rawv = hbv[:, 8192:12800].rearrange("p (g t) -> p g t", g=4)

            P.op("pool", lambda e: e.memset(zeros_bf[:], 0.0))
            for (A_, lo, hi) in ((A128, 1, 3), (A128b, 0, 2)):
                P.op("pool", lambda e, A_=A_: e.memset(A_, 1.0))
                P.op("pool", lambda e, A_=A_, lo=lo: e.affine_select(out=A_, in_=A_, pattern=[[-4, 32]], compare_op=ALU.is_ge,
                                                                     fill=freg(e, 0.0), base=lo, channel_multiplier=1))
                P.op("pool", lambda e, A_=A_, hi=hi: e.affine_select(out=A_, in_=A_, pattern=[[4, 32]], compare_op=ALU.is_ge,
                                                                     fill=freg(e, 0.0), base=hi, channel_multiplier=-1))
            P.op("pool", lambda e: e.tensor_tensor(out=A128, in0=A128, in1=A128b, op=ALU.add))
            P.op("pool", lambda e: e.memset(Aprev, 1.0))
            P.op("pool", lambda e: e.affine_select(out=Aprev, in_=Aprev, pattern=[[128, 32]], compare_op=ALU.is_equal,
                                                   fill=freg(e, 0.0), base=-127, channel_multiplier=1))
            P.op("pool", lambda e: e.iota(out=pcolp, pattern=[[0, 1]], base=0, channel_multiplier=1,
                                          allow_small_or_imprecise_dtypes=True))
            t_sc = P.op("pool", lambda e: e.memset(tiny[:, 5:6], 0.0), sig=True)
            t_pt = ring_in.issue("sp", lambda e: e.dma_start(out=idx_all.rearrange("p b g -> p (b g)"),
                                                             in_=pt_d.rearrange("b g -> (b g)").partition_broadcast(128)))
            t_idx = P.op("dve", lambda e: e.tensor_scalar(out=idx_all, in0=idx_all, scalar1=128.0, scalar2=pcolp,
                                                          op0=ALU.mult, op1=ALU.add), waits=[t_pt, t_sc], sig=True)

            def page_rows(n, b, pg, dst, waits):
                return ring_pl.issue("pool", lambda e: e.indirect_dma_start(
                    out=dst, out_offset=None, in_=cache_d[n][:, :],
                    in_offset=bass.IndirectOffsetOnAxis(ap=idx_all[:, b, pg:pg + 1], axis=0)), waits=[t_idx] + list(waits))

            ta = ring_in.issue("sp", lambda e: e.dma_start(
                out=a[:, :, 0:NSMP], in_=a1_s[NT].rearrange("p (k t) -> p k t", k=KC)[:, :, 0:NSMP]),
                waits=[P.lastsig.get("pe")])
            tcs = ring_in.issue("sp", lambda e: e.dma_start(
                out=cs_t[0:NSMP].rearrange("p a b -> p (a b)"),
                in_=cs_d[T].rearrange("a b -> (a b)").partition_broadcast(NSMP)), waits=[P.lastsig.get("dve")])
            for cbk in range(8):
                c0 = cbk * 512
                halves = [wload(wqg_v[:, hf_ * 16:(hf_ + 1) * 16, c0:c0 + 512], 16, 512, [ta]) for hf_ in range(2)]
                bq = cbk % 2
                tp = None
                for hf_ in range(2):
                    slot, wv, tk = halves[hf_]
                    for k in range(16):
                        kk = hf_ * 16 + k
                        tp = P.op("pe", lambda e, bq=bq, wv=wv, k=k, kk=kk: e.matmul(
                            ps[bq][0:NSMP, 0:512], lhsT=a[:, kk, 0:NSMP], rhs=wv[:, k, :],
                            start=(kk == 0), stop=(kk == KC - 1)), waits=[tk, ta, ps_free[bq]], sig=(k == 15))
                    wb_free[slot] = tp
                qn = stg[0]
                qr = stg[1]
                tq1 = P.op("act", lambda e, bq=bq: e.activation(out=junk[0:NSMP, 0:512], in_=ps[bq][0:NSMP, 0:512], func=AF.Square),
                           waits=[tp], sig=True)
                P.op("dve", lambda e: e.tensor_reduce(out=ss4[0:NSMP, 0:4], in_=junk[0:NSMP, 0:512].rearrange("p (g h) -> p g h", g=4),
                                                      axis=mybir.AxisListType.X, op=ALU.add), waits=[tq1])
                tq3 = P.op("dve", lambda e: e.tensor_scalar(out=ss4[0:NSMP, 0:4], in0=ss4[0:NSMP, 0:4], scalar1=1.0 / HD, scalar2=EPS,
                                                            op0=ALU.mult, op1=ALU.add), sig=True)
                tss = P.op("act", lambda e: e.activation(out=ss4[0:NSMP, 4:8], in_=ss4[0:NSMP, 0:4], func=AF.Sqrt), waits=[tq3], sig=True)
                P.op("dve", lambda e: e.reciprocal(out=ss4[0:NSMP, 4:8], in_=ss4[0:NSMP, 4:8]),
                     waits=[tss, state.get("stg0_free"), state.get("stg1_free")])
                for hh in range(4):
                    P.op("dve", lambda e, bq=bq, hh=hh: e.scalar_tensor_tensor(
                        out=qn[0:NSMP, hh * 128:(hh + 1) * 128], in0=ps[bq][0:NSMP, hh * 128:(hh + 1) * 128],
                        scalar=ss4[0:NSMP, 4 + hh:5 + hh], in1=hvb[0:NSMP, 3, :], op0=ALU.mult, op1=ALU.mult))
                tqn = P.op("dve", lambda e: e.tensor_copy(out=tiny[:, 3:4], in_=tiny[:, 3:4]), sig=True)
                ps_free[bq] = tqn
                qnv = qn[0:NSMP, :].rearrange("p (g h) -> p g h", g=4)
                qrv = qr[0:NSMP, :].rearrange("p (g h) -> p g h", g=4)
                cosb = cs_t[0:NSMP, 0:1, :].to_broadcast([NSMP, 4, 64])
                sinb = cs_t[0:NSMP, 1:2, :].to_broadcast([NSMP, 4, 64])
                rr = [rp[i_][0:NSMP] for i_ in range(4)]
                P.op("dve", lambda e, qnv=qnv, cosb=cosb, rr=rr: e.tensor_tensor(out=rr[0], in0=qnv[:, :, 0:64], in1=cosb, op=ALU.mult), waits=[tcs])
                P.op("dve", lambda e, qnv=qnv, sinb=sinb, rr=rr: e.tensor_tensor(out=rr[1], in0=qnv[:, :, 64:128], in1=sinb, op=ALU.mult))
                P.op("dve", lambda e, qnv=qnv, cosb=cosb, rr=rr: e.tensor_tensor(out=rr[2], in0=qnv[:, :, 64:128], in1=cosb, op=ALU.mult))
                P.op("dve", lambda e, qnv=qnv, sinb=sinb, rr=rr: e.tensor_tensor(out=rr[3], in0=qnv[:, :, 0:64], in1=sinb, op=ALU.mult))
                P.op("dve", lambda e, qrv=qrv, rr=rr: e.tensor_tensor(out=qrv[:, :, 0:64], in0=rr[0], in1=rr[1], op=ALU.subtract))
                tqr = P.op("dve", lambda e, qrv=qrv, rr=rr: e.tensor_tensor(out=qrv[:, :, 64:128], in0=rr[2], in1=rr[3], op=ALU.add), sig=True)
                for (src_, dstT, key) in ((qn, qT_s, "stg0_free"), (qr, qrT_s, "stg1_free")):
                    b_ = misc_bank()
                    tpx = None
                    for hh in range(4):
                        tpx = P.op("pe", lambda e, b_=b_, hh=hh, src_=src_: e.transpose(
                            out=ps[b_][:, hh * NSMP:(hh + 1) * NSMP], in_=src_[0:NSMP, hh * 128:(hh + 1) * 128],
                            identity=ident[0:NSMP, 0:NSMP]), waits=[tqr, ps_free[b_]], sig=(hh == 3))
                    state[key] = tpx
                    tev = P.op("act", lambda e, b_=b_, dstT=dstT, cbk=cbk: e.activation(
                        out=dstT[:, :, cbk * 4:cbk * 4 + 4], in_=ps[b_][:, 0:4 * NSMP].rearrange("p (h b) -> p b h", b=NSMP),
                        func=AF.Copy), waits=[tpx], sig=True)
                    ps_free[b_] = tev
            b_ = misc_bank()
            tp = None
            for kc in range(KC):
                tp = P.op("pe", lambda e, b_=b_, kc=kc: e.matmul(ps[b_][0:NSMP, 0:96], lhsT=a[:, kc, 0:NSMP], rhs=wg_sb[:, kc, :],
                                                                 start=(kc == 0), stop=(kc == KC - 1)),
                          waits=[ta, t_wg, ps_free[b_]], sig=(kc == KC - 1))
            tsg = P.op("act", lambda e, b_=b_: e.activation(out=gs2[0:NSMP, 0:96], in_=ps[b_][0:NSMP, 0:96], func=AF.Sigmoid),
                       waits=[tp], sig=True)
            ps_free[b_] = tsg
            state["a_free"] = tp
            tgr = None
            for b in range(NSMP):
                tg_ = P.op("pe", lambda e, b=b: e.matmul(ps[6][:, 0:96], lhsT=identb[0:NSMP, b:b + 1].to_broadcast([NSMP, 128]),
                                                         rhs=gs2[0:NSMP, 0:96], start=True, stop=True), waits=[tsg, ps_free[6]], sig=True)
                tgr = P.op("act", lambda e, b=b: e.activation(out=Grep[:, b, :], in_=ps[6][:, 0:96], func=AF.Copy), waits=[tg_], sig=True)
                ps_free[6] = tgr

            sfin = dict(step=0, e_free=[None, None], pt_free=[None, None])

            def s_init():
                return P.op("pe", lambda e: e.matmul(ps[2][:, 0:64], lhsT=zeros_bf[:, :], rhs=zeros_bf[:, 0:64], start=True, stop=False),
                            waits=[ps_free[2]], sig=True)

            def s_step(b, kT_fn, nk, v_fn, qsrc, mask_ap, e_dst=None, waits=()):
                i = sfin["step"] % 2
                sfin["step"] += 1
                tS = None
                for g in range(4):
                    tS = P.op("pe", lambda e, g=g: e.matmul(ps[i][0:nk, g * 8:(g + 1) * 8], lhsT=kT_fn(g), rhs=qsrc[:, b, g * 8:(g + 1) * 8],
                                                            start=True, stop=True), waits=[ps_free[i]] + list(waits), sig=(g == 3))
                ed = e_dst if e_dst is not None else es_sb[i]
                tE = P.op("act", lambda e: e.activation(out=ed[0:nk], in_=ps[i][0:nk, 0:32], func=AF.Exp, scale=SCALE),
                          waits=[tS, sfin["e_free"][i]], sig=True)
                ps_free[i] = tE
                if mask_ap is not None:
                    tM = P.op("dve", lambda e: e.tensor_tensor(
                        out=pts_sb[i][0:nk].rearrange("p (g r) -> p g r", g=4), in0=ed[0:nk].rearrange("p (g r) -> p g r", g=4),
                        in1=mask_ap, op=ALU.mult), waits=[tE, sfin["pt_free"][i]], sig=True)
                    src = pts_sb[i]
                    sfin["e_free"][i] = tM
                else:
                    tM = tE
                    src = ed
                tO = None
                for g in range(4):
                    P.op("pe", lambda e, g=g: e.matmul(ps[2][:, g * 8:(g + 1) * 8], lhsT=v_fn(g), rhs=src[0:nk, g * 8:(g + 1) * 8],
                                                       start=False, stop=False, skip_group_check=True), waits=[tM])
                tO = P.op("pe", lambda e: e.matmul(ps[2][:, 32:64], lhsT=ones_bf[0:nk, :], rhs=src[0:nk, 0:32],
                                                   start=False, stop=False, skip_group_check=True), sig=True)
                if mask_ap is not None:
                    sfin["pt_free"][i] = tO
                elif e_dst is None:
                    sfin["e_free"][i] = tO
                return tO

            def s_finish(b, tO, br, first):
                t1 = P.op("dve", lambda e: e.tensor_scalar(out=rec_s, in0=ps[2][:, 32:64], scalar1=1e-30, scalar2=None, op0=ALU.add),
                          waits=[tO], sig=True)
                P.op("dve", lambda e: e.reciprocal(out=rec_s, in_=rec_s))
                P.op("dve", lambda e: e.tensor_tensor(out=coef_s, in0=rec_s, in1=Grep[:, b, br:96:3], op=ALU.mult), waits=[tgr])
                if first:
                    t4 = P.op("dve", lambda e: e.tensor_tensor(out=acc_s[:, b, :], in0=ps[2][:, 0:32], in1=coef_s, op=ALU.mult), sig=True)
                else:
                    P.op("dve", lambda e: e.tensor_tensor(out=coef_s, in0=ps[2][:, 0:32], in1=coef_s, op=ALU.mult))
                    t4 = P.op("dve", lambda e: e.tensor_tensor(out=acc_s[:, b, :], in0=acc_s[:, b, :], in1=coef_s, op=ALU.add), sig=True)
                ps_free[2] = t4
                return t4

            e0col = ident[:, 0:1]
            for b in range(NSMP):
                t_seg_free = None
                for kvi in range(2):
                    tw1 = ring_pl.issue("pool", lambda e, kvi=kvi: e.dma_start(
                        out=w1_sb, in_=wc1_d[kvi].rearrange("(r p) h -> p r h", p=128)), waits=[P.lastsig.get("pe")])
                    tw2 = ring_pl.issue("pool", lambda e, kvi=kvi: e.dma_start(
                        out=w2_sb, in_=wc2_d[kvi].rearrange("(c p) h -> p c h", p=128)), waits=[P.lastsig.get("pe")])
                    raw = rawk if kvi == 0 else rawv
                    for sg_ in range(16):
                        nblk = 64 if sg_ < 15 else 63
                        npg = 9 if sg_ < 15 else 8
                        tl = None
                        rf = state["rows_free"]
                        for pg_ in range(npg):
                            sl = pg_ % 2
                            tk = page_rows(kvi, b, 8 * sg_ + pg_, rowst[:, sl, :], [rf[sl], t_seg_free])
                            b_ = misc_bank()
                            tpx = None
                            for g in range(4):
                                tpx = P.op("pe", lambda e, b_=b_, g=g, sl=sl: e.transpose(
                                    out=ps[b_][:, g * 128:(g + 1) * 128], in_=rowst[:, sl, g * 128:(g + 1) * 128], identity=ident[:]),
                                    waits=[tk, ps_free[b_]], sig=(g == 3))
                            rf[sl] = tpx
                            tl = P.op("act", lambda e, b_=b_, pg_=pg_, raw=raw: e.activation(
                                out=raw[:, :, pg_ * 128:(pg_ + 1) * 128], in_=ps[b_][:, :].rearrange("p (g t) -> p g t", g=4),
                                func=AF.Copy), waits=[tpx], sig=True)
                            ps_free[b_] = tl
                        tsil = None
                        for hc in range(2):
                            bb = hc
                            tm = None
                            for r in range(32):
                                tm = P.op("pe", lambda e, bb=bb, hc=hc, r=r, raw=raw, nblk=nblk: e.matmul(
                                    ps[bb][:, 0:4 * nblk].rearrange("p (g n) -> p g n", g=4), lhsT=w1_sb[:, r, hc * 128:(hc + 1) * 128],
                                    rhs=raw[:, :, r:r + 16 * (nblk - 1) + 1:16], start=(r == 0), stop=(r == 31)),
                                    waits=[tw1, tl, ps_free[bb]], sig=(r == 31))
                            tsil = P.op("act", lambda e, bb=bb, hc=hc, kvi=kvi, nblk=nblk: e.activation(
                                out=sT[:, hc, 0:4 * nblk], in_=ps[bb][:, 0:4 * nblk], func=AF.Silu,
                                bias=bias_sb[:, 2 * kvi + hc:2 * kvi + hc + 1]), waits=[tm, state.get("sT_free")], sig=True)
                            ps_free[bb] = tsil
                        t_seg_free = tm
                        if kvi == 0:
                            tm = None
                            for hc in range(2):
                                tm = P.op("pe", lambda e, hc=hc, nblk=nblk: e.matmul(ps[3][:, 0:4 * nblk], lhsT=w2_sb[:, hc, :],
                                                                                     rhs=sT[:, hc, 0:4 * nblk], start=(hc == 0), stop=(hc == 1)),
                                          waits=[tsil, tw2, ps_free[3]], sig=(hc == 1))
                            state["sT_free"] = tm
                            tsq = P.op("act", lambda e, nblk=nblk: e.activation(out=sqb[:, 0:4 * nblk], in_=ps[3][:, 0:4 * nblk], func=AF.Square),
                                       waits=[tm], sig=True)
                            tss = P.op("pe", lambda e, nblk=nblk: e.matmul(ps[6][:, 0:4 * nblk], lhsT=ones_bf[:], rhs=sqb[:, 0:4 * nblk],
                                                                            start=True, stop=True), waits=[tsq, ps_free[6]], sig=True)
                            t1 = P.op("dve", lambda e, nblk=nblk: e.tensor_scalar(out=rec_sb[:, 0:4 * nblk], in0=ps[6][:, 0:4 * nblk],
                                                                                  scalar1=1.0 / HD, scalar2=EPS, op0=ALU.mult, op1=ALU.add),
                                      waits=[tss], sig=True)
                            ps_free[6] = t1
                            t2 = P.op("pool", lambda e, nblk=nblk: e.tensor_tensor(out=rec_sb[:, 0:4 * nblk], in0=rec_sb[:, 0:4 * nblk],
                                                                                   in1=mhalf[:, 0:1].to_broadcast([128, 4 * nblk]), op=ALU.pow),
                                      waits=[t1], sig=True)
                            t3_ = P.op("dve", lambda e, nblk=nblk, sg_=sg_: e.scalar_tensor_tensor(
                                out=ckT_s[:, :, sg_ * 64:sg_ * 64 + nblk], in0=ps[3][:, 0:4 * nblk].rearrange("p (g n) -> p g n", g=4),
                                scalar=gk_col[:, 0:1], in1=rec_sb[:, 0:4 * nblk].rearrange("p (g n) -> p g n", g=4),
                                op0=ALU.mult, op1=ALU.mult), waits=[t2], sig=True)
                            ps_free[3] = t3_
                        else:
                            tm = None
                            p0 = (sg_ % 2) * 64
                            for g in range(4):
                                for hc in range(2):
                                    tm = P.op("pe", lambda e, g=g, hc=hc, nblk=nblk, p0=p0: e.matmul(
                                        ps[3][p0:p0 + nblk, g * 128:(g + 1) * 128], lhsT=sT[:, hc, g * nblk:(g + 1) * nblk], rhs=w2_sb[:, hc, :],
                                        start=(hc == 0), stop=(hc == 1)), waits=[tsil, tw2, ps_free[3]], sig=(hc == 1))
                            state["sT_free"] = tm
                            tcv = P.op("act", lambda e, sg_=sg_, nblk=nblk, p0=p0: e.activation(
                                out=cv_s[p0:p0 + nblk, sg_ // 2, :, :], in_=ps[3][p0:p0 + nblk, :].rearrange("p (g h) -> p g h", g=4),
                                func=AF.Copy), waits=[tm], sig=True)
                            ps_free[3] = tcv
                t_cmp_s = [P.lastsig.get("dve"), P.lastsig.get("act")]
                P.op("dve", lambda e: e.memset(PTall, 0.0) if hasattr(e, "memset") else None) if False else None
                tz = P.op("pool", lambda e: e.memset(PTall, 0.0), waits=[P.lastsig.get("pe"), P.lastsig.get("dve")], sig=True)
                s_init()
                tO = None
                for t_ in range(8):
                    nk = 128 if t_ < 7 else 127
                    tO = s_step(b, lambda g, t_=t_, nk=nk: ckT_s[:, g, t_ * 128:t_ * 128 + nk], nk,
                                lambda g, t_=t_, nk=nk: cv_s[0:nk, t_, g, :], qT_s, None, e_dst=PTall[:, t_, :],
                                waits=t_cmp_s + [tz])
                s_finish(b, tO, 0, True)
                P.op("dve", lambda e: e.tensor_tensor(out=pn_all, in0=PTall, in1=rec_s.unsqueeze(1).to_broadcast([128, 8, 32]), op=ALU.mult))
                P.op("dve", lambda e: e.tensor_reduce(out=pcol_f, in_=pn_all.rearrange("p t (g r) -> p t g r", g=4),
                                                      axis=mybir.AxisListType.X, op=ALU.add))
                tpc = P.op("dve", lambda e: e.tensor_copy(out=pcol_bf, in_=pcol_f), sig=True)
                tsl = None
                for t_ in range(8):
                    tsl = P.op("pe", lambda e, t_=t_: e.matmul(ps[7][0:4, t_ * 32:(t_ + 1) * 32], lhsT=pcol_bf[:, t_, :], rhs=A128,
                                                               start=True, stop=(t_ == 0)), waits=[tpc, ps_free[7]], sig=True)
                    if t_ >= 1:
                        tsl = P.op("pe", lambda e, t_=t_: e.matmul(ps[7][0:4, t_ * 32:(t_ + 1) * 32], lhsT=pcol_bf[:, t_ - 1, :], rhs=Aprev,
                                                                   start=False, stop=True), sig=True)
                P.op("pool", lambda e: e.memset(sc_s[0:4, 0:264], 0.0), waits=[P.lastsig.get("dve")])
                tcp = P.op("act", lambda e: e.activation(out=sc_s[0:4, 0:256], in_=ps[7][0:4, 0:256], func=AF.Copy), waits=[tsl, P.lastsig.get("pool")], sig=True)
                ps_free[7] = tcp
                P.op("pool", lambda e: e.memset(sc_s[0:4, 255:256], 1e9), waits=[tcp])
                P.op("pool", lambda e: e.memset(sc_s[0:4, 256:257], 2e9))
                tfz = P.op("pool", lambda e: e.memset(sc_s[0:4, 0:1], 3e9), sig=True)
                P.op("dve", lambda e: e.max(out=m8a[0:4], in_=sc_s[0:4, 0:257]), waits=[tfz])
                P.op("dve", lambda e: e.match_replace(out=sc2_s[0:4, 0:257], in_to_replace=m8a[0:4], in_values=sc_s[0:4, 0:257], imm_value=-3e38))
                P.op("dve", lambda e: e.max(out=m8b[0:4], in_=sc2_s[0:4, 0:257]))
                tse = P.op("dve", lambda e: e.tensor_scalar(out=sel_bf[0:4, 0:257], in0=sc_s[0:4, 0:257], scalar1=m8b[0:4, 7:8], scalar2=None,
                                                            op0=ALU.is_ge), sig=True)
                tmk = None
                for g in range(4):
                    tx = P.op("pe", lambda e, g=g: e.matmul(ps[6][:, 0:257], lhsT=identb[0:4, g:g + 1].to_broadcast([4, 128]),
                                                            rhs=sel_bf[0:4, 0:257], start=True, stop=True), waits=[tse, ps_free[6]], sig=True)
                    P.op("act", lambda e, g=g: e.activation(out=maskT[0:64, 0:128, g], in_=ps[6][0:64, 0:256:2], func=AF.Copy), waits=[tx])
                    tmk = P.op("act", lambda e, g=g: e.activation(out=maskT[64:128, 0:128, g], in_=ps[6][64:128, 1:256:2], func=AF.Copy), sig=True)
                    ps_free[6] = tmk
                tmk = P.op("act", lambda e: e.activation(out=maskT[:, 128, :], in_=ident[:, 0:1].to_broadcast([128, 4]), func=AF.Copy), sig=True)
                s_init()
                tO = None
                kfree = [None, None]
                vfree = [None, None]
                for t_ in range(129):
                    sl = t_ % 2
                    if t_ < 128:
                        tk = page_rows(2, b, t_, rowst[:, sl, :], [state["rows_free"][sl]])
                        tv = page_rows(3, b, t_, Vp[sl], [vfree[sl]])
                    else:
                        tzk = P.op("pool", lambda e, sl=sl: e.memset(rowst[:, sl, :], 0.0), waits=[state["rows_free"][sl]], sig=True)
                        tzv = P.op("pool", lambda e, sl=sl: e.memset(Vp[sl], 0.0), waits=[vfree[sl]], sig=True)
                        tk = ring_in.issue("sp", lambda e, sl=sl, b=b: e.dma_start(out=rowst[0:1, sl, :], in_=kvs_o[2][b:b + 1, :]), waits=[tzk])
                        tv = ring_pl.issue("pool", lambda e, sl=sl, b=b: e.dma_start(out=Vp[sl][0:1, :], in_=kvs_o[3][b:b + 1, :]), waits=[tzv])
                    b_ = misc_bank()
                    tpx = None
                    for g in range(4):
                        tpx = P.op("pe", lambda e, b_=b_, g=g, sl=sl: e.transpose(
                            out=ps[b_][:, g * 128:(g + 1) * 128], in_=rowst[:, sl, g * 128:(g + 1) * 128], identity=ident[:]),
                            waits=[tk, ps_free[b_]], sig=(g == 3))
                    state["rows_free"][sl] = tpx
                    tkt = P.op("act", lambda e, b_=b_, sl=sl: e.activation(out=kTp[sl], in_=ps[b_][:, :].rearrange("p (g t) -> p g t", g=4),
                                                                            func=AF.Copy), waits=[tpx, kfree[sl]], sig=True)
                    ps_free[b_] = tkt
                    tO = s_step(b, lambda g, sl=sl: kTp[sl][:, g, :], 128, lambda g, sl=sl: Vp[sl][:, g * 128:(g + 1) * 128], qrT_s,
                                maskT[:, t_, :].unsqueeze(2).to_broadcast([128, 4, 8]), waits=[tkt, tv, tmk])
                    kfree[sl] = tO
                    vfree[sl] = tO
                s_finish(b, tO, 1, False)
                s_init()
                for t_ in range(4):
                    sl = t_ % 2
                    tk = ring_in.issue("sp", lambda e, sl=sl, b=b, t_=t_: e.dma_start(out=rowst[:, sl, :], in_=wins_o[0][b, t_ * 128:(t_ + 1) * 128, :]),
                                       waits=[state["rows_free"][sl]])
                    tv = ring_pl.issue("pool", lambda e, sl=sl, b=b, t_=t_: e.dma_start(out=Vp[sl], in_=wins_o[1][b, t_ * 128:(t_ + 1) * 128, :]),
                                       waits=[vfree[sl]])
                    b_ = misc_bank()
                    tpx = None
                    for g in range(4):
                        tpx = P.op("pe", lambda e, b_=b_, g=g, sl=sl: e.transpose(
                            out=ps[b_][:, g * 128:(g + 1) * 128], in_=rowst[:, sl, g * 128:(g + 1) * 128], identity=ident[:]),
                            waits=[tk, ps_free[b_]], sig=(g == 3))
                    state["rows_free"][sl] = tpx
                    tkt = P.op("act", lambda e, b_=b_, sl=sl: e.activation(out=kTp[sl], in_=ps[b_][:, :].rearrange("p (g t) -> p g t", g=4),
                                                                            func=AF.Copy), waits=[tpx, kfree[sl]], sig=True)
                    ps_free[b_] = tkt
                    tO = s_step(b, lambda g, sl=sl: kTp[sl][:, g, :], 128, lambda g, sl=sl: Vp[sl][:, g * 128:(g + 1) * 128], qrT_s,
                                None, waits=[tkt, tv])
                    kfree[sl] = tO
                    vfree[sl] = tO
                s_finish(b, tO, 2, False)

            P.strict = set()
            barrier()
            state["scA_free"] = None
            state["scB_free"] = None
            state["a_free"] = None
            state["gfree"] = [None, None]
            state["jfree"] = [None, None]
            def phase2_tile(qt):
                smp = (qt == NT)
                W = NSMP if smp else TW
                t0 = qt * TW
                P.strict = {"act", "dve", "pool"} if smp else set()
                th = ring_in.issue("sp", lambda e, qt=qt, W=W: e.dma_start(
                    out=h[:, :, 0:W], in_=h1_s[qt].rearrange("p (k t) -> p k t", k=KC)[:, :, 0:W]), waits=[state.get("y_done")])
                if smp:
                    to_ = P.op("dve", lambda e: e.tensor_copy(out=a[:, :, 0:NSMP], in_=acc_s[:].rearrange("p b c -> p c b")),
                               waits=[state.get("y_done")], sig=True)
                else:
                    to_ = ring_in.issue("sp", lambda e, qt=qt: e.dma_start(out=a[:, :, :], in_=oT_s[qt].rearrange("p (k t) -> p k t", k=KC)),
                                        waits=[state.get("y_done")])

                def epi_o(j, pp, tp):
                    return P.op("dve", lambda e: e.tensor_tensor(out=h[:, j, 0:W], in0=pp, in1=h[:, j, 0:W], op=ALU.add),
                                waits=[tp, th], sig=True)
                tmix = dense_fm(wo_v, 0, KC, 0, D, lambda kk: a[:, kk, 0:W], W, epi_o, banks=(0, 1, 2, 3), waits=[to_])
                state["a_free"] = tmix
                if smp:
                    rows_list = [(ps_d[1], NSMP)]
                else:
                    rows_list = [(p_d[1, t0 + tb * 128:t0 + (tb + 1) * 128, :], 128) for tb in range(4)]
                th2 = mlp_and_ple(1, W, None, rows_list, [tmix])
                stage = scB[:].rearrange("p a b -> p (a b)")
                ty = None
                if smp:
                    tcp = None
                    for c0 in range(0, KC, 4):
                        b = misc_bank()
                        tpe = None
                        for i_ in range(4):
                            tpe = P.op("pe", lambda e, b=b, i_=i_, kc=c0 + i_: e.transpose(
                                out=ps[b][0:NSMP, i_ * 128:(i_ + 1) * 128], in_=h[:, kc, 0:NSMP], identity=ident[:]),
                                waits=[th2, ps_free[b]], sig=(i_ == 3))
                        tcp = P.op("act", lambda e, b=b, c0=c0: e.activation(out=stage[0:NSMP, c0 * 128:(c0 + 4) * 128],
                                                                               in_=ps[b][0:NSMP, :], func=AF.Copy),
                                   waits=[tpe, state["scB_free"]], sig=True)
                        ps_free[b] = tcp
                    ty = store_rows(ys_o[:, :], stage[0:NSMP, 0:D], [tcp])
                    return
                for tb in range(4):
                    tcp = None
                    for c0 in range(0, KC, 4):
                        b = misc_bank()
                        tpe = None
                        for i_ in range(4):
                            tpe = P.op("pe", lambda e, b=b, i_=i_, kc=c0 + i_, tb=tb: e.transpose(
                                out=ps[b][:, i_ * 128:(i_ + 1) * 128], in_=h[:, kc, tb * 128:(tb + 1) * 128], identity=ident[:]),
                                waits=[th2, ps_free[b]], sig=(i_ == 3))
                        eng = "act" if (c0 // 4) % 2 == 0 else "dve"
                        if eng == "act":
                            tcp = P.op("act", lambda e, b=b, c0=c0: e.activation(out=stage[:, c0 * 128:(c0 + 4) * 128], in_=ps[b][:, :],
                                                                                   func=AF.Copy), waits=[tpe, state["scB_free"]], sig=True)
                        else:
                            tcp = P.op("dve", lambda e, b=b, c0=c0: e.tensor_copy(out=stage[:, c0 * 128:(c0 + 4) * 128], in_=ps[b][:, :]),
                                       waits=[tpe, state["scB_free"]], sig=True)
                        ps_free[b] = tcp
                        state["y_pe"] = tpe
                    ty = store_rows(y_o[t0 + tb * 128:t0 + (tb + 1) * 128, :], stage[:, 0:D],
                                    [P.lastsig.get("act"), P.lastsig.get("dve")])
                    state["scB_free"] = ty
                state["y_done"] = state["y_pe"]
                state["h_free"] = state["y_pe"]

            for qt in range(NT + 1):
                phase2_tile(qt)
            P.strict = set()


    for sq in range(NSEQ):
        run_pass(sq)

    P.op("sp", lambda e: e.nop(), waits=[t for t in ring_out.last + ring_in.last + ring_pl.last if t is not None])

    with nc.Block() as block:
        @block.tensor
        def _(e):
            P.replay("pe", e)

        @block.scalar
        def _(e):
            P.replay("act", e)

        @block.vector
        def _(e):
            P.replay("dve", e)

        @block.gpsimd
        def _(e):
            P.replay("pool", e)

        @block.sync
        def _(e):
            P.replay("sp", e)
    stack.close()
    return nc


def rope_tables():
    half = 64
    inv = (10000.0 ** (-np.arange(half, dtype=np.float32) / half)).astype(np.float32)
    pos = np.concatenate([np.arange(T, dtype=np.float32), np.array([PAST], np.float32)])
    ang = pos[:, None].astype(np.float32) * inv[None, :]
    return np.stack([np.cos(ang), np.sin(ang)], axis=1).astype(np.float32)


def make_in_maps(inp, ncores):
    f = np.ascontiguousarray
    vecs = f(np.stack([inp["g_mix"][0], inp["g_mix"][1], inp["pool_scale"][0], inp["g_kv"],
                       inp["g_ffn"][0], inp["g_ffn"][1], inp["g_ple"][0], inp["g_ple"][1]]))
    hvecs = f(np.stack([inp["g_k_cmp"], inp["g_k_sel"], inp["g_k_win"], inp["g_q"][0]]))
    cs = rope_tables()
    NS = NSEQ * NSMP
    maps = []
    for c in range(ncores):
        sq = slice(NSEQ * c, NSEQ * (c + 1))
        sl = slice(NS * c, NS * (c + 1))
        m = dict(
            x=f(inp["x_prompt"][sq]), xs=f(inp["x_sample"][sl, 0].reshape(NSEQ, NSMP, D)),
            spool=f(inp["state_pool"][0, sl].reshape(NSEQ, NSMP, 15, D)),
            p=f(inp["p_prompt"][:, sq].transpose(1, 0, 2, 3)),
            psm=f(inp["p_sample"][:, sl, 0].reshape(2, NSEQ, NSMP, PLE).transpose(1, 0, 2, 3)),
            vecs=vecs, hvecs=hvecs, cossin=cs,
            w_pool=f(inp["w_pool"][0]), w_up0=inp["w_up"][0], w_up1=inp["w_up"][1],
            w_down0=inp["w_down"][0], w_down1=inp["w_down"][1],
            w_gate0=inp["w_ple_gate"][0], w_gate1=inp["w_ple_gate"][1],
            w_ple0=inp["w_ple"][0], w_ple1=inp["w_ple"][1],
            w_kv=f(inp["w_kv"].reshape(D, 6 * 512)),
            st_kwin=f(inp["state_k_win"][sl].reshape(NSEQ, NSMP, WIN, 512)),
            st_vwin=f(inp["state_v_win"][sl].reshape(NSEQ, NSMP, WIN, 512)),
            w_qg=inp["w_qg"][0], w_o=inp["w_o"][0],
            w_cmp_k1=inp["w_cmp_k1"], w_cmp_v1=inp["w_cmp_v1"], w_cmp_k2=inp["w_cmp_k2"], w_cmp_v2=inp["w_cmp_v2"],
            pe_cmp_k=inp["pe_cmp_k"], pe_cmp_v=inp["pe_cmp_v"],
            ptab=f(inp["page_table"][sl].reshape(NSEQ, NSMP, 128)),
            cache0=inp["cache_k_cmp"].reshape(1280 * 128, 512), cache1=inp["cache_v_cmp"].reshape(1280 * 128, 512),
            cache2=inp["cache_k_sel"].reshape(1280 * 128, 512), cache3=inp["cache_v_sel"].reshape(1280 * 128, 512),
        )
        maps.append(m)
    return maps


def kernel(**inp):
    ncores = 4 // NSEQ
    nc = build(stages=("A", "B"))
    maps = make_in_maps(inp, ncores)
    res = run_bass_kernel_spmd(nc, maps, core_ids=list(range(ncores)))
    R = res.results
    B = 4

    def cat(k):
        return np.concatenate([R[c][k] for c in range(ncores)], axis=0)
    y_p = cat("y")
    y_s = cat("ysm").reshape(8, 1, D)
    pool_p = cat("pool_p")[None]
    pool_s = cat("pool_s").reshape(8, 15, D)[None]
    kv_p = [cat("kv%d_p" % n).reshape(B, T, 4, 128) for n in range(4)]
    win_p = [cat(k).reshape(B, WIN, 4, 128) for k in ("kwin_p", "vwin_p")]
    kv_s = [cat("kv%d_s" % n).reshape(8, 1, 4, 128) for n in range(4)]
    win_s = [cat(k).reshape(8, WIN, 4, 128) for k in ("kwin_s", "vwin_s")]
    return (y_p, y_s, pool_p, pool_s, *kv_p, *win_p, *kv_s, *win_s)
```
